# Optimizing a Trainium2 kernel written in Bass

```python
import jax, jax.numpy as jnp
from jax import lax
import numpy as np

D_MODEL = 2048
BATCH = 8
SEQ = 2048
DEPTH = 2

N_MEM = 256
CONV_CH = 1024
CONV_WIDTH = 31
HEAD_DIM = 128
N_Q_HEADS = 8
N_KV_HEADS = 2
Q_WIDTH = N_Q_HEADS * HEAD_DIM
KV_WIDTH = N_KV_HEADS * HEAD_DIM
WINDOW = 128
BLOCK = 128
ROT_DIM = HEAD_DIM // 4
ROPE_THETA = 500000.0
MEM_HEADS = 4
MEM_HEAD_DIM = 256
MEM_WIDTH = MEM_HEADS * MEM_HEAD_DIM
N_BRANCH = 3
D_FF = 5632
IN_SIZES = (2 * CONV_CH, Q_WIDTH, KV_WIDTH, KV_WIDTH, MEM_WIDTH, N_BRANCH * D_MODEL)
IN_WIDTH = 2 * CONV_CH + Q_WIDTH + 2 * KV_WIDTH + MEM_WIDTH + N_BRANCH * D_MODEL
ALPHA = (2 * DEPTH) ** 0.25
BETA = (8 * DEPTH) ** -0.25
LN_EPS = 1e-5
NEG_INF = -1e30

kernel_name = "hybrid_conv_swa_memory_macaron_deepnorm"


def layer_norm(x, g, b):
    xf = x.astype(jnp.float32)
    mu = jnp.mean(xf, axis=-1, keepdims=True)
    var = jnp.mean(jnp.square(xf - mu), axis=-1, keepdims=True)
    y = (xf - mu) * lax.rsqrt(var + LN_EPS) * g.astype(jnp.float32) + b.astype(jnp.float32)
    return y.astype(x.dtype)


def swiglu(x, w_up, w_down):
    gu = x @ w_up
    gate, up = jnp.split(gu, 2, axis=-1)
    return (jax.nn.silu(gate) * up) @ w_down


def rope_tables(seq_len):
    pos = jnp.arange(seq_len, dtype=jnp.float32)
    inv_freq = ROPE_THETA ** (-jnp.arange(0, ROT_DIM, 2, dtype=jnp.float32) / ROT_DIM)
    ang = pos[:, None] * inv_freq[None, :]
    return jnp.cos(ang), jnp.sin(ang)


def apply_partial_rope(x, cos, sin):
    c = cos[None, :, None, :].astype(x.dtype)
    s = sin[None, :, None, :].astype(x.dtype)
    x1 = x[..., : ROT_DIM // 2]
    x2 = x[..., ROT_DIM // 2: ROT_DIM]
    return jnp.concatenate([x1 * c - x2 * s, x2 * c + x1 * s, x[..., ROT_DIM:]], axis=-1)


def conv_module(u, dw_w, dw_b, ln_g, ln_b, w_pw):
    a, g = jnp.split(u, 2, axis=-1)
    h = a * jax.nn.sigmoid(g)
    h = lax.conv_general_dilated(
        h, dw_w[:, None, :].astype(h.dtype), window_strides=(1,),
        padding=[(CONV_WIDTH // 2, CONV_WIDTH // 2)],
        dimension_numbers=("NWC", "WIO", "NWC"),
        feature_group_count=CONV_CH) + dw_b
    h = jax.nn.silu(layer_norm(h, ln_g, ln_b))
    return h @ w_pw


def window_attention(q, k, v, sink, w_o):
    B, S = q.shape[0], q.shape[1]
    nb = S // BLOCK
    G = N_Q_HEADS // N_KV_HEADS
    qb = q.reshape(B, nb, BLOCK, N_KV_HEADS, G, HEAD_DIM)
    pad = ((0, 0), (BLOCK, BLOCK), (0, 0), (0, 0))
    kp = jnp.pad(k, pad).reshape(B, nb + 2, BLOCK, N_KV_HEADS, HEAD_DIM)
    vp = jnp.pad(v, pad).reshape(B, nb + 2, BLOCK, N_KV_HEADS, HEAD_DIM)
    kb = jnp.concatenate([kp[:, :-2], kp[:, 1:-1], kp[:, 2:]], axis=2)
    vb = jnp.concatenate([vp[:, :-2], vp[:, 1:-1], vp[:, 2:]], axis=2)
    scores = jnp.einsum("bnqkgd,bnckd->bnkgqc", qb, kb).astype(jnp.float32) * (HEAD_DIM ** -0.5)
    qpos = jnp.arange(BLOCK)[:, None] + BLOCK
    kpos = jnp.arange(3 * BLOCK)[None, :]
    kabs = jnp.arange(nb)[:, None, None] * BLOCK + kpos[None] - BLOCK
    valid = (jnp.abs(kpos - qpos) <= WINDOW)[None] & (kabs >= 0) & (kabs < S)
    scores = jnp.where(valid[None, :, None, None], scores, NEG_INF)
    sink_b = jnp.broadcast_to(sink.astype(jnp.float32).reshape(1, 1, N_KV_HEADS, G, 1, 1),
                              scores.shape[:-1] + (1,))
    p = jax.nn.softmax(jnp.concatenate([scores, sink_b], axis=-1), axis=-1)[..., :-1]
    out = jnp.einsum("bnkgqc,bnckd->bnqkgd", p.astype(vb.dtype), vb)
    return out.reshape(B, S, Q_WIDTH) @ w_o


def memory_attention(q, mk, mv, w_o):
    B, S = q.shape[0], q.shape[1]
    scores = jnp.einsum("bshd,bmhd->bhsm", q, mk).astype(jnp.float32) * (MEM_HEAD_DIM ** -0.5)
    p = jax.nn.softmax(scores, axis=-1)
    out = jnp.einsum("bhsm,bmhd->bshd", p.astype(mv.dtype), mv)
    return out.reshape(B, S, MEM_WIDTH) @ w_o


def token_mixer(h, mem, cos, sin, w_in, conv_dw_w, conv_dw_b, conv_ln_g, conv_ln_b, conv_w_out,
                win_w_o, win_sink, mem_w_kv, mem_w_o, w_out):
    B, S, _ = h.shape
    proj = h @ w_in
    offs = np.cumsum(IN_SIZES)[:-1].tolist()
    u_conv, q, k, v, q_mem, g_logits = jnp.split(proj, offs, axis=-1)
    y_conv = conv_module(u_conv, conv_dw_w, conv_dw_b, conv_ln_g, conv_ln_b, conv_w_out)
    q = apply_partial_rope(q.reshape(B, S, N_Q_HEADS, HEAD_DIM), cos, sin)
    k = apply_partial_rope(k.reshape(B, S, N_KV_HEADS, HEAD_DIM), cos, sin)
    v = v.reshape(B, S, N_KV_HEADS, HEAD_DIM)
    y_win = window_attention(q, k, v, win_sink, win_w_o)
    mk, mv = jnp.split(mem @ mem_w_kv, 2, axis=-1)
    M = mem.shape[1]
    y_mem = memory_attention(q_mem.reshape(B, S, MEM_HEADS, MEM_HEAD_DIM),
                             mk.reshape(B, M, MEM_HEADS, MEM_HEAD_DIM),
                             mv.reshape(B, M, MEM_HEADS, MEM_HEAD_DIM), mem_w_o)
    gates = jax.nn.sigmoid(g_logits.astype(jnp.float32)).astype(h.dtype).reshape(B, S, N_BRANCH, D_MODEL)
    merged = gates[:, :, 0] * y_conv + gates[:, :, 1] * y_win + gates[:, :, 2] * y_mem
    return merged @ w_out


def setup_inputs(seed: int = 0) -> dict:
    key = jax.random.key(seed)
    ks = jax.random.split(key, 24)

    def nrm(k, shape, scale):
        return jax.random.normal(k, shape, jnp.float32) * scale

    L, D = DEPTH, D_MODEL
    return {
        "x": nrm(ks[0], (BATCH, SEQ, D), 1.0),
        "mem": nrm(ks[1], (BATCH, N_MEM, D), 1.0),
        "ln1_g": 1.0 + nrm(ks[2], (L, D), 0.02),
        "ln1_b": nrm(ks[3], (L, D), 0.02),
        "ffn1_w_up": nrm(ks[4], (L, D, 2 * D_FF), D ** -0.5),
        "ffn1_w_down": nrm(ks[5], (L, D_FF, D), BETA * D_FF ** -0.5),
        "w_in": nrm(ks[6], (L, D, IN_WIDTH), D ** -0.5),
        "conv_dw_w": nrm(ks[7], (L, CONV_WIDTH, CONV_CH), CONV_WIDTH ** -0.5),
        "conv_dw_b": nrm(ks[8], (L, CONV_CH), 0.02),
        "conv_ln_g": 1.0 + nrm(ks[9], (L, CONV_CH), 0.02),
        "conv_ln_b": nrm(ks[10], (L, CONV_CH), 0.02),
        "conv_w_out": nrm(ks[11], (L, CONV_CH, D), BETA * CONV_CH ** -0.5),
        "win_w_o": nrm(ks[12], (L, Q_WIDTH, D), BETA * Q_WIDTH ** -0.5),
        "win_sink": nrm(ks[13], (L, N_Q_HEADS), 0.5),
        "mem_w_kv": nrm(ks[14], (L, D, 2 * MEM_WIDTH), D ** -0.5),
        "mem_w_o": nrm(ks[15], (L, MEM_WIDTH, D), BETA * MEM_WIDTH ** -0.5),
        "w_out": nrm(ks[16], (L, D, D), BETA * D ** -0.5),
        "ln2_g": 1.0 + nrm(ks[17], (L, D), 0.02),
        "ln2_b": nrm(ks[18], (L, D), 0.02),
        "ffn2_w_up": nrm(ks[19], (L, D, 2 * D_FF), D ** -0.5),
        "ffn2_w_down": nrm(ks[20], (L, D_FF, D), BETA * D_FF ** -0.5),
        "ln3_g": 1.0 + nrm(ks[21], (L, D), 0.02),
        "ln3_b": nrm(ks[22], (L, D), 0.02),
    }


def reference(x, mem, ln1_g, ln1_b, ffn1_w_up, ffn1_w_down, w_in, conv_dw_w, conv_dw_b,
              conv_ln_g, conv_ln_b, conv_w_out, win_w_o, win_sink, mem_w_kv, mem_w_o, w_out,
              ln2_g, ln2_b, ffn2_w_up, ffn2_w_down, ln3_g, ln3_b):
    cos, sin = rope_tables(x.shape[1])
    h = x
    for l in range(DEPTH):
        h = layer_norm(ALPHA * h + 0.5 * swiglu(h, ffn1_w_up[l], ffn1_w_down[l]), ln1_g[l], ln1_b[l])
        y = token_mixer(h, mem, cos, sin, w_in[l], conv_dw_w[l], conv_dw_b[l], conv_ln_g[l],
                        conv_ln_b[l], conv_w_out[l], win_w_o[l], win_sink[l], mem_w_kv[l],
                        mem_w_o[l], w_out[l])
        h = layer_norm(ALPHA * h + y, ln2_g[l], ln2_b[l])
        h = layer_norm(ALPHA * h + 0.5 * swiglu(h, ffn2_w_up[l], ffn2_w_down[l]), ln3_g[l], ln3_b[l])
    return h
```

```python
import contextlib
import numpy as np
import concourse.bass as bass
import concourse.mybir as mybir
from concourse.bass_utils import run_bass_kernel_spmd

F32 = mybir.dt.float32
BF16 = mybir.dt.bfloat16
U8 = mybir.dt.uint8
AF = mybir.ActivationFunctionType
ALU = mybir.AluOpType
AX = mybir.AxisListType

D = 2048
T = 2048
L = 2
NMEM = 256
DFF = 5632
FC = DFF // 128
CONV_CH = 1024
CW = 31
IN_WIDTH = 10752
NCH_IN = IN_WIDTH // 128
ALPHA = float((2 * L) ** 0.25)
EPS = 1e-5
NEG = -1e30
ENGS = ["pe", "act", "dve", "pool", "sp"]


class Buf:
    __slots__ = ("name", "w", "r")

    def __init__(self, name=""):
        self.name = name
        self.w = None
        self.r = {}


class Prog:
    def __init__(self, nc):
        self.nc = nc
        self.q = {e: [] for e in ENGS}
        self.cnt = {}
        self.seen = {e: {} for e in ENGS}
        self.semnames = []
        self.n_wait = 0
        self.n_ins = 0
        for e in ENGS:
            self._sem("eng_" + e)

    def _sem(self, key):
        if key not in self.cnt:
            self.cnt[key] = 0
            self.semnames.append(key)
        return key

    def fresh(self, name=""):
        b = Buf(name)
        b.r = {k: v for k, v in self.cnt.items() if v > 0}
        return b

    def _deps(self, eng, reads, writes):
        deps = {}

        def add(k, v):
            if deps.get(k, 0) < v:
                deps[k] = v
        for b in reads:
            if b.w is not None:
                add(*b.w)
        for b in writes:
            if b.w is not None:
                add(*b.w)
            for k, v in b.r.items():
                add(k, v)
        waits = []
        own = "eng_" + eng
        seen = self.seen[eng]
        for k, v in deps.items():
            if eng == "pe" and k == own:
                continue
            if seen.get(k, 0) < v:
                seen[k] = v
                waits.append((k, v))
        return waits

    def _commit(self, ev, reads, writes):
        k, v = ev
        for b in reads:
            if b.r.get(k, 0) < v:
                b.r[k] = v
        for b in writes:
            b.w = ev
            b.r = {}

    def op(self, eng, fns, reads=(), writes=()):
        if callable(fns):
            fns = [fns]
        waits = self._deps(eng, reads, writes)
        key = "eng_" + eng
        self.cnt[key] += 1
        ev = (key, self.cnt[key])
        self.q[eng].append(("op", waits, fns, key))
        self._commit(ev, reads, writes)
        self.n_wait += len(waits)
        self.n_ins += len(fns)
        return ev

    def dma(self, eng, pairs, semkey, reads=(), writes=(), **kw):
        if isinstance(pairs, tuple):
            pairs = [pairs]
        self._sem(semkey)
        waits = self._deps(eng, reads, writes)
        self.cnt[semkey] += 16 * len(pairs)
        ev = (semkey, self.cnt[semkey])
        self.q[eng].append(("dma", waits, (pairs, kw), semkey))
        self._commit(ev, reads, writes)
        self.n_wait += len(waits)
        self.n_ins += len(pairs)
        return ev

    def wait_all(self, eng, bufs):
        waits = self._deps(eng, bufs, ())
        self.q[eng].append(("wait", waits, None, None))

    def emit(self):
        nc = self.nc
        with contextlib.ExitStack() as st:
            sems = {}
            for k in self.semnames:
                sems[k] = st.enter_context(nc.semaphore(k))
            block = st.enter_context(nc.Block())

            def run(e, items):
                for kind, waits, payload, key in items:
                    for (k, v) in waits:
                        e.wait_ge(sems[k], v)
                    if kind == "op":
                        ins = None
                        for f in payload:
                            ins = f(e)
                        ins.then_inc(sems[key], 1)
                    elif kind == "dma":
                        pairs, kw = payload
                        for (o, i) in pairs:
                            e.dma_start(out=o, in_=i, **kw).then_inc(sems[key], 16)

            @block.tensor
            def _(e):
                run(e, self.q["pe"])

            @block.scalar
            def _(e):
                run(e, self.q["act"])

            @block.vector
            def _(e):
                run(e, self.q["dve"])

            @block.gpsimd
            def _(e):
                run(e, self.q["pool"])

            @block.sync
            def _(e):
                run(e, self.q["sp"])


class Tile:
    __slots__ = ("ap", "buf", "sem", "pinned")

    def __init__(self, ap, buf, sem):
        self.ap = ap
        self.buf = buf
        self.sem = sem
        self.pinned = False


class Ring:
    def __init__(self, tiles):
        self.tiles = tiles
        self.i = 0

    def get(self, pin=False):
        for _ in range(2 * len(self.tiles)):
            t = self.tiles[self.i % len(self.tiles)]
            self.i += 1
            if not t.pinned:
                t.pinned = pin
                return t
        raise RuntimeError("ring exhausted (all tiles pinned)")


def build(depth=L, stop=None, dbg=False):
    nc = bass.Bass("TRN2", target_bir_lowering=False)
    P = Prog(nc)

    def din(name, shape, dt=F32):
        return nc.dram_tensor(name, list(shape), dt, kind="ExternalInput").ap()

    def dscr(name, shape, dt=F32, out=False):
        if out:
            return nc.dram_tensor(name, list(shape), dt, kind="ExternalOutput").ap()
        return nc.dram_tensor(name, list(shape), dt).ap()

    _ins = {}

    def lazy(name, shape):
        def get():
            if name not in _ins:
                _ins[name] = din(name, shape)
            return _ins[name]
        return get
    xT = din("xT", [D, T]); _ins["xT"] = xT
    g_memT = lazy("memT", [D, NMEM])
    g_w_up = [lazy(f"ffn{i}_w_up", [L, FC, 128, 16, 256]) for i in (1, 2)]
    g_w_dn = [lazy(f"ffn{i}_w_down", [L, 16, 128, FC, 128]) for i in (1, 2)]
    g_w_in = lazy("w_in", [L, NCH_IN, 128, 16, 128])
    g_w_co = lazy("conv_w_out", [L, 16, 128, 8, 128])
    g_w_wo = lazy("win_w_o", [L, 16, 128, 8, 128])
    g_w_mo = lazy("mem_w_o", [L, 16, 128, 8, 128])
    g_w_kv = lazy("mem_w_kv", [L, 16, 128, 16, 128])
    g_w_out = lazy("w_out", [L, 16, 128, 16, 128])
    lnp = din("lnp", [128, L * 3 * 2 * 16]); _ins["lnp"] = lnp
    cvp = din("cvp", [128, L * 8 * (CW + 3)]); _ins["cvp"] = cvp
    sinkb = din("sinkb", [128, L * 8]); _ins["sinkb"] = sinkb
    g_ropeC = lazy("ropeC", [32, T])
    g_ropeS = lazy("ropeS", [32, T])
    g_mask = lazy("mask", [128, 384])
    identd = din("ident", [128, 128]); _ins["ident"] = identd

    last_dbg = dbg
    outT = dscr("outT", [D, T], out=True)
    h32a = dscr("h32a", [D, T], out=last_dbg)
    zd = dscr("zd", [D, T])
    hc_d = dscr("hc_d", [CONV_CH, T], BF16)
    qkb_d = dscr("qkb_d", [1280, T], BF16)
    v_tok = dscr("v_tok", [T, 256], BF16)
    qm_d = dscr("qm_d", [1024, T], BF16)
    conv_d = dscr("conv_d", [1024, T], BF16, out=last_dbg)
    attn_d = dscr("attn_d", [1024, T], BF16, out=last_dbg)
    memo_d = dscr("memo_d", [1024, T], BF16, out=last_dbg)

    fm = lambda ap: ap.rearrange("(c p) t -> p c t", p=128)
    h32a_v, zd_v, outT_v, xT_v = fm(h32a), fm(zd), fm(outT), fm(xT)
    hc_v, qk_v, qm_v, conv_v, attn_v, memo_v = fm(hc_d), fm(qkb_d), fm(qm_d), fm(conv_d), fm(attn_d), fm(memo_d)

    def grid(n, m, nm):
        return [[Buf(f"{nm}{i}_{j}") for j in range(m)] for i in range(n)]
    b_h32a = grid(16, 4, "h32a")
    b_zd = grid(16, 4, "zd")
    b_hc = grid(8, 4, "hc")
    b_qk = grid(10, 4, "qk")
    b_vtok = [Buf() for _ in range(16)]
    b_qm = grid(8, 4, "qm")
    b_conv = grid(8, 4, "convd")
    b_attn = [Buf() for _ in range(8)]
    b_memo = [Buf() for _ in range(4)]
    b_out = Buf()

    ARENA = 212800
    arena = nc.alloc_sbuf_tensor("arena", [128, ARENA], U8)
    off = [0]

    def carve(nbytes, at=None):
        o = off[0] if at is None else at
        o = (o + 31) // 32 * 32
        if at is None:
            off[0] = o + nbytes
        assert o + nbytes <= ARENA, (o, nbytes)
        return o

    def view(o, shape, dt):
        esz = 2 if dt == BF16 else 4
        n = int(np.prod(shape)) * esz
        ap = arena[:, o:o + n].bitcast(dt)
        if len(shape) == 2:
            ap = ap.rearrange("p (a b) -> p a b", b=shape[1])
        elif len(shape) == 3:
            ap = ap.rearrange("p (a b c) -> p a b c", b=shape[1], c=shape[2])
        return ap

    XT = view(carve(16 * T * 2), [16, T], BF16)
    b_XT = [Buf(f"XT{i}") for i in range(4)]
    WSB = 11264
    ws_off = [carve(WSB), carve(WSB)]
    b_ws = [Buf("ws0"), Buf("ws1")]
    wsi = [0]
    c_off = carve(8192)
    co = [c_off]

    def ccarve(n):
        o = (co[0] + 31) // 32 * 32
        co[0] = o + n
        assert co[0] <= c_off + 8192
        return o
    ident32 = view(ccarve(512), [128], F32)
    identb = view(ccarve(256), [128], BF16)
    onesb = view(ccarve(256), [128], BF16)
    lnp_sb = view(ccarve(L * 3 * 2 * 16 * 4), [L * 3 * 2 * 16], F32)
    lnpa_sb = view(ccarve(L * 3 * 2 * 16 * 4), [L * 3 * 2 * 16], F32)
    cvp_sb = view(ccarve(L * 8 * (CW + 3) * 4), [L * 8 * (CW + 3)], F32)
    sink_sb = view(ccarve(L * 8 * 4), [L * 8], F32)
    small = view(ccarve(64 * 4), [64], F32)
    mask_sb = view(ccarve(1536), [384], F32)
    b_const = Buf("const")
    AT_BYTES = FC * 1024 * 2
    at_off = carve(AT_BYTES)
    A_ts = [view(carve(2048), [512], F32) for _ in range(2)]
    B_ts = [view(carve(2048), [512], F32) for _ in range(2)]
    b_As, b_Bs = [Buf("A0"), Buf("A1")], [Buf("B0"), Buf("B1")]
    NR = 7
    ring = Ring([Tile(view(carve(2048), [512], F32), Buf(f"r{i}"), f"r{i}") for i in range(NR)])
    NB = 3
    bring = Ring([Tile(view(carve(1024), [512], BF16), Buf(f"b{i}"), f"b{i}") for i in range(NB)])
    ps = [nc.alloc_psum_tensor(f"ps{i}", [128, 512], F32) for i in range(8)]
    b_ps = [Buf(f"ps{i}") for i in range(8)]
    bank_ring = [list(range(8))]
    bank_i = [0]

    def bank():
        r = bank_ring[0]
        b = r[bank_i[0] % len(r)]
        bank_i[0] += 1
        return b

    def wslot():
        s = wsi[0] % 2
        wsi[0] += 1
        return s

    def ws_view(sl, shape, byte_off=0):
        return view(ws_off[sl] + byte_off, shape, BF16)

    def load_w(sl, pieces):
        P.dma("pool", pieces, f"w{sl}", writes=[b_ws[sl]])

    def mm(b, lhs_fn, rhs_fn, kc_n, reads, ncols=512, extra_writes=()):
        fns = []
        for kc in range(kc_n):
            lh, rh = lhs_fn(kc), rhs_fn(kc)
            fns.append(lambda e, kc=kc, lh=lh, rh=rh: e.matmul(ps[b][:, 0:ncols], lhsT=lh, rhs=rh,
                                                               start=(kc == 0), stop=(kc == kc_n - 1)))
        P.op("pe", fns, reads=reads, writes=[b_ps[b]] + list(extra_writes))

    P.dma("sp", [(ident32, identd), (lnp_sb, lnp), (cvp_sb, cvp), (sink_sb, sinkb), (mask_sb, g_mask())], "cld", writes=[b_const])
    P.op("dve", lambda e: e.tensor_copy(out=identb, in_=ident32), reads=[b_const], writes=[b_const])
    P.op("dve", lambda e: e.memset(onesb, 1.0), writes=[b_const])
    P.op("dve", lambda e: e.tensor_scalar(out=lnpa_sb, in0=lnp_sb, scalar1=ALPHA, scalar2=None, op0=ALU.mult),
         reads=[b_const], writes=[b_const])

    def lnvec(l, k, gb, c, scaled=False):
        base = ((l * 3 + k) * 2 + gb) * 16 + c
        t = lnpa_sb if scaled else lnp_sb
        return t[:, base:base + 1]

    def cvvec(l, i, j):
        base = (l * 8 + i) * (CW + 3) + j
        return cvp_sb[:, base:base + 1]

    for tt in range(4):
        P.dma("pool", [(XT[:, :, tt * 512:(tt + 1) * 512], xT_v[:, :, tt * 512:(tt + 1) * 512])], "xld",
              writes=[b_XT[tt]])

    S1 = [4, 6]
    S2 = [5, 7]

    pending_stats = []

    def flush_stats():
        while pending_stats:
            pending_stats.pop(0)()

    raw_x = [False]

    def resid_epilogue(b, c, tok0, tt, scale, first, last):
        flush_stats()
        gt = tok0 // 512
        tk = slice(tok0, tok0 + 512)
        hres = ring.get()
        zt = ring.get()
        if raw_x[0]:
            P.dma("sp", [(hres.ap, xT_v[:, c, tk])], hres.sem, writes=[hres.buf])
            P.op("dve", lambda e: e.tensor_scalar(out=zt.ap, in0=ps[b][:, :], scalar1=float(scale), scalar2=None, op0=ALU.mult),
                 reads=[b_ps[b]], writes=[zt.buf])
            P.op("dve", lambda e: e.scalar_tensor_tensor(out=zt.ap, in0=hres.ap, scalar=ALPHA, in1=zt.ap,
                                                         op0=ALU.mult, op1=ALU.add),
                 reads=[hres.buf, zt.buf], writes=[zt.buf])
        else:
            P.dma("sp", [(hres.ap, h32a_v[:, c, tk])], hres.sem, reads=[b_h32a[c][gt]], writes=[hres.buf])
            P.op("dve", lambda e: e.scalar_tensor_tensor(out=zt.ap, in0=ps[b][:, :], scalar=float(scale), in1=hres.ap,
                                                         op0=ALU.mult, op1=ALU.add),
                 reads=[b_ps[b], hres.buf], writes=[zt.buf])
        P.dma("sp", [(zd_v[:, c, tk], zt.ap)], zt.sem, reads=[zt.buf], writes=[b_zd[c][gt]])
        zb = bring.get()
        P.op("act", lambda e: e.activation(out=zb.ap, in_=zt.ap, func=AF.Copy), reads=[zt.buf], writes=[zb.buf])
        zq = bring.get()
        P.op("act", lambda e: e.activation(out=zq.ap, in_=zt.ap, func=AF.Square), reads=[zt.buf], writes=[zq.buf])
        pending_stats.append(lambda: P.op(
            "pe", [lambda e: e.matmul(ps[S1[tt]][:, :], lhsT=onesb, rhs=zb.ap, start=first, stop=last),
                   lambda e: e.matmul(ps[S2[tt]][:, :], lhsT=onesb, rhs=zq.ap, start=first, stop=last)],
            reads=[zb.buf, zq.buf, b_const], writes=[b_ps[S1[tt]], b_ps[S2[tt]]]))

    def stats_to_AB(s1b, s2b, dn, ai=0):
        A_t, B_t, b_A, b_B = A_ts[ai], B_ts[ai], b_As[ai], b_Bs[ai]
        m = ring.get()
        P.op("dve", lambda e: e.tensor_scalar(out=m.ap, in0=ps[s1b][:, :], scalar1=1.0 / dn, scalar2=None, op0=ALU.mult),
             reads=[b_ps[s1b]], writes=[m.buf])
        v = ring.get()
        P.op("dve", lambda e: e.tensor_tensor(out=v.ap, in0=m.ap, in1=m.ap, op=ALU.mult), reads=[m.buf], writes=[v.buf])
        P.op("dve", lambda e: e.scalar_tensor_tensor(out=v.ap, in0=ps[s2b][:, :], scalar=1.0 / dn, in1=v.ap,
                                                     op0=ALU.mult, op1=ALU.subtract),
             reads=[b_ps[s2b], v.buf], writes=[v.buf])
        P.op("dve", lambda e: e.tensor_scalar(out=v.ap, in0=v.ap, scalar1=EPS, scalar2=None, op0=ALU.add),
             reads=[v.buf], writes=[v.buf])
        P.op("act", lambda e: e.activation(out=v.ap, in_=v.ap, func=AF.Sqrt), reads=[v.buf], writes=[v.buf])
        P.op("dve", lambda e: e.reciprocal(out=A_t, in_=v.ap), reads=[v.buf], writes=[b_A])
        P.op("dve", lambda e: e.scalar_tensor_tensor(out=B_t, in0=m.ap, scalar=-1.0, in1=A_t, op0=ALU.mult, op1=ALU.mult),
             reads=[m.buf, b_A], writes=[b_B])

    pending_norm = [None]

    def bg_step(n=1):
        g = pending_norm[0]
        if g is None:
            return
        for _ in range(n):
            try:
                next(g)
            except StopIteration:
                pending_norm[0] = None
                return

    def drain_bg():
        while pending_norm[0] is not None:
            bg_step(8)

    def normalize(l, k, t0, ntt, final):
        assert pending_norm[0] is None
        flush_stats()
        for tt in range(ntt):
            stats_to_AB(S1[tt], S2[tt], D, tt)

        def gen():
            PF = 1
            tiles = [(tt, c) for tt in range(ntt) for c in range(16)]
            loads = {}

            def issue_load(idx):
                tt, c = tiles[idx]
                tok0 = t0 + tt * 512
                zl = ring.get(pin=True)
                P.dma("sp", [(zl.ap, zd_v[:, c, tok0:tok0 + 512])], zl.sem, reads=[b_zd[c][tok0 // 512]], writes=[zl.buf])
                loads[idx] = zl
            for idx in range(min(PF, len(tiles))):
                issue_load(idx)
            for idx, (tt, c) in enumerate(tiles):
                tok0 = t0 + tt * 512
                gt = tok0 // 512
                tk = slice(tok0, tok0 + 512)
                A_t, B_t, b_A, b_B = A_ts[tt], B_ts[tt], b_As[tt], b_Bs[tt]
                if idx + PF < len(tiles):
                    issue_load(idx + PF)
                zl = loads.pop(idx)
                P.op("dve", lambda e, zl=zl, A_t=A_t: e.tensor_tensor(out=zl.ap, in0=zl.ap, in1=A_t, op=ALU.mult),
                     reads=[zl.buf, b_A], writes=[zl.buf])
                P.op("dve", lambda e, zl=zl, B_t=B_t: e.tensor_tensor(out=zl.ap, in0=zl.ap, in1=B_t, op=ALU.add),
                     reads=[zl.buf, b_B], writes=[zl.buf])
                if final:
                    ho = ring.get()
                    P.op("act", lambda e, zl=zl, ho=ho, c=c: e.activation(
                        out=ho.ap, in_=zl.ap, func=AF.Identity, scale=lnvec(l, k, 0, c), bias=lnvec(l, k, 1, c)),
                        reads=[zl.buf, b_const], writes=[ho.buf])
                    P.dma("sp", [(outT_v[:, c, tk], ho.ap)], ho.sem, reads=[ho.buf], writes=[b_out])
                else:
                    P.op("act", lambda e, zl=zl, c=c, tk=tk: e.activation(
                        out=XT[:, c, tk], in_=zl.ap, func=AF.Identity, scale=lnvec(l, k, 0, c), bias=lnvec(l, k, 1, c)),
                        reads=[zl.buf, b_const], writes=[b_XT[gt]])
                    ho = ring.get()
                    P.op("act", lambda e, zl=zl, ho=ho, c=c: e.activation(
                        out=ho.ap, in_=zl.ap, func=AF.Identity, scale=lnvec(l, k, 0, c, True), bias=lnvec(l, k, 1, c, True)),
                        reads=[zl.buf, b_const], writes=[ho.buf])
                    P.dma("sp", [(h32a_v[:, c, tk], ho.ap)], ho.sem, reads=[ho.buf], writes=[b_h32a[c][gt]])
                zl.pinned = False
                yield
        pending_norm[0] = gen()

    def ffn(l, which, final):
        wu = g_w_up[which]()[l]
        wd = g_w_dn[which]()[l]
        AT = view(at_off, [FC, 1024], BF16)
        for th in range(2):
            t0 = th * 1024
            b_AT = [P.fresh("AT0"), P.fresh("AT1")]
            bank_ring[0] = list(range(8))
            for f in range(FC):
                sl = wslot()
                wv = ws_view(sl, [16, 256])
                load_w(sl, [(wv[:, 0:8, :], wu[f][:, 0:8, :]), (wv[:, 8:16, :], wu[f][:, 8:16, :])])
                for tt in range(2):
                    gt = th * 2 + tt
                    tk = slice(t0 + tt * 512, t0 + (tt + 1) * 512)
                    bg, bu = bank(), bank()
                    mm(bg, lambda kc: wv[:, kc, 0:128], lambda kc: XT[:, kc, tk], 16, [b_ws[sl], b_XT[gt]])
                    mm(bu, lambda kc: wv[:, kc, 128:256], lambda kc: XT[:, kc, tk], 16, [b_ws[sl], b_XT[gt]])
                    sg = ring.get()
                    P.op("act", lambda e, sg=sg, bg=bg: e.activation(out=sg.ap, in_=ps[bg][:, :], func=AF.Silu),
                         reads=[b_ps[bg]], writes=[sg.buf])
                    P.op("dve", lambda e, sg=sg, bu=bu, f=f, tt=tt: e.tensor_tensor(
                        out=AT[:, f, tt * 512:(tt + 1) * 512], in0=sg.ap, in1=ps[bu][:, :], op=ALU.mult),
                        reads=[sg.buf, b_ps[bu]], writes=[b_AT[tt]])
                    bg_step()
            drain_bg()
            bank_ring[0] = [0, 1, 2, 3]
            for n in range(16):
                sl = wslot()
                wv = ws_view(sl, [FC, 128])
                load_w(sl, [(wv[:, 0:16, :], wd[n][:, 0:16, :]), (wv[:, 16:32, :], wd[n][:, 16:32, :]),
                            (wv[:, 32:44, :], wd[n][:, 32:44, :])])
                for tt in range(2):
                    b = bank()
                    mm(b, lambda fc: wv[:, fc, :], lambda fc: AT[:, fc, tt * 512:(tt + 1) * 512], FC,
                       [b_ws[sl], b_AT[tt]])
                    raw_x[0] = (l == 0 and which == 0)
                    resid_epilogue(b, n, t0 + tt * 512, tt, 0.5, n == 0, n == 15)
                    raw_x[0] = False
            normalize(l, 0 if which == 0 else 2, t0, 2, final)

    b_small = [Buf(f"sm{i}") for i in range(64)]
    small_i = [0]

    def col():
        i = small_i[0] % 64
        small_i[0] += 1
        return small[:, i:i + 1], b_small[i]

    def load_chunks(srcs, kc):
        sl = wslot()
        views = [ws_view(sl, [kc, 128], g * kc * 256) for g in range(len(srcs))]
        load_w(sl, [(views[g], srcs[g]) for g in range(len(srcs))])
        return sl, views

    def run_rr(gens):
        gens = list(gens)
        while gens:
            for g in list(gens):
                try:
                    next(g)
                except StopIteration:
                    gens.remove(g)

    def attn_pipeline(items, s_ring, p_ring, pT_ring, K, w_out_cols, evac):
        st = [dict() for _ in items]

        def stage_a(it, d):
            nk = it["nk"]
            b = bank()
            it["qk"](b)
            yield
            s = s_ring.get()
            sa = s.ap[:, 0:nk]
            if it["mask"] is not None:
                mk = it["mask"]
                P.op("dve", lambda e: e.scalar_tensor_tensor(out=sa, in0=ps[b][:, 0:nk], scalar=float(it["scale"]), in1=mk,
                                                             op0=ALU.mult, op1=ALU.add),
                     reads=[b_ps[b], it["mask_buf"]], writes=[s.buf])
            else:
                P.op("dve", lambda e: e.tensor_scalar(out=sa, in0=ps[b][:, 0:nk], scalar1=float(it["scale"]), scalar2=None,
                                                      op0=ALU.mult), reads=[b_ps[b]], writes=[s.buf])
            yield
            mx, b_mx = col()
            P.op("dve", lambda e: e.reduce_max(out=mx, in_=sa, axis=AX.X), reads=[s.buf], writes=[b_mx])
            yield
            if it["sink"] is not None:
                P.op("dve", lambda e: e.tensor_tensor(out=mx, in0=mx, in1=it["sink"], op=ALU.max),
                     reads=[b_mx, b_const], writes=[b_mx])
                yield
            nm, b_nm = col()
            rs, b_rs = col()
            P.op("dve", [lambda e: e.memset(rs, 0.0),
                         lambda e: e.tensor_scalar(out=nm, in0=mx, scalar1=-1.0, scalar2=None, op0=ALU.mult)],
                 reads=[b_mx], writes=[b_nm, b_rs])
            yield
            P.op("act", lambda e: e.activation(out=sa, in_=sa, func=AF.Exp, bias=nm, accum_out=rs),
                 reads=[s.buf, b_nm, b_rs], writes=[s.buf, b_rs])
            if it["sink"] is not None:
                es, b_es = col()
                P.op("act", lambda e: e.activation(out=es, in_=nm, func=AF.Exp, bias=it["sink"]),
                     reads=[b_nm, b_const], writes=[b_es])
                yield
                P.op("dve", lambda e: e.tensor_tensor(out=rs, in0=rs, in1=es, op=ALU.add), reads=[b_rs, b_es], writes=[b_rs])
            yield
            P.op("dve", lambda e: e.reciprocal(out=rs, in_=rs), reads=[b_rs], writes=[b_rs])
            yield
            pb = p_ring.get()
            P.op("dve", lambda e: e.tensor_scalar(out=pb.ap[:, 0:nk], in0=sa, scalar1=rs, scalar2=None, op0=ALU.mult),
                 reads=[s.buf, b_rs], writes=[pb.buf])
            d["pb"] = pb
            yield

        def stage_b(it, d):
            nk = it["nk"]
            pb = d["pb"]
            bt = bank()
            pst = ps[bt][:, :].bitcast(BF16)
            fns = []
            for kb in range(nk // 128):
                fns.append(lambda e, kb=kb: e.transpose(out=pst[:, kb * 128:(kb + 1) * 128],
                                                        in_=pb.ap[:, kb * 128:(kb + 1) * 128], identity=identb))
            P.op("pe", fns, reads=[pb.buf, b_const], writes=[b_ps[bt]])
            yield
            pT = pT_ring.get()
            P.op("act", lambda e: e.activation(out=pT.ap[:, 0:nk], in_=pst[:, 0:nk], func=AF.Copy),
                 reads=[b_ps[bt]], writes=[pT.buf])
            d["pT"] = pT
            yield

        def stage_c(i0, n_it):
            per = 512 // w_out_cols
            for j0 in range(0, n_it, per):
                bo = bank()
                cnt = min(per, n_it - j0)
                for j in range(cnt):
                    d = st[i0 + j0 + j]
                    items[i0 + j0 + j]["pv"](bo, j, cnt, d["pT"].ap, d["pT"].buf)
                evac(bo, i0 + j0, cnt)

        def stage_a_group(idxs):
            k = len(idxs)
            blk16 = (small_i[0] % 4) * 16
            small_i[0] += 1
            mxs, nms, rss, ess = [small[:, blk16 + q * 4: blk16 + q * 4 + k] for q in range(4)]
            bmx = [b_small[blk16 + j] for j in range(k)]
            bnm = [b_small[blk16 + 4 + j] for j in range(k)]
            brs = [b_small[blk16 + 8 + j] for j in range(k)]
            bes = [b_small[blk16 + 12 + j] for j in range(k)]
            its = [items[i] for i in idxs]
            sink = its[0]["sink"]
            banks, ss = [], []
            for j, it in enumerate(its):
                b = bank()
                it["qk"](b)
                banks.append(b)
            for j, it in enumerate(its):
                nk = it["nk"]
                b = banks[j]
                s_ = s_ring.get()
                ss.append(s_)
                sa = s_.ap[:, 0:nk]
                if it["mask"] is not None:
                    P.op("dve", lambda e, sa=sa, b=b, nk=nk, it=it: e.scalar_tensor_tensor(
                        out=sa, in0=ps[b][:, 0:nk], scalar=float(it["scale"]), in1=it["mask"], op0=ALU.mult, op1=ALU.add),
                        reads=[b_ps[b], it["mask_buf"]], writes=[s_.buf])
                else:
                    P.op("dve", lambda e, sa=sa, b=b, nk=nk, it=it: e.tensor_scalar(
                        out=sa, in0=ps[b][:, 0:nk], scalar1=float(it["scale"]), scalar2=None, op0=ALU.mult),
                        reads=[b_ps[b]], writes=[s_.buf])
            for j, it in enumerate(its):
                sa = ss[j].ap[:, 0:it["nk"]]
                P.op("dve", lambda e, sa=sa, j=j: e.reduce_max(out=mxs[:, j:j + 1], in_=sa, axis=AX.X),
                     reads=[ss[j].buf], writes=[bmx[j]])
            if sink is not None:
                P.op("dve", lambda e: e.tensor_scalar(out=mxs, in0=mxs, scalar1=sink, scalar2=None, op0=ALU.max),
                     reads=bmx + [b_const], writes=bmx)
            P.op("dve", [lambda e: e.memset(rss, 0.0),
                         lambda e: e.tensor_scalar(out=nms, in0=mxs, scalar1=-1.0, scalar2=None, op0=ALU.mult)],
                 reads=bmx, writes=bnm + brs)
            for j, it in enumerate(its):
                sa = ss[j].ap[:, 0:it["nk"]]
                P.op("act", lambda e, sa=sa, j=j: e.activation(out=sa, in_=sa, func=AF.Exp, bias=nms[:, j:j + 1],
                                                               accum_out=rss[:, j:j + 1]),
                     reads=[ss[j].buf, bnm[j], brs[j]], writes=[ss[j].buf, brs[j]])
            if sink is not None:
                P.op("act", lambda e: e.activation(out=ess, in_=nms, func=AF.Exp, bias=sink), reads=bnm + [b_const], writes=bes)
            return (idxs, its, ss, rss, ess, brs, bes, sink)

        def stage_a2(state):
            idxs, its, ss, rss, ess, brs, bes, sink = state
            if sink is not None:
                P.op("dve", lambda e: e.tensor_tensor(out=rss, in0=rss, in1=ess, op=ALU.add), reads=brs + bes, writes=brs)
            P.op("dve", lambda e: e.reciprocal(out=rss, in_=rss), reads=brs, writes=brs)
            for j, it in enumerate(its):
                nk = it["nk"]
                sa = ss[j].ap[:, 0:nk]
                pb = p_ring.get()
                P.op("dve", lambda e, sa=sa, pb=pb, nk=nk, j=j: e.tensor_scalar(
                    out=pb.ap[:, 0:nk], in0=sa, scalar1=rss[:, j:j + 1], scalar2=None, op0=ALU.mult),
                    reads=[ss[j].buf, brs[j]], writes=[pb.buf])
                st[idxs[j]]["pb"] = pb

        n = len(items)
        groups = [(i, min(K, n - i)) for i in range(0, n, K)]
        G = len(groups)
        a_state = {}
        for step in range(G + 3):
            if step < G:
                i0, c = groups[step]
                a_state[step] = stage_a_group(list(range(i0, i0 + c)))
            if 0 <= step - 1 < G:
                stage_a2(a_state.pop(step - 1))
            if 0 <= step - 2 < G:
                i0, c = groups[step - 2]
                run_rr(stage_b(items[i], st[i]) for i in range(i0, i0 + c))
            if 0 <= step - 3 < G:
                i0, c = groups[step - 3]
                stage_c(i0, c)

    def mixer(l, final):
        win = g_w_in()[l]
        wkv = g_w_kv()[l]
        memT_v = fm(g_memT())
        reg = [at_off]

        def lcarve(n):
            o = (reg[0] + 31) // 32 * 32
            reg[0] = o + n
            assert reg[0] <= at_off + AT_BYTES, (reg[0] - at_off, AT_BYTES)
            return o
        bank_ring[0] = list(range(8))
        memTb = view(lcarve(16 * 256 * 2), [16, 256], BF16)
        b_memT = P.fresh()
        mkT = view(lcarve(8 * 256 * 2), [8, 256], BF16)
        b_mk = P.fresh()
        mvt = view(lcarve(2 * 1024 * 2), [2, 1024], BF16)
        b_mv = P.fresh()
        m0_end = reg[0]
        P.dma("pool", [(memTb[:, 0:8, :], memT_v[:, 0:8, :]), (memTb[:, 8:16, :], memT_v[:, 8:16, :])], "xld",
              writes=[b_memT])
        for pr in range(4):
            sl, vs = load_chunks([wkv[2 * pr], wkv[2 * pr + 1]], 16)
            for g in range(2):
                b = bank()
                mm(b, lambda kc: vs[g][:, kc, :], lambda kc: memTb[:, kc, :], 16, [b_ws[sl], b_memT], ncols=256)
                P.op("act", lambda e, b=b, ch=2 * pr + g: e.activation(out=mkT[:, ch, :], in_=ps[b][:, 0:256], func=AF.Copy),
                     reads=[b_ps[b]], writes=[b_mk])
                bg_step(2)
        for ct in range(4):
            sl = wslot()
            wv = ws_view(sl, [16, 256])
            load_w(sl, [(wv[:, :, 0:128], wkv[8 + 2 * ct]), (wv[:, :, 128:256], wkv[9 + 2 * ct])])
            for mb in range(2):
                b = bank()
                mm(b, lambda kc: memTb[:, kc, mb * 128:(mb + 1) * 128], lambda kc: wv[:, kc, :], 16,
                   [b_ws[sl], b_memT], ncols=256)
                P.op("act", lambda e, b=b, mb=mb, ct=ct: e.activation(out=mvt[:, mb, ct * 256:(ct + 1) * 256],
                                                                      in_=ps[b][:, 0:256], func=AF.Copy),
                     reads=[b_ps[b]], writes=[b_mv])
                bg_step(2)
        drain_bg()
        rC = view(lcarve(8192), [T], F32)
        rS = view(lcarve(8192), [T], F32)
        b_rt = P.fresh()
        P.dma("sp", [(rC[0:32, :], g_ropeC()), (rS[0:32, :], g_ropeS())], "ld_rt", writes=[b_rt])
        for i in range(8):
            sl, (va, vg) = load_chunks([win[i], win[8 + i]], 16)
            for tt in range(4):
                tk = slice(tt * 512, (tt + 1) * 512)
                ba, bg = bank(), bank()
                mm(ba, lambda kc: va[:, kc, :], lambda kc: XT[:, kc, tk], 16, [b_ws[sl], b_XT[tt]])
                mm(bg, lambda kc: vg[:, kc, :], lambda kc: XT[:, kc, tk], 16, [b_ws[sl], b_XT[tt]])
                sg = ring.get()
                P.op("act", lambda e, sg=sg, bg=bg: e.activation(out=sg.ap, in_=ps[bg][:, :], func=AF.Sigmoid),
                     reads=[b_ps[bg]], writes=[sg.buf])
                hb = bring.get()
                P.op("dve", lambda e, sg=sg, ba=ba, hb=hb: e.tensor_tensor(out=hb.ap, in0=sg.ap, in1=ps[ba][:, :], op=ALU.mult),
                     reads=[sg.buf, b_ps[ba]], writes=[hb.buf])
                P.dma("sp", [(hc_v[:, i, tk], hb.ap)], hb.sem, reads=[hb.buf], writes=[b_hc[i][tt]])
        qk_chunks = [(16 + 2 * hp, 17 + 2 * hp, 2 * hp) for hp in range(4)] + [(24, 25, 8)]
        for (c0, c1, row0) in qk_chunks:
            sl, vs = load_chunks([win[c0], win[c1]], 16)
            for tt in range(4):
                tk = slice(tt * 512, (tt + 1) * 512)
                for g in range(2):
                    b = bank()
                    mm(b, lambda kc: vs[g][:, kc, :], lambda kc: XT[:, kc, tk], 16, [b_ws[sl], b_XT[tt]])
                    r = ring.get()
                    P.op("act", lambda e, r=r, b=b: e.activation(out=r.ap, in_=ps[b][:, :], func=AF.Copy),
                         reads=[b_ps[b]], writes=[r.buf])
                    ro = ring.get()
                    P.dma("act", [(ro.ap[0:16, :], r.ap[16:32, :]), (ro.ap[16:32, :], r.ap[0:16, :])], ro.sem,
                          reads=[r.buf], writes=[ro.buf])
                    P.op("dve", lambda e, ro=ro, tk=tk: e.tensor_tensor(out=ro.ap[0:32, :], in0=ro.ap[0:32, :], in1=rS[0:32, tk], op=ALU.mult),
                         reads=[ro.buf, b_rt], writes=[ro.buf])
                    P.op("dve", lambda e, r=r, tk=tk: e.tensor_tensor(out=r.ap[0:32, :], in0=r.ap[0:32, :], in1=rC[0:32, tk], op=ALU.mult),
                         reads=[r.buf, b_rt], writes=[r.buf])
                    P.op("dve", lambda e, r=r, ro=ro: e.tensor_tensor(out=r.ap[0:32, :], in0=r.ap[0:32, :], in1=ro.ap[0:32, :], op=ALU.add),
                         reads=[r.buf, ro.buf], writes=[r.buf])
                    qb_ = bring.get()
                    P.op("act", lambda e, r=r, qb_=qb_: e.activation(out=qb_.ap, in_=r.ap, func=AF.Copy),
                         reads=[r.buf], writes=[qb_.buf])
                    P.dma("sp", [(qk_v[:, row0 + g, tk], qb_.ap)], qb_.sem, reads=[qb_.buf], writes=[b_qk[row0 + g][tt]])
        sl = wslot()
        wv = ws_view(sl, [16, 256])
        load_w(sl, [(wv[:, :, 0:128], win[26]), (wv[:, :, 128:256], win[27])])
        for tb in range(16):
            b = bank()
            mm(b, lambda kc: XT[:, kc, tb * 128:(tb + 1) * 128], lambda kc: wv[:, kc, :], 16, [b_ws[sl], b_XT[tb // 4]],
               ncols=256)
            hb = bring.get()
            P.op("act", lambda e, hb=hb, b=b: e.activation(out=hb.ap[:, 0:256], in_=ps[b][:, 0:256], func=AF.Copy),
                 reads=[b_ps[b]], writes=[hb.buf])
            P.dma("sp", [(v_tok[tb * 128:(tb + 1) * 128, :], hb.ap[:, 0:256])], hb.sem, reads=[hb.buf], writes=[b_vtok[tb]])
        for hp in range(4):
            sl, vs = load_chunks([win[28 + 2 * hp], win[29 + 2 * hp]], 16)
            for tt in range(4):
                tk = slice(tt * 512, (tt + 1) * 512)
                for g in range(2):
                    b = bank()
                    mm(b, lambda kc: vs[g][:, kc, :], lambda kc: XT[:, kc, tk], 16, [b_ws[sl], b_XT[tt]])
                    hb = bring.get()
                    P.op("act", lambda e, hb=hb, b=b: e.activation(out=hb.ap, in_=ps[b][:, :], func=AF.Copy),
                         reads=[b_ps[b]], writes=[hb.buf])
                    P.dma("sp", [(qm_v[:, 2 * hp + g, tk], hb.ap)], hb.sem, reads=[hb.buf], writes=[b_qm[2 * hp + g][tt]])

        KI = 4

        def mk_rings():
            s_r = Ring([Tile(view(lcarve(1536), [384], F32), P.fresh(), None) for _ in range(2 * KI)])
            p_r = Ring([Tile(view(lcarve(768), [384], BF16), P.fresh(), None) for _ in range(2 * KI)])
            t_r = Ring([Tile(view(lcarve(768), [384], BF16), P.fresh(), None) for _ in range(2 * KI)])
            return s_r, p_r, t_r
        reg[0] = m0_end
        kTb = view(lcarve(4096), [T], BF16); b_kT = P.fresh()
        vt = view(lcarve(4096), [16, 128], BF16); b_vt = P.fresh()
        qTb = [view(lcarve(4096), [T], BF16) for _ in range(2)]; b_qT = [P.fresh(), P.fresh()]
        ast = [view(lcarve(4096), [T], BF16) for _ in range(2)]; b_ast = [P.fresh(), P.fresh()]
        s_r, p_r, t_r = mk_rings()

        def load_q(h):
            P.dma("sp", [(qTb[h % 2], qk_v[:, h, :])], f"ld_q{h % 2}", reads=b_qk[h], writes=[b_qT[h % 2]])
        load_q(0)
        for h in range(8):
            kvh = h // 4
            if h % 4 == 0:
                P.dma("sp", [(kTb, qk_v[:, 8 + kvh, :])], "ld_k", reads=b_qk[8 + kvh], writes=[b_kT])
                P.dma("sp", [(vt, v_tok.rearrange("(tb p) d -> p tb d", p=128)[:, :, kvh * 128:(kvh + 1) * 128])], "ld_vt",
                      reads=b_vtok, writes=[b_vt])
            qT = qTb[h % 2]
            if h + 1 < 8:
                load_q(h + 1)
            a_st = ast[h % 2]
            sink_ap = sink_sb[:, l * 8 + h:l * 8 + h + 1]
            items = []
            for blk in range(16):
                lo, hi = max(0, blk - 1), min(15, blk + 1)
                nk = (hi - lo + 1) * 128
                m0 = (lo - (blk - 1)) * 128

                def qk(b, blk=blk, lo=lo, hi=hi, nk=nk, qT=qT, h=h):
                    P.op("pe", [lambda e: e.matmul(ps[b][:, 0:nk], lhsT=qT[:, blk * 128:(blk + 1) * 128],
                                                   rhs=kTb[:, lo * 128:(hi + 1) * 128], start=True, stop=True)],
                         reads=[b_qT[h % 2], b_kT], writes=[b_ps[b]])

                def pv(bo, j, cnt, pT, b_pT, blk=blk, lo=lo, nk=nk):
                    fns = []
                    nkb = nk // 128
                    for kb in range(nkb):
                        fns.append(lambda e, kb=kb: e.matmul(ps[bo][:, j * 128:(j + 1) * 128], lhsT=vt[:, lo + kb, :],
                                                             rhs=pT[:, kb * 128:(kb + 1) * 128], start=(kb == 0),
                                                             stop=(kb == nkb - 1)))
                    P.op("pe", fns, reads=[b_vt, b_pT], writes=[b_ps[bo]])
                items.append(dict(qk=qk, nk=nk, mask=mask_sb[:, m0:m0 + nk], mask_buf=b_const, scale=128 ** -0.5,
                                  sink=sink_ap, pv=pv))

            def evac3(bo, i0, cnt, a_st=a_st, h=h):
                P.op("act", lambda e: e.activation(out=a_st[:, i0 * 128:(i0 + cnt) * 128], in_=ps[bo][:, 0:cnt * 128], func=AF.Copy),
                     reads=[b_ps[bo]], writes=[b_ast[h % 2]])
            attn_pipeline(items, s_r, p_r, t_r, KI, 128, evac3)
            P.dma("sp", [(attn_v[:, h, :], a_st)], f"st_ast{h % 2}", reads=[b_ast[h % 2]], writes=[b_attn[h]])

        reg[0] = m0_end
        qmb = [view(lcarve(8192), [2, T], BF16) for _ in range(2)]; b_qmb = [P.fresh(), P.fresh()]
        mst = [view(lcarve(8192), [2, T], BF16) for _ in range(2)]; b_mst = [P.fresh(), P.fresh()]
        s_r, p_r, t_r = mk_rings()
        for mh in range(4):
            qb = qmb[mh % 2]
            ms = mst[mh % 2]
            P.dma("sp", [(qb, qm_v[:, 2 * mh:2 * mh + 2, :])], f"ld_qm{mh % 2}", reads=b_qm[2 * mh] + b_qm[2 * mh + 1],
                  writes=[b_qmb[mh % 2]])
            items = []
            for blk in range(16):
                def qk(b, blk=blk, qb=qb, mh=mh):
                    P.op("pe", [lambda e, dc=dc: e.matmul(ps[b][:, 0:256], lhsT=qb[:, dc, blk * 128:(blk + 1) * 128],
                                                          rhs=mkT[:, 2 * mh + dc, :], start=(dc == 0), stop=(dc == 1))
                                for dc in range(2)],
                         reads=[b_qmb[mh % 2], b_mk], writes=[b_ps[b]])

                def pv(bo, j, cnt, pT, b_pT, blk=blk, mh=mh):
                    fns = []
                    for dc in range(2):
                        for mb in range(2):
                            c0 = dc * 256 + j * 128
                            fns.append(lambda e, dc=dc, mb=mb, c0=c0: e.matmul(
                                ps[bo][:, c0:c0 + 128], lhsT=mvt[:, mb, mh * 256 + dc * 128:mh * 256 + (dc + 1) * 128],
                                rhs=pT[:, mb * 128:(mb + 1) * 128], start=(mb == 0), stop=(mb == 1)))
                    P.op("pe", fns, reads=[b_mv, b_pT], writes=[b_ps[bo]])
                items.append(dict(qk=qk, nk=256, mask=None, mask_buf=None, scale=256 ** -0.5, sink=None, pv=pv))

            def evac4(bo, i0, cnt, ms=ms, mh=mh):
                assert cnt == 2
                P.op("act", lambda e: e.activation(out=ms[:, :, i0 * 128:(i0 + 2) * 128],
                                                   in_=ps[bo][:, 0:512].rearrange("p (a b) -> p a b", a=2), func=AF.Copy),
                     reads=[b_ps[bo]], writes=[b_mst[mh % 2]])
            attn_pipeline(items, s_r, p_r, t_r, KI, 256, evac4)
            P.dma("sp", [(memo_v[:, 2 * mh:2 * mh + 2, :], ms)], f"st_mst{mh % 2}", reads=[b_mst[mh % 2]], writes=[b_memo[mh]])

        reg[0] = at_off
        acc = view(lcarve(8 * T * 4), [8, T], F32)
        b_acc = [P.fresh() for _ in range(4)]
        diag = [view(lcarve(CW * 128 * 2), [CW, 128], BF16) for _ in range(2)]; b_dg = [P.fresh(), P.fresh()]
        hcp = [view(lcarve(2080 * 2), [2080], BF16) for _ in range(2)]; b_hcp = [P.fresh(), P.fresh()]
        for i in range(8):
            hb_, dg = hcp[i % 2], diag[i % 2]
            P.op("dve", [lambda e, hb_=hb_: e.memset(hb_[:, 0:16], 0.0), lambda e, hb_=hb_: e.memset(hb_[:, 2064:2080], 0.0)],
                 writes=[b_hcp[i % 2]])
            P.dma("sp", [(hb_[:, 16:2064], hc_v[:, i, :])], f"ld_hcp{i % 2}", reads=b_hc[i], writes=[b_hcp[i % 2]])
            P.op("dve", [lambda e, j=j, dg=dg, i=i: e.tensor_scalar(out=dg[:, j, :], in0=identb, scalar1=cvvec(l, i, j),
                                                                     scalar2=None, op0=ALU.mult) for j in range(CW)],
                 reads=[b_const], writes=[b_dg[i % 2]])
            for tt in range(4):
                b = bank()
                fns = [lambda e, j=j, dg=dg, hb_=hb_, tt=tt, b=b: e.matmul(
                    ps[b][:, :], lhsT=dg[:, j, :], rhs=hb_[:, tt * 512 + j + 1:tt * 512 + j + 513], start=(j == 0),
                    stop=(j == CW - 1)) for j in range(CW)]
                P.op("pe", fns, reads=[b_dg[i % 2], b_hcp[i % 2]], writes=[b_ps[b]])
                P.op("act", lambda e, b=b, i=i, tt=tt: e.activation(out=acc[:, i, tt * 512:(tt + 1) * 512], in_=ps[b][:, :],
                                                                    func=AF.Identity, bias=cvvec(l, i, CW)),
                     reads=[b_ps[b], b_const], writes=[b_acc[tt]])
        for tt in range(4):
            tk = slice(tt * 512, (tt + 1) * 512)
            s1, s2 = bank(), bank()
            for i in range(8):
                zb = bring.get()
                P.op("act", lambda e, zb=zb, i=i, tk=tk: e.activation(out=zb.ap, in_=acc[:, i, tk], func=AF.Copy),
                     reads=[b_acc[tt]], writes=[zb.buf])
                zq = bring.get()
                P.op("act", lambda e, zq=zq, i=i, tk=tk: e.activation(out=zq.ap, in_=acc[:, i, tk], func=AF.Square),
                     reads=[b_acc[tt]], writes=[zq.buf])
                P.op("pe", [lambda e, zb=zb, i=i, s1=s1: e.matmul(ps[s1][:, :], lhsT=onesb, rhs=zb.ap, start=(i == 0), stop=(i == 7)),
                            lambda e, zq=zq, i=i, s2=s2: e.matmul(ps[s2][:, :], lhsT=onesb, rhs=zq.ap, start=(i == 0), stop=(i == 7))],
                     reads=[zb.buf, zq.buf, b_const], writes=[b_ps[s1], b_ps[s2]])
            stats_to_AB(s1, s2, CONV_CH, 0)
            A_t, B_t, b_A, b_B = A_ts[0], B_ts[0], b_As[0], b_Bs[0]
            for i in range(8):
                y = ring.get()
                P.op("dve", lambda e, y=y, i=i, tk=tk: e.tensor_tensor(out=y.ap, in0=acc[:, i, tk], in1=A_t, op=ALU.mult),
                     reads=[b_acc[tt], b_A], writes=[y.buf])
                P.op("dve", lambda e, y=y: e.tensor_tensor(out=y.ap, in0=y.ap, in1=B_t, op=ALU.add),
                     reads=[y.buf, b_B], writes=[y.buf])
                cb = bring.get()
                P.op("act", lambda e, y=y, cb=cb, i=i: e.activation(out=cb.ap, in_=y.ap, func=AF.Silu,
                                                                    scale=cvvec(l, i, CW + 1), bias=cvvec(l, i, CW + 2)),
                     reads=[y.buf, b_const], writes=[cb.buf])
                P.dma("sp", [(conv_v[:, i, tk], cb.ap)], cb.sem, reads=[cb.buf], writes=[b_conv[i][tt]])

        wbr = [g_w_co()[l], g_w_wo()[l], g_w_mo()[l]]
        wo = g_w_out()[l]
        for th in range(2):
            t0 = th * 1024
            reg[0] = at_off
            br = [view(lcarve(8 * 1024 * 2), [8, 1024], BF16) for _ in range(3)]
            b_br = [P.fresh() for _ in range(3)]
            mg = view(lcarve(16 * 1024 * 2), [16, 1024], BF16)
            b_mg = [P.fresh(), P.fresh()]
            bank_ring[0] = list(range(8))
            rd_conv = [b_conv[i][2 * th + j] for i in range(8) for j in range(2)]
            P.dma("sp", [(br[0], conv_v[:, :, t0:t0 + 1024])], "ld_br0", reads=rd_conv, writes=[b_br[0]])
            P.dma("sp", [(br[1], attn_v[:, :, t0:t0 + 1024])], "ld_br1", reads=b_attn, writes=[b_br[1]])
            P.dma("sp", [(br[2], memo_v[:, :, t0:t0 + 1024])], "ld_br2", reads=b_memo, writes=[b_br[2]])
            for n in range(16):
                slA, (g0v, g1v) = load_chunks([win[36 + n], win[52 + n]], 16)
                slB = wslot()
                g2v = ws_view(slB, [16, 128], 0)
                bw = [ws_view(slB, [8, 128], 4096 + j * 2048) for j in range(3)]
                load_w(slB, [(g2v, win[68 + n])] + [(bw[j], wbr[j][n]) for j in range(3)])
                gv = [g0v, g1v, g2v]
                gsl = [slA, slA, slB]
                for tt in range(2):
                    gt = 2 * th + tt
                    tk = slice(t0 + tt * 512, t0 + (tt + 1) * 512)
                    lk = slice(tt * 512, (tt + 1) * 512)
                    gb = [bank() for _ in range(3)]
                    for j in range(3):
                        mm(gb[j], lambda kc: gv[j][:, kc, :], lambda kc: XT[:, kc, tk], 16, [b_ws[gsl[j]], b_XT[gt]])
                    yb = [bank() for _ in range(3)]
                    for j in range(3):
                        mm(yb[j], lambda kc: bw[j][:, kc, :], lambda kc: br[j][:, kc, lk], 8, [b_ws[slB], b_br[j]])
                    gts = [ring.get() for _ in range(3)]
                    for j in range(3):
                        P.op("act", lambda e, j=j, gts=gts, gb=gb: e.activation(out=gts[j].ap, in_=ps[gb[j]][:, :], func=AF.Sigmoid),
                             reads=[b_ps[gb[j]]], writes=[gts[j].buf])
                    m_, t_ = ring.get(), ring.get()
                    P.op("dve", lambda e, m_=m_, gts=gts, yb=yb: e.tensor_tensor(out=m_.ap, in0=gts[0].ap, in1=ps[yb[0]][:, :], op=ALU.mult),
                         reads=[gts[0].buf, b_ps[yb[0]]], writes=[m_.buf])
                    P.op("dve", lambda e, t_=t_, gts=gts, yb=yb: e.tensor_tensor(out=t_.ap, in0=gts[1].ap, in1=ps[yb[1]][:, :], op=ALU.mult),
                         reads=[gts[1].buf, b_ps[yb[1]]], writes=[t_.buf])
                    P.op("dve", lambda e, m_=m_, t_=t_: e.tensor_tensor(out=m_.ap, in0=m_.ap, in1=t_.ap, op=ALU.add),
                         reads=[m_.buf, t_.buf], writes=[m_.buf])
                    P.op("dve", lambda e, t_=t_, gts=gts, yb=yb: e.tensor_tensor(out=t_.ap, in0=gts[2].ap, in1=ps[yb[2]][:, :], op=ALU.mult),
                         reads=[gts[2].buf, b_ps[yb[2]]], writes=[t_.buf])
                    P.op("dve", lambda e, m_=m_, t_=t_, n=n, lk=lk: e.tensor_tensor(out=mg[:, n, lk], in0=m_.ap, in1=t_.ap, op=ALU.add),
                         reads=[m_.buf, t_.buf], writes=[b_mg[tt]])
                    bg_step()
            drain_bg()
            bank_ring[0] = [0, 1, 2, 3]
            for np_ in range(8):
                sl, vs = load_chunks([wo[2 * np_], wo[2 * np_ + 1]], 16)
                for g in range(2):
                    n = 2 * np_ + g
                    for tt in range(2):
                        b = bank()
                        lk = slice(tt * 512, (tt + 1) * 512)
                        mm(b, lambda kc: vs[g][:, kc, :], lambda kc: mg[:, kc, lk], 16, [b_ws[sl], b_mg[tt]])
                        resid_epilogue(b, n, t0 + tt * 512, tt, 1.0, n == 0, n == 15)
            normalize(l, 1, t0, 2, final)

    phases = []
    for l in range(depth):
        phases.append(("ffn", l, 0))
        phases.append(("mix", l))
        phases.append(("ffn", l, 1))
    if stop is not None:
        phases = phases[:stop]
    for i, ph in enumerate(phases):
        final = (i == len(phases) - 1)
        if ph[0] == "ffn":
            ffn(ph[1], ph[2], final)
        else:
            mixer(ph[1], final)

    drain_bg()
    fin = Buf()
    fin.w = b_out.w
    allb = [b_out] + [b for row in b_h32a for b in row] + b_attn + b_memo + [b for row in b_conv for b in row]
    P.wait_all("sp", allb)
    P.emit()
    print("n_ins", P.n_ins, "n_wait", P.n_wait, "arena", off[0])
    nc._used_inputs = set(_ins)
    return nc


def _c(a):
    return np.ascontiguousarray(a, dtype=np.float32)


def host_consts():
    pos = np.arange(T, dtype=np.float32)
    inv_freq = (np.float32(500000.0) ** (-np.arange(0, 32, 2, dtype=np.float32) / np.float32(32))).astype(np.float32)
    ang = (pos[:, None] * inv_freq[None, :]).astype(np.float32)
    cos, sin = np.cos(ang).astype(np.float32).T, np.sin(ang).astype(np.float32).T
    ropeC = np.concatenate([cos, cos], 0)
    ropeS = np.concatenate([-sin, sin], 0)
    i = np.arange(128)[:, None]
    c = np.arange(384)[None, :]
    mask = np.where((c >= i) & (c <= i + 256), 0.0, NEG).astype(np.float32)
    return {"ropeC": _c(ropeC), "ropeS": _c(ropeS), "mask": _c(mask), "ident": np.eye(128, dtype=np.float32)}


def host_weights(inp):
    o = {}
    for i in (1, 2):
        wu = np.asarray(inp[f"ffn{i}_w_up"]).reshape(L, 16, 128, 2, FC, 128)
        o[f"ffn{i}_w_up"] = _c(wu.transpose(0, 4, 2, 1, 3, 5)).reshape(L, FC, 128, 16, 256)
        wd = np.asarray(inp[f"ffn{i}_w_down"]).reshape(L, FC, 128, 16, 128)
        o[f"ffn{i}_w_down"] = _c(wd.transpose(0, 3, 2, 1, 4))
    wi = np.asarray(inp["w_in"]).reshape(L, 16, 128, NCH_IN, 128)
    o["w_in"] = _c(wi.transpose(0, 3, 2, 1, 4))
    for nm in ("conv_w_out", "win_w_o", "mem_w_o"):
        w = np.asarray(inp[nm]).reshape(L, 8, 128, 16, 128)
        o[nm] = _c(w.transpose(0, 3, 2, 1, 4))
    for nm in ("mem_w_kv", "w_out"):
        w = np.asarray(inp[nm]).reshape(L, 16, 128, 16, 128)
        o[nm] = _c(w.transpose(0, 3, 2, 1, 4))
    lnp = np.zeros((L, 3, 2, 16, 128), np.float32)
    for k in range(3):
        lnp[:, k, 0] = np.asarray(inp[f"ln{k + 1}_g"]).reshape(L, 16, 128)
        lnp[:, k, 1] = np.asarray(inp[f"ln{k + 1}_b"]).reshape(L, 16, 128)
    o["lnp"] = _c(lnp.transpose(4, 0, 1, 2, 3)).reshape(128, -1)
    cvp = np.zeros((L, 8, CW + 3, 128), np.float32)
    cvp[:, :, :CW] = np.asarray(inp["conv_dw_w"]).reshape(L, CW, 8, 128).transpose(0, 2, 1, 3)
    cvp[:, :, CW] = np.asarray(inp["conv_dw_b"]).reshape(L, 8, 128)
    cvp[:, :, CW + 1] = np.asarray(inp["conv_ln_g"]).reshape(L, 8, 128)
    cvp[:, :, CW + 2] = np.asarray(inp["conv_ln_b"]).reshape(L, 8, 128)
    o["cvp"] = _c(cvp.transpose(3, 0, 1, 2)).reshape(128, -1)
    o["sinkb"] = _c(np.broadcast_to(np.asarray(inp["win_sink"]).reshape(1, L * 8), (128, L * 8)))
    o.update(host_consts())
    return o


_NC_CACHE = {}


def kernel(**inputs):
    x = np.asarray(inputs["x"], dtype=np.float32)
    mem = np.asarray(inputs["mem"], dtype=np.float32)
    nb = x.shape[0]
    shared = host_weights(inputs)
    in_maps = []
    for b in range(nb):
        m = dict(shared)
        m["xT"] = _c(x[b].T)
        m["memT"] = _c(mem[b].T)
        in_maps.append(m)
    if "nc" not in _NC_CACHE:
        _NC_CACHE["nc"] = build()
    nc = _NC_CACHE["nc"]
    in_maps = [{k: v for k, v in m.items() if k in nc._used_inputs} for m in in_maps]
    res = run_bass_kernel_spmd(nc, in_maps, core_ids=list(range(nb)))
    out = np.stack([np.ascontiguousarray(r["outT"].T) for r in res.results], 0)
    return out.astype(np.float32)
```

```python
import contextlib
import numpy as np
import concourse.bass as bass
import concourse.mybir as mybir
from concourse.bass_utils import run_bass_kernel_spmd

F32 = mybir.dt.float32
BF16 = mybir.dt.bfloat16
U8 = mybir.dt.uint8
AF = mybir.ActivationFunctionType
ALU = mybir.AluOpType
AX = mybir.AxisListType

D = 2048
T = 2048
L = 2
NMEM = 256
DFF = 5632
FC = DFF // 128
CONV_CH = 1024
CW = 31
IN_WIDTH = 10752
NCH_IN = IN_WIDTH // 128
ALPHA = float((2 * L) ** 0.25)
EPS = 1e-5
NEG = -1e30
ENGS = ["pe", "act", "dve", "pool", "sp"]


class Buf:
    __slots__ = ("name", "w", "r")

    def __init__(self, name=""):
        self.name = name
        self.w = None
        self.r = {}


class Prog:
    def __init__(self, nc):
        self.nc = nc
        self.q = {e: [] for e in ENGS}
        self.cnt = {}
        self.seen = {e: {} for e in ENGS}
        self.semnames = []
        self.n_wait = 0
        self.n_ins = 0
        for e in ENGS:
            self._sem("eng_" + e)

    def _sem(self, key):
        if key not in self.cnt:
            self.cnt[key] = 0
            self.semnames.append(key)
        return key

    def fresh(self, name=""):
        b = Buf(name)
        b.r = {k: v for k, v in self.cnt.items() if v > 0}
        return b

    def _deps(self, eng, reads, writes):
        deps = {}

        def add(k, v):
            if deps.get(k, 0) < v:
                deps[k] = v
        for b in reads:
            if b.w is not None:
                add(*b.w)
        for b in writes:
            if b.w is not None:
                add(*b.w)
            for k, v in b.r.items():
                add(k, v)
        waits = []
        own = "eng_" + eng
        seen = self.seen[eng]
        for k, v in deps.items():
            if eng == "pe" and k == own:
                continue
            if seen.get(k, 0) < v:
                seen[k] = v
                waits.append((k, v))
        return waits

    def _commit(self, ev, reads, writes):
        k, v = ev
        for b in reads:
            if b.r.get(k, 0) < v:
                b.r[k] = v
        for b in writes:
            b.w = ev
            b.r = {}

    def op(self, eng, fns, reads=(), writes=()):
        if callable(fns):
            fns = [fns]
        waits = self._deps(eng, reads, writes)
        key = "eng_" + eng
        self.cnt[key] += 1
        ev = (key, self.cnt[key])
        self.q[eng].append(("op", waits, fns, key))
        self._commit(ev, reads, writes)
        self.n_wait += len(waits)
        self.n_ins += len(fns)
        return ev

    def dma(self, eng, pairs, semkey, reads=(), writes=(), **kw):
        if isinstance(pairs, tuple):
            pairs = [pairs]
        self._sem(semkey)
        waits = self._deps(eng, reads, writes)
        self.cnt[semkey] += 16 * len(pairs)
        ev = (semkey, self.cnt[semkey])
        self.q[eng].append(("dma", waits, (pairs, kw), semkey))
        self._commit(ev, reads, writes)
        self.n_wait += len(waits)
        self.n_ins += len(pairs)
        return ev

    def wait_all(self, eng, bufs):
        waits = self._deps(eng, bufs, ())
        self.q[eng].append(("wait", waits, None, None))

    def emit(self):
        nc = self.nc
        with contextlib.ExitStack() as st:
            sems = {}
            for k in self.semnames:
                sems[k] = st.enter_context(nc.semaphore(k))
            block = st.enter_context(nc.Block())

            def run(e, items):
                for kind, waits, payload, key in items:
                    for (k, v) in waits:
                        e.wait_ge(sems[k], v)
                    if kind == "op":
                        ins = None
                        for f in payload:
                            ins = f(e)
                        ins.then_inc(sems[key], 1)
                    elif kind == "dma":
                        pairs, kw = payload
                        for (o, i) in pairs:
                            e.dma_start(out=o, in_=i, **kw).then_inc(sems[key], 16)

            @block.tensor
            def _(e):
                run(e, self.q["pe"])

            @block.scalar
            def _(e):
                run(e, self.q["act"])

            @block.vector
            def _(e):
                run(e, self.q["dve"])

            @block.gpsimd
            def _(e):
                run(e, self.q["pool"])

            @block.sync
            def _(e):
                run(e, self.q["sp"])


class Tile:
    __slots__ = ("ap", "buf", "sem", "pinned")

    def __init__(self, ap, buf, sem):
        self.ap = ap
        self.buf = buf
        self.sem = sem
        self.pinned = False


class Ring:
    def __init__(self, tiles):
        self.tiles = tiles
        self.i = 0

    def get(self, pin=False):
        for _ in range(2 * len(self.tiles)):
            t = self.tiles[self.i % len(self.tiles)]
            self.i += 1
            if not t.pinned:
                t.pinned = pin
                return t
        raise RuntimeError("ring exhausted (all tiles pinned)")


def build(depth=L, stop=None, dbg=False):
    nc = bass.Bass("TRN2", target_bir_lowering=False)
    P = Prog(nc)

    def din(name, shape, dt=F32):
        return nc.dram_tensor(name, list(shape), dt, kind="ExternalInput").ap()

    def dscr(name, shape, dt=F32, out=False):
        if out:
            return nc.dram_tensor(name, list(shape), dt, kind="ExternalOutput").ap()
        return nc.dram_tensor(name, list(shape), dt).ap()

    _ins = {}

    def lazy(name, shape):
        def get():
            if name not in _ins:
                _ins[name] = din(name, shape)
            return _ins[name]
        return get
    xT = din("xT", [D, T]); _ins["xT"] = xT
    g_memT = lazy("memT", [D, NMEM])
    g_w_up = [lazy(f"ffn{i}_w_up", [L, FC, 128, 16, 256]) for i in (1, 2)]
    g_w_dn = [lazy(f"ffn{i}_w_down", [L, 16, 128, FC, 128]) for i in (1, 2)]
    g_w_in = lazy("w_in", [L, NCH_IN, 128, 16, 128])
    g_w_co = lazy("conv_w_out", [L, 16, 128, 8, 128])
    g_w_wo = lazy("win_w_o", [L, 16, 128, 8, 128])
    g_w_mo = lazy("mem_w_o", [L, 16, 128, 8, 128])
    g_w_kv = lazy("mem_w_kv", [L, 16, 128, 16, 128])
    g_w_out = lazy("w_out", [L, 16, 128, 16, 128])
    lnp = din("lnp", [128, L * 3 * 2 * 16]); _ins["lnp"] = lnp
    cvp = din("cvp", [128, L * 8 * (CW + 3)]); _ins["cvp"] = cvp
    sinkb = din("sinkb", [128, L * 8]); _ins["sinkb"] = sinkb
    g_ropeC = lazy("ropeC", [32, T])
    g_ropeS = lazy("ropeS", [32, T])
    g_mask = lazy("mask", [128, 384])
    identd = din("ident", [128, 128]); _ins["ident"] = identd

    last_dbg = dbg
    outT = dscr("outT", [D, T], out=True)
    h32a = dscr("h32a", [D, T], out=last_dbg)
    zd = dscr("zd", [D, T])
    hc_d = dscr("hc_d", [CONV_CH, T], BF16)
    qkb_d = dscr("qkb_d", [1280, T], BF16)
    v_tok = dscr("v_tok", [T, 256], BF16)
    qm_d = dscr("qm_d", [1024, T], BF16)
    conv_d = dscr("conv_d", [1024, T], BF16, out=last_dbg)
    attn_d = dscr("attn_d", [1024, T], BF16, out=last_dbg)
    memo_d = dscr("memo_d", [1024, T], BF16, out=last_dbg)

    fm = lambda ap: ap.rearrange("(c p) t -> p c t", p=128)
    h32a_v, zd_v, outT_v, xT_v = fm(h32a), fm(zd), fm(outT), fm(xT)
    hc_v, qk_v, qm_v, conv_v, attn_v, memo_v = fm(hc_d), fm(qkb_d), fm(qm_d), fm(conv_d), fm(attn_d), fm(memo_d)

    def grid(n, m, nm):
        return [[Buf(f"{nm}{i}_{j}") for j in range(m)] for i in range(n)]
    b_h32a = grid(16, 4, "h32a")
    b_zd = grid(16, 4, "zd")
    b_hc = grid(8, 4, "hc")
    b_qk = grid(10, 4, "qk")
    b_vtok = [Buf() for _ in range(16)]
    b_qm = grid(8, 4, "qm")
    b_conv = grid(8, 4, "convd")
    b_attn = [Buf() for _ in range(8)]
    b_memo = [Buf() for _ in range(4)]
    b_out = Buf()

    ARENA = 212800
    arena = nc.alloc_sbuf_tensor("arena", [128, ARENA], U8)
    off = [0]

    def carve(nbytes, at=None):
        o = off[0] if at is None else at
        o = (o + 31) // 32 * 32
        if at is None:
            off[0] = o + nbytes
        assert o + nbytes <= ARENA, (o, nbytes)
        return o

    def view(o, shape, dt):
        esz = 2 if dt == BF16 else 4
        n = int(np.prod(shape)) * esz
        ap = arena[:, o:o + n].bitcast(dt)
        if len(shape) == 2:
            ap = ap.rearrange("p (a b) -> p a b", b=shape[1])
        elif len(shape) == 3:
            ap = ap.rearrange("p (a b c) -> p a b c", b=shape[1], c=shape[2])
        return ap

    XT = view(carve(16 * T * 2), [16, T], BF16)
    b_XT = [Buf(f"XT{i}") for i in range(4)]
    WSB = 11264
    ws_off = [carve(WSB), carve(WSB)]
    b_ws = [Buf("ws0"), Buf("ws1")]
    wsi = [0]
    c_off = carve(8192)
    co = [c_off]

    def ccarve(n):
        o = (co[0] + 31) // 32 * 32
        co[0] = o + n
        assert co[0] <= c_off + 8192
        return o
    ident32 = view(ccarve(512), [128], F32)
    identb = view(ccarve(256), [128], BF16)
    onesb = view(ccarve(256), [128], BF16)
    lnp_sb = view(ccarve(L * 3 * 2 * 16 * 4), [L * 3 * 2 * 16], F32)
    lnpa_sb = view(ccarve(L * 3 * 2 * 16 * 4), [L * 3 * 2 * 16], F32)
    cvp_sb = view(ccarve(L * 8 * (CW + 3) * 4), [L * 8 * (CW + 3)], F32)
    sink_sb = view(ccarve(L * 8 * 4), [L * 8], F32)
    small = view(ccarve(64 * 4), [64], F32)
    mask_sb = view(ccarve(1536), [384], F32)
    b_const = Buf("const")
    AT_BYTES = FC * 1024 * 2
    at_off = carve(AT_BYTES)
    A_ts = [view(carve(2048), [512], F32) for _ in range(2)]
    B_ts = [view(carve(2048), [512], F32) for _ in range(2)]
    b_As, b_Bs = [Buf("A0"), Buf("A1")], [Buf("B0"), Buf("B1")]
    NR = 7
    ring = Ring([Tile(view(carve(2048), [512], F32), Buf(f"r{i}"), f"r{i}") for i in range(NR)])
    NB = 3
    bring = Ring([Tile(view(carve(1024), [512], BF16), Buf(f"b{i}"), f"b{i}") for i in range(NB)])
    ps = [nc.alloc_psum_tensor(f"ps{i}", [128, 512], F32) for i in range(8)]
    b_ps = [Buf(f"ps{i}") for i in range(8)]
    bank_ring = [list(range(8))]
    bank_i = [0]

    def bank():
        r = bank_ring[0]
        b = r[bank_i[0] % len(r)]
        bank_i[0] += 1
        return b

    def wslot():
        s = wsi[0] % 2
        wsi[0] += 1
        return s

    def ws_view(sl, shape, byte_off=0):
        return view(ws_off[sl] + byte_off, shape, BF16)

    def load_w(sl, pieces):
        P.dma("pool", pieces, f"w{sl}", writes=[b_ws[sl]])

    def mm(b, lhs_fn, rhs_fn, kc_n, reads, ncols=512, extra_writes=()):
        fns = []
        for kc in range(kc_n):
            lh, rh = lhs_fn(kc), rhs_fn(kc)
            fns.append(lambda e, kc=kc, lh=lh, rh=rh: e.matmul(ps[b][:, 0:ncols], lhsT=lh, rhs=rh,
                                                               start=(kc == 0), stop=(kc == kc_n - 1)))
        P.op("pe", fns, reads=reads, writes=[b_ps[b]] + list(extra_writes))

    P.dma("sp", [(ident32, identd), (lnp_sb, lnp), (cvp_sb, cvp), (sink_sb, sinkb), (mask_sb, g_mask())], "cld", writes=[b_const])
    P.op("dve", lambda e: e.tensor_copy(out=identb, in_=ident32), reads=[b_const], writes=[b_const])
    P.op("dve", lambda e: e.memset(onesb, 1.0), writes=[b_const])
    P.op("dve", lambda e: e.tensor_scalar(out=lnpa_sb, in0=lnp_sb, scalar1=ALPHA, scalar2=None, op0=ALU.mult),
         reads=[b_const], writes=[b_const])

    def lnvec(l, k, gb, c, scaled=False):
        base = ((l * 3 + k) * 2 + gb) * 16 + c
        t = lnpa_sb if scaled else lnp_sb
        return t[:, base:base + 1]

    def cvvec(l, i, j):
        base = (l * 8 + i) * (CW + 3) + j
        return cvp_sb[:, base:base + 1]

    for tt in range(4):
        P.dma("pool", [(XT[:, :, tt * 512:(tt + 1) * 512], xT_v[:, :, tt * 512:(tt + 1) * 512])], "xld",
              writes=[b_XT[tt]])

    S1 = [4, 6]
    S2 = [5, 7]

    pending_stats = []

    def flush_stats():
        while pending_stats:
            pending_stats.pop(0)()

    raw_x = [False]

    def resid_epilogue(b, c, tok0, tt, scale, first, last):
        flush_stats()
        gt = tok0 // 512
        tk = slice(tok0, tok0 + 512)
        hres = ring.get()
        zt = ring.get()
        if raw_x[0]:
            P.dma("sp", [(hres.ap, xT_v[:, c, tk])], hres.sem, writes=[hres.buf])
            P.op("dve", lambda e: e.tensor_scalar(out=zt.ap, in0=ps[b][:, :], scalar1=float(scale), scalar2=None, op0=ALU.mult),
                 reads=[b_ps[b]], writes=[zt.buf])
            P.op("dve", lambda e: e.scalar_tensor_tensor(out=zt.ap, in0=hres.ap, scalar=ALPHA, in1=zt.ap,
                                                         op0=ALU.mult, op1=ALU.add),
                 reads=[hres.buf, zt.buf], writes=[zt.buf])
        else:
            P.dma("sp", [(hres.ap, h32a_v[:, c, tk])], hres.sem, reads=[b_h32a[c][gt]], writes=[hres.buf])
            P.op("dve", lambda e: e.scalar_tensor_tensor(out=zt.ap, in0=ps[b][:, :], scalar=float(scale), in1=hres.ap,
                                                         op0=ALU.mult, op1=ALU.add),
                 reads=[b_ps[b], hres.buf], writes=[zt.buf])
        P.dma("sp", [(zd_v[:, c, tk], zt.ap)], zt.sem, reads=[zt.buf], writes=[b_zd[c][gt]])
        zb = bring.get()
        P.op("act", lambda e: e.activation(out=zb.ap, in_=zt.ap, func=AF.Copy), reads=[zt.buf], writes=[zb.buf])
        zq = bring.get()
        P.op("act", lambda e: e.activation(out=zq.ap, in_=zt.ap, func=AF.Square), reads=[zt.buf], writes=[zq.buf])
        pending_stats.append(lambda: P.op(
            "pe", [lambda e: e.matmul(ps[S1[tt]][:, :], lhsT=onesb, rhs=zb.ap, start=first, stop=last),
                   lambda e: e.matmul(ps[S2[tt]][:, :], lhsT=onesb, rhs=zq.ap, start=first, stop=last)],
            reads=[zb.buf, zq.buf, b_const], writes=[b_ps[S1[tt]], b_ps[S2[tt]]]))

    def stats_to_AB(s1b, s2b, dn, ai=0):
        A_t, B_t, b_A, b_B = A_ts[ai], B_ts[ai], b_As[ai], b_Bs[ai]
        m = ring.get()
        P.op("dve", lambda e: e.tensor_scalar(out=m.ap, in0=ps[s1b][:, :], scalar1=1.0 / dn, scalar2=None, op0=ALU.mult),
             reads=[b_ps[s1b]], writes=[m.buf])
        v = ring.get()
        P.op("dve", lambda e: e.tensor_tensor(out=v.ap, in0=m.ap, in1=m.ap, op=ALU.mult), reads=[m.buf], writes=[v.buf])
        P.op("dve", lambda e: e.scalar_tensor_tensor(out=v.ap, in0=ps[s2b][:, :], scalar=1.0 / dn, in1=v.ap,
                                                     op0=ALU.mult, op1=ALU.subtract),
             reads=[b_ps[s2b], v.buf], writes=[v.buf])
        P.op("dve", lambda e: e.tensor_scalar(out=v.ap, in0=v.ap, scalar1=EPS, scalar2=None, op0=ALU.add),
             reads=[v.buf], writes=[v.buf])
        P.op("act", lambda e: e.activation(out=v.ap, in_=v.ap, func=AF.Sqrt), reads=[v.buf], writes=[v.buf])
        P.op("dve", lambda e: e.reciprocal(out=A_t, in_=v.ap), reads=[v.buf], writes=[b_A])
        P.op("dve", lambda e: e.scalar_tensor_tensor(out=B_t, in0=m.ap, scalar=-1.0, in1=A_t, op0=ALU.mult, op1=ALU.mult),
             reads=[m.buf, b_A], writes=[b_B])

    pending_norm = [None]

    def bg_step(n=1):
        g = pending_norm[0]
        if g is None:
            return
        for _ in range(n):
            try:
                next(g)
            except StopIteration:
                pending_norm[0] = None
                return

    def drain_bg():
        while pending_norm[0] is not None:
            bg_step(8)

    def normalize(l, k, t0, ntt, final):
        assert pending_norm[0] is None
        flush_stats()
        for tt in range(ntt):
            stats_to_AB(S1[tt], S2[tt], D, tt)

        def gen():
            PF = 1
            tiles = [(tt, c) for tt in range(ntt) for c in range(16)]
            loads = {}

            def issue_load(idx):
                tt, c = tiles[idx]
                tok0 = t0 + tt * 512
                zl = ring.get(pin=True)
                P.dma("sp", [(zl.ap, zd_v[:, c, tok0:tok0 + 512])], zl.sem, reads=[b_zd[c][tok0 // 512]], writes=[zl.buf])
                loads[idx] = zl
            for idx in range(min(PF, len(tiles))):
                issue_load(idx)
            for idx, (tt, c) in enumerate(tiles):
                tok0 = t0 + tt * 512
                gt = tok0 // 512
                tk = slice(tok0, tok0 + 512)
                A_t, B_t, b_A, b_B = A_ts[tt], B_ts[tt], b_As[tt], b_Bs[tt]
                if idx + PF < len(tiles):
                    issue_load(idx + PF)
                zl = loads.pop(idx)
                P.op("dve", lambda e, zl=zl, A_t=A_t: e.tensor_tensor(out=zl.ap, in0=zl.ap, in1=A_t, op=ALU.mult),
                     reads=[zl.buf, b_A], writes=[zl.buf])
                P.op("dve", lambda e, zl=zl, B_t=B_t: e.tensor_tensor(out=zl.ap, in0=zl.ap, in1=B_t, op=ALU.add),
                     reads=[zl.buf, b_B], writes=[zl.buf])
                if final:
                    ho = ring.get()
                    P.op("act", lambda e, zl=zl, ho=ho, c=c: e.activation(
                        out=ho.ap, in_=zl.ap, func=AF.Identity, scale=lnvec(l, k, 0, c), bias=lnvec(l, k, 1, c)),
                        reads=[zl.buf, b_const], writes=[ho.buf])
                    P.dma("sp", [(outT_v[:, c, tk], ho.ap)], ho.sem, reads=[ho.buf], writes=[b_out])
                else:
                    P.op("act", lambda e, zl=zl, c=c, tk=tk: e.activation(
                        out=XT[:, c, tk], in_=zl.ap, func=AF.Identity, scale=lnvec(l, k, 0, c), bias=lnvec(l, k, 1, c)),
                        reads=[zl.buf, b_const], writes=[b_XT[gt]])
                    ho = ring.get()
                    P.op("act", lambda e, zl=zl, ho=ho, c=c: e.activation(
                        out=ho.ap, in_=zl.ap, func=AF.Identity, scale=lnvec(l, k, 0, c, True), bias=lnvec(l, k, 1, c, True)),
                        reads=[zl.buf, b_const], writes=[ho.buf])
                    P.dma("sp", [(h32a_v[:, c, tk], ho.ap)], ho.sem, reads=[ho.buf], writes=[b_h32a[c][gt]])
                zl.pinned = False
                yield
        pending_norm[0] = gen()

    def ffn(l, which, final):
        wu = g_w_up[which]()[l]
        wd = g_w_dn[which]()[l]
        AT = view(at_off, [FC, 1024], BF16)
        for th in range(2):
            t0 = th * 1024
            b_AT = [P.fresh("AT0"), P.fresh("AT1")]
            bank_ring[0] = list(range(8))
            for f in range(FC):
                sl = wslot()
                wv = ws_view(sl, [16, 256])
                load_w(sl, [(wv[:, 0:8, :], wu[f][:, 0:8, :]), (wv[:, 8:16, :], wu[f][:, 8:16, :])])
                for tt in range(2):
                    gt = th * 2 + tt
                    tk = slice(t0 + tt * 512, t0 + (tt + 1) * 512)
                    bg, bu = bank(), bank()
                    mm(bg, lambda kc: wv[:, kc, 0:128], lambda kc: XT[:, kc, tk], 16, [b_ws[sl], b_XT[gt]])
                    mm(bu, lambda kc: wv[:, kc, 128:256], lambda kc: XT[:, kc, tk], 16, [b_ws[sl], b_XT[gt]])
                    sg = ring.get()
                    P.op("act", lambda e, sg=sg, bg=bg: e.activation(out=sg.ap, in_=ps[bg][:, :], func=AF.Silu),
                         reads=[b_ps[bg]], writes=[sg.buf])
                    P.op("dve", lambda e, sg=sg, bu=bu, f=f, tt=tt: e.tensor_tensor(
                        out=AT[:, f, tt * 512:(tt + 1) * 512], in0=sg.ap, in1=ps[bu][:, :], op=ALU.mult),
                        reads=[sg.buf, b_ps[bu]], writes=[b_AT[tt]])
                    bg_step()
            drain_bg()
            bank_ring[0] = [0, 1, 2, 3]
            for n in range(16):
                sl = wslot()
                wv = ws_view(sl, [FC, 128])
                load_w(sl, [(wv[:, 0:16, :], wd[n][:, 0:16, :]), (wv[:, 16:32, :], wd[n][:, 16:32, :]),
                            (wv[:, 32:44, :], wd[n][:, 32:44, :])])
                for tt in range(2):
                    b = bank()
                    mm(b, lambda fc: wv[:, fc, :], lambda fc: AT[:, fc, tt * 512:(tt + 1) * 512], FC,
                       [b_ws[sl], b_AT[tt]])
                    raw_x[0] = (l == 0 and which == 0)
                    resid_epilogue(b, n, t0 + tt * 512, tt, 0.5, n == 0, n == 15)
                    raw_x[0] = False
            normalize(l, 0 if which == 0 else 2, t0, 2, final)

    b_small = [Buf(f"sm{i}") for i in range(64)]
    small_i = [0]

    def col():
        i = small_i[0] % 64
        small_i[0] += 1
        return small[:, i:i + 1], b_small[i]

    def load_chunks(srcs, kc):
        sl = wslot()
        views = [ws_view(sl, [kc, 128], g * kc * 256) for g in range(len(srcs))]
        load_w(sl, [(views[g], srcs[g]) for g in range(len(srcs))])
        return sl, views

    def run_rr(gens):
        gens = list(gens)
        while gens:
            for g in list(gens):
                try:
                    next(g)
                except StopIteration:
                    gens.remove(g)

    def attn_pipeline(items, s_ring, p_ring, pT_ring, K, w_out_cols, evac):
        st = [dict() for _ in items]

        def stage_a(it, d):
            nk = it["nk"]
            b = bank()
            it["qk"](b)
            yield
            s = s_ring.get()
            sa = s.ap[:, 0:nk]
            if it["mask"] is not None:
                mk = it["mask"]
                P.op("dve", lambda e: e.scalar_tensor_tensor(out=sa, in0=ps[b][:, 0:nk], scalar=float(it["scale"]), in1=mk,
                                                             op0=ALU.mult, op1=ALU.add),
                     reads=[b_ps[b], it["mask_buf"]], writes=[s.buf])
            else:
                P.op("dve", lambda e: e.tensor_scalar(out=sa, in0=ps[b][:, 0:nk], scalar1=float(it["scale"]), scalar2=None,
                                                      op0=ALU.mult), reads=[b_ps[b]], writes=[s.buf])
            yield
            mx, b_mx = col()
            P.op("dve", lambda e: e.reduce_max(out=mx, in_=sa, axis=AX.X), reads=[s.buf], writes=[b_mx])
            yield
            if it["sink"] is not None:
                P.op("dve", lambda e: e.tensor_tensor(out=mx, in0=mx, in1=it["sink"], op=ALU.max),
                     reads=[b_mx, b_const], writes=[b_mx])
                yield
            nm, b_nm = col()
            rs, b_rs = col()
            P.op("dve", [lambda e: e.memset(rs, 0.0),
                         lambda e: e.tensor_scalar(out=nm, in0=mx, scalar1=-1.0, scalar2=None, op0=ALU.mult)],
                 reads=[b_mx], writes=[b_nm, b_rs])
            yield
            P.op("act", lambda e: e.activation(out=sa, in_=sa, func=AF.Exp, bias=nm, accum_out=rs),
                 reads=[s.buf, b_nm, b_rs], writes=[s.buf, b_rs])
            if it["sink"] is not None:
                es, b_es = col()
                P.op("act", lambda e: e.activation(out=es, in_=nm, func=AF.Exp, bias=it["sink"]),
                     reads=[b_nm, b_const], writes=[b_es])
                yield
                P.op("dve", lambda e: e.tensor_tensor(out=rs, in0=rs, in1=es, op=ALU.add), reads=[b_rs, b_es], writes=[b_rs])
            yield
            P.op("dve", lambda e: e.reciprocal(out=rs, in_=rs), reads=[b_rs], writes=[b_rs])
            yield
            pb = p_ring.get()
            P.op("dve", lambda e: e.tensor_scalar(out=pb.ap[:, 0:nk], in0=sa, scalar1=rs, scalar2=None, op0=ALU.mult),
                 reads=[s.buf, b_rs], writes=[pb.buf])
            d["pb"] = pb
            yield

        def stage_b(it, d):
            nk = it["nk"]
            pb = d["pb"]
            bt = bank()
            pst = ps[bt][:, :].bitcast(BF16)
            fns = []
            for kb in range(nk // 128):
                fns.append(lambda e, kb=kb: e.transpose(out=pst[:, kb * 128:(kb + 1) * 128],
                                                        in_=pb.ap[:, kb * 128:(kb + 1) * 128], identity=identb))
            P.op("pe", fns, reads=[pb.buf, b_const], writes=[b_ps[bt]])
            yield
            pT = pT_ring.get()
            P.op("act", lambda e: e.activation(out=pT.ap[:, 0:nk], in_=pst[:, 0:nk], func=AF.Copy),
                 reads=[b_ps[bt]], writes=[pT.buf])
            d["pT"] = pT
            yield

        def stage_c(i0, n_it):
            per = 512 // w_out_cols
            for j0 in range(0, n_it, per):
                bo = bank()
                cnt = min(per, n_it - j0)
                for j in range(cnt):
                    d = st[i0 + j0 + j]
                    items[i0 + j0 + j]["pv"](bo, j, cnt, d["pT"].ap, d["pT"].buf)
                evac(bo, i0 + j0, cnt)

        def stage_a_group(idxs):
            k = len(idxs)
            blk16 = (small_i[0] % 4) * 16
            small_i[0] += 1
            mxs, nms, rss, ess = [small[:, blk16 + q * 4: blk16 + q * 4 + k] for q in range(4)]
            bmx = [b_small[blk16 + j] for j in range(k)]
            bnm = [b_small[blk16 + 4 + j] for j in range(k)]
            brs = [b_small[blk16 + 8 + j] for j in range(k)]
            bes = [b_small[blk16 + 12 + j] for j in range(k)]
            its = [items[i] for i in idxs]
            sink = its[0]["sink"]
            banks, ss = [], []
            for j, it in enumerate(its):
                b = bank()
                it["qk"](b)
                banks.append(b)
            for j, it in enumerate(its):
                nk = it["nk"]
                b = banks[j]
                s_ = s_ring.get()
                ss.append(s_)
                sa = s_.ap[:, 0:nk]
                if it["mask"] is not None:
                    P.op("dve", lambda e, sa=sa, b=b, nk=nk, it=it: e.scalar_tensor_tensor(
                        out=sa, in0=ps[b][:, 0:nk], scalar=float(it["scale"]), in1=it["mask"], op0=ALU.mult, op1=ALU.add),
                        reads=[b_ps[b], it["mask_buf"]], writes=[s_.buf])
                else:
                    P.op("dve", lambda e, sa=sa, b=b, nk=nk, it=it: e.tensor_scalar(
                        out=sa, in0=ps[b][:, 0:nk], scalar1=float(it["scale"]), scalar2=None, op0=ALU.mult),
                        reads=[b_ps[b]], writes=[s_.buf])
            for j, it in enumerate(its):
                sa = ss[j].ap[:, 0:it["nk"]]
                P.op("dve", lambda e, sa=sa, j=j: e.reduce_max(out=mxs[:, j:j + 1], in_=sa, axis=AX.X),
                     reads=[ss[j].buf], writes=[bmx[j]])
            if sink is not None:
                P.op("dve", lambda e: e.tensor_scalar(out=mxs, in0=mxs, scalar1=sink, scalar2=None, op0=ALU.max),
                     reads=bmx + [b_const], writes=bmx)
            P.op("dve", [lambda e: e.memset(rss, 0.0),
                         lambda e: e.tensor_scalar(out=nms, in0=mxs, scalar1=-1.0, scalar2=None, op0=ALU.mult)],
                 reads=bmx, writes=bnm + brs)
            for j, it in enumerate(its):
                sa = ss[j].ap[:, 0:it["nk"]]
                P.op("act", lambda e, sa=sa, j=j: e.activation(out=sa, in_=sa, func=AF.Exp, bias=nms[:, j:j + 1],
                                                               accum_out=rss[:, j:j + 1]),
                     reads=[ss[j].buf, bnm[j], brs[j]], writes=[ss[j].buf, brs[j]])
            if sink is not None:
                P.op("act", lambda e: e.activation(out=ess, in_=nms, func=AF.Exp, bias=sink), reads=bnm + [b_const], writes=bes)
            return (idxs, its, ss, rss, ess, brs, bes, sink)

        def stage_a2(state):
            idxs, its, ss, rss, ess, brs, bes, sink = state
            if sink is not None:
                P.op("dve", lambda e: e.tensor_tensor(out=rss, in0=rss, in1=ess, op=ALU.add), reads=brs + bes, writes=brs)
            P.op("dve", lambda e: e.reciprocal(out=rss, in_=rss), reads=brs, writes=brs)
            for j, it in enumerate(its):
                nk = it["nk"]
                sa = ss[j].ap[:, 0:nk]
                pb = p_ring.get()
                P.op("dve", lambda e, sa=sa, pb=pb, nk=nk, j=j: e.tensor_scalar(
                    out=pb.ap[:, 0:nk], in0=sa, scalar1=rss[:, j:j + 1], scalar2=None, op0=ALU.mult),
                    reads=[ss[j].buf, brs[j]], writes=[pb.buf])
                st[idxs[j]]["pb"] = pb

        n = len(items)
        groups = [(i, min(K, n - i)) for i in range(0, n, K)]
        G = len(groups)
        a_state = {}
        for step in range(G + 3):
            if step < G:
                i0, c = groups[step]
                a_state[step] = stage_a_group(list(range(i0, i0 + c)))
            if 0 <= step - 1 < G:
                stage_a2(a_state.pop(step - 1))
            if 0 <= step - 2 < G:
                i0, c = groups[step - 2]
                run_rr(stage_b(items[i], st[i]) for i in range(i0, i0 + c))
            if 0 <= step - 3 < G:
                i0, c = groups[step - 3]
                stage_c(i0, c)

    def mixer(l, final):
        win = g_w_in()[l]
        wkv = g_w_kv()[l]
        memT_v = fm(g_memT())
        reg = [at_off]

        def lcarve(n):
            o = (reg[0] + 31) // 32 * 32
            reg[0] = o + n
            assert reg[0] <= at_off + AT_BYTES, (reg[0] - at_off, AT_BYTES)
            return o
        bank_ring[0] = list(range(8))
        memTb = view(lcarve(16 * 256 * 2), [16, 256], BF16)
        b_memT = P.fresh()
        mkT = view(lcarve(8 * 256 * 2), [8, 256], BF16)
        b_mk = P.fresh()
        mvt = view(lcarve(2 * 1024 * 2), [2, 1024], BF16)
        b_mv = P.fresh()
        m0_end = reg[0]
        P.dma("pool", [(memTb[:, 0:8, :], memT_v[:, 0:8, :]), (memTb[:, 8:16, :], memT_v[:, 8:16, :])], "xld",
              writes=[b_memT])
        for pr in range(4):
            sl, vs = load_chunks([wkv[2 * pr], wkv[2 * pr + 1]], 16)
            for g in range(2):
                b = bank()
                mm(b, lambda kc: vs[g][:, kc, :], lambda kc: memTb[:, kc, :], 16, [b_ws[sl], b_memT], ncols=256)
                P.op("act", lambda e, b=b, ch=2 * pr + g: e.activation(out=mkT[:, ch, :], in_=ps[b][:, 0:256], func=AF.Copy),
                     reads=[b_ps[b]], writes=[b_mk])
                bg_step(2)
        for ct in range(4):
            sl = wslot()
            wv = ws_view(sl, [16, 256])
            load_w(sl, [(wv[:, :, 0:128], wkv[8 + 2 * ct]), (wv[:, :, 128:256], wkv[9 + 2 * ct])])
            for mb in range(2):
                b = bank()
                mm(b, lambda kc: memTb[:, kc, mb * 128:(mb + 1) * 128], lambda kc: wv[:, kc, :], 16,
                   [b_ws[sl], b_memT], ncols=256)
                P.op("act", lambda e, b=b, mb=mb, ct=ct: e.activation(out=mvt[:, mb, ct * 256:(ct + 1) * 256],
                                                                      in_=ps[b][:, 0:256], func=AF.Copy),
                     reads=[b_ps[b]], writes=[b_mv])
                bg_step(2)
        drain_bg()
        rC = view(lcarve(8192), [T], F32)
        rS = view(lcarve(8192), [T], F32)
        b_rt = P.fresh()
        P.dma("sp", [(rC[0:32, :], g_ropeC()), (rS[0:32, :], g_ropeS())], "ld_rt", writes=[b_rt])
        for i in range(8):
            sl, (va, vg) = load_chunks([win[i], win[8 + i]], 16)
            for tt in range(4):
                tk = slice(tt * 512, (tt + 1) * 512)
                ba, bg = bank(), bank()
                mm(ba, lambda kc: va[:, kc, :], lambda kc: XT[:, kc, tk], 16, [b_ws[sl], b_XT[tt]])
                mm(bg, lambda kc: vg[:, kc, :], lambda kc: XT[:, kc, tk], 16, [b_ws[sl], b_XT[tt]])
                sg = ring.get()
                P.op("act", lambda e, sg=sg, bg=bg: e.activation(out=sg.ap, in_=ps[bg][:, :], func=AF.Sigmoid),
                     reads=[b_ps[bg]], writes=[sg.buf])
                hb = bring.get()
                P.op("dve", lambda e, sg=sg, ba=ba, hb=hb: e.tensor_tensor(out=hb.ap, in0=sg.ap, in1=ps[ba][:, :], op=ALU.mult),
                     reads=[sg.buf, b_ps[ba]], writes=[hb.buf])
                P.dma("sp", [(hc_v[:, i, tk], hb.ap)], hb.sem, reads=[hb.buf], writes=[b_hc[i][tt]])
        qk_pending = []
        qk_chunks = [(16 + 2 * hp, 17 + 2 * hp, 2 * hp) for hp in range(4)] + [(24, 25, 8)]
        for (c0, c1, row0) in qk_chunks:
            sl, vs = load_chunks([win[c0], win[c1]], 16)
            for tt in range(4):
                tk = slice(tt * 512, (tt + 1) * 512)
                for g in range(2):
                    b = bank()
                    mm(b, lambda kc: vs[g][:, kc, :], lambda kc: XT[:, kc, tk], 16, [b_ws[sl], b_XT[tt]])
                    r = ring.get()
                    P.op("act", lambda e, r=r, b=b: e.activation(out=r.ap, in_=ps[b][:, :], func=AF.Copy),
                         reads=[b_ps[b]], writes=[r.buf])
                    ro = ring.get()
                    P.dma("sp", [(ro.ap[0:16, :], r.ap[16:32, :]), (ro.ap[16:32, :], r.ap[0:16, :])], ro.sem,
                          reads=[r.buf], writes=[ro.buf])
                    if qk_pending:
                        qk_pending.pop(0)()
                    P.op("dve", lambda e, ro=ro, tk=tk: e.tensor_tensor(out=ro.ap[0:32, :], in0=ro.ap[0:32, :], in1=rS[0:32, tk], op=ALU.mult),
                         reads=[ro.buf, b_rt], writes=[ro.buf])
                    P.op("dve", lambda e, r=r, tk=tk: e.tensor_tensor(out=r.ap[0:32, :], in0=r.ap[0:32, :], in1=rC[0:32, tk], op=ALU.mult),
                         reads=[r.buf, b_rt], writes=[r.buf])
                    P.op("dve", lambda e, r=r, ro=ro: e.tensor_tensor(out=r.ap[0:32, :], in0=r.ap[0:32, :], in1=ro.ap[0:32, :], op=ALU.add),
                         reads=[r.buf, ro.buf], writes=[r.buf])
                    qb_ = bring.get()
                    P.op("act", lambda e, r=r, qb_=qb_: e.activation(out=qb_.ap, in_=r.ap, func=AF.Copy),
                         reads=[r.buf], writes=[qb_.buf])
                    qk_pending.append(lambda qb_=qb_, row=row0 + g, tk=tk, tt=tt: P.dma(
                        "sp", [(qk_v[:, row, tk], qb_.ap)], qb_.sem, reads=[qb_.buf], writes=[b_qk[row][tt]]))
        while qk_pending:
            qk_pending.pop(0)()
        sl = wslot()
        wv = ws_view(sl, [16, 256])
        load_w(sl, [(wv[:, :, 0:128], win[26]), (wv[:, :, 128:256], win[27])])
        for tb in range(16):
            b = bank()
            mm(b, lambda kc: XT[:, kc, tb * 128:(tb + 1) * 128], lambda kc: wv[:, kc, :], 16, [b_ws[sl], b_XT[tb // 4]],
               ncols=256)
            hb = bring.get()
            P.op("act", lambda e, hb=hb, b=b: e.activation(out=hb.ap[:, 0:256], in_=ps[b][:, 0:256], func=AF.Copy),
                 reads=[b_ps[b]], writes=[hb.buf])
            P.dma("sp", [(v_tok[tb * 128:(tb + 1) * 128, :], hb.ap[:, 0:256])], hb.sem, reads=[hb.buf], writes=[b_vtok[tb]])
        for hp in range(4):
            sl, vs = load_chunks([win[28 + 2 * hp], win[29 + 2 * hp]], 16)
            for tt in range(4):
                tk = slice(tt * 512, (tt + 1) * 512)
                for g in range(2):
                    b = bank()
                    mm(b, lambda kc: vs[g][:, kc, :], lambda kc: XT[:, kc, tk], 16, [b_ws[sl], b_XT[tt]])
                    hb = bring.get()
                    P.op("act", lambda e, hb=hb, b=b: e.activation(out=hb.ap, in_=ps[b][:, :], func=AF.Copy),
                         reads=[b_ps[b]], writes=[hb.buf])
                    P.dma("sp", [(qm_v[:, 2 * hp + g, tk], hb.ap)], hb.sem, reads=[hb.buf], writes=[b_qm[2 * hp + g][tt]])

        KI = 4

        def mk_rings():
            s_r = Ring([Tile(view(lcarve(1536), [384], F32), P.fresh(), None) for _ in range(2 * KI)])
            p_r = Ring([Tile(view(lcarve(768), [384], BF16), P.fresh(), None) for _ in range(2 * KI)])
            t_r = Ring([Tile(view(lcarve(768), [384], BF16), P.fresh(), None) for _ in range(2 * KI)])
            return s_r, p_r, t_r
        reg[0] = m0_end
        kTb = view(lcarve(4096), [T], BF16); b_kT = P.fresh()
        vt = view(lcarve(4096), [16, 128], BF16); b_vt = P.fresh()
        qTb = [view(lcarve(4096), [T], BF16) for _ in range(2)]; b_qT = [P.fresh(), P.fresh()]
        ast = [view(lcarve(4096), [T], BF16) for _ in range(2)]; b_ast = [P.fresh(), P.fresh()]
        s_r, p_r, t_r = mk_rings()

        def load_q(h):
            P.dma("sp", [(qTb[h % 2], qk_v[:, h, :])], f"ld_q{h % 2}", reads=b_qk[h], writes=[b_qT[h % 2]])
        load_q(0)
        for h in range(8):
            kvh = h // 4
            if h % 4 == 0:
                P.dma("sp", [(kTb, qk_v[:, 8 + kvh, :])], "ld_k", reads=b_qk[8 + kvh], writes=[b_kT])
                P.dma("sp", [(vt, v_tok.rearrange("(tb p) d -> p tb d", p=128)[:, :, kvh * 128:(kvh + 1) * 128])], "ld_vt",
                      reads=b_vtok, writes=[b_vt])
            qT = qTb[h % 2]
            if h + 1 < 8:
                load_q(h + 1)
            a_st = ast[h % 2]
            sink_ap = sink_sb[:, l * 8 + h:l * 8 + h + 1]
            items = []
            for blk in range(16):
                lo, hi = max(0, blk - 1), min(15, blk + 1)
                nk = (hi - lo + 1) * 128
                m0 = (lo - (blk - 1)) * 128

                def qk(b, blk=blk, lo=lo, hi=hi, nk=nk, qT=qT, h=h):
                    P.op("pe", [lambda e: e.matmul(ps[b][:, 0:nk], lhsT=qT[:, blk * 128:(blk + 1) * 128],
                                                   rhs=kTb[:, lo * 128:(hi + 1) * 128], start=True, stop=True)],
                         reads=[b_qT[h % 2], b_kT], writes=[b_ps[b]])

                def pv(bo, j, cnt, pT, b_pT, blk=blk, lo=lo, nk=nk):
                    fns = []
                    nkb = nk // 128
                    for kb in range(nkb):
                        fns.append(lambda e, kb=kb: e.matmul(ps[bo][:, j * 128:(j + 1) * 128], lhsT=vt[:, lo + kb, :],
                                                             rhs=pT[:, kb * 128:(kb + 1) * 128], start=(kb == 0),
                                                             stop=(kb == nkb - 1)))
                    P.op("pe", fns, reads=[b_vt, b_pT], writes=[b_ps[bo]])
                items.append(dict(qk=qk, nk=nk, mask=mask_sb[:, m0:m0 + nk], mask_buf=b_const, scale=128 ** -0.5,
                                  sink=sink_ap, pv=pv))

            def evac3(bo, i0, cnt, a_st=a_st, h=h):
                P.op("act", lambda e: e.activation(out=a_st[:, i0 * 128:(i0 + cnt) * 128], in_=ps[bo][:, 0:cnt * 128], func=AF.Copy),
                     reads=[b_ps[bo]], writes=[b_ast[h % 2]])
            attn_pipeline(items, s_r, p_r, t_r, KI, 128, evac3)
            P.dma("sp", [(attn_v[:, h, :], a_st)], f"st_ast{h % 2}", reads=[b_ast[h % 2]], writes=[b_attn[h]])

        reg[0] = m0_end
        qmb = [view(lcarve(8192), [2, T], BF16) for _ in range(2)]; b_qmb = [P.fresh(), P.fresh()]
        mst = [view(lcarve(8192), [2, T], BF16) for _ in range(2)]; b_mst = [P.fresh(), P.fresh()]
        s_r, p_r, t_r = mk_rings()
        for mh in range(4):
            qb = qmb[mh % 2]
            ms = mst[mh % 2]
            P.dma("sp", [(qb, qm_v[:, 2 * mh:2 * mh + 2, :])], f"ld_qm{mh % 2}", reads=b_qm[2 * mh] + b_qm[2 * mh + 1],
                  writes=[b_qmb[mh % 2]])
            items = []
            for blk in range(16):
                def qk(b, blk=blk, qb=qb, mh=mh):
                    P.op("pe", [lambda e, dc=dc: e.matmul(ps[b][:, 0:256], lhsT=qb[:, dc, blk * 128:(blk + 1) * 128],
                                                          rhs=mkT[:, 2 * mh + dc, :], start=(dc == 0), stop=(dc == 1))
                                for dc in range(2)],
                         reads=[b_qmb[mh % 2], b_mk], writes=[b_ps[b]])

                def pv(bo, j, cnt, pT, b_pT, blk=blk, mh=mh):
                    fns = []
                    for dc in range(2):
                        for mb in range(2):
                            c0 = dc * 256 + j * 128
                            fns.append(lambda e, dc=dc, mb=mb, c0=c0: e.matmul(
                                ps[bo][:, c0:c0 + 128], lhsT=mvt[:, mb, mh * 256 + dc * 128:mh * 256 + (dc + 1) * 128],
                                rhs=pT[:, mb * 128:(mb + 1) * 128], start=(mb == 0), stop=(mb == 1)))
                    P.op("pe", fns, reads=[b_mv, b_pT], writes=[b_ps[bo]])
                items.append(dict(qk=qk, nk=256, mask=None, mask_buf=None, scale=256 ** -0.5, sink=None, pv=pv))

            def evac4(bo, i0, cnt, ms=ms, mh=mh):
                assert cnt == 2
                P.op("act", lambda e: e.activation(out=ms[:, :, i0 * 128:(i0 + 2) * 128],
                                                   in_=ps[bo][:, 0:512].rearrange("p (a b) -> p a b", a=2), func=AF.Copy),
                     reads=[b_ps[bo]], writes=[b_mst[mh % 2]])
            attn_pipeline(items, s_r, p_r, t_r, KI, 256, evac4)
            P.dma("sp", [(memo_v[:, 2 * mh:2 * mh + 2, :], ms)], f"st_mst{mh % 2}", reads=[b_mst[mh % 2]], writes=[b_memo[mh]])

        reg[0] = at_off
        acc = view(lcarve(8 * T * 4), [8, T], F32)
        b_acc = [P.fresh() for _ in range(4)]
        diag = [view(lcarve(CW * 128 * 2), [CW, 128], BF16) for _ in range(2)]; b_dg = [P.fresh(), P.fresh()]
        hcp = [view(lcarve(2080 * 2), [2080], BF16) for _ in range(2)]; b_hcp = [P.fresh(), P.fresh()]
        for i in range(8):
            hb_, dg = hcp[i % 2], diag[i % 2]
            P.op("dve", [lambda e, hb_=hb_: e.memset(hb_[:, 0:16], 0.0), lambda e, hb_=hb_: e.memset(hb_[:, 2064:2080], 0.0)],
                 writes=[b_hcp[i % 2]])
            P.dma("sp", [(hb_[:, 16:2064], hc_v[:, i, :])], f"ld_hcp{i % 2}", reads=b_hc[i], writes=[b_hcp[i % 2]])
            P.op("dve", [lambda e, j=j, dg=dg, i=i: e.tensor_scalar(out=dg[:, j, :], in0=identb, scalar1=cvvec(l, i, j),
                                                                     scalar2=None, op0=ALU.mult) for j in range(CW)],
                 reads=[b_const], writes=[b_dg[i % 2]])
            for tt in range(4):
                b = bank()
                fns = [lambda e, j=j, dg=dg, hb_=hb_, tt=tt, b=b: e.matmul(
                    ps[b][:, :], lhsT=dg[:, j, :], rhs=hb_[:, tt * 512 + j + 1:tt * 512 + j + 513], start=(j == 0),
                    stop=(j == CW - 1)) for j in range(CW)]
                P.op("pe", fns, reads=[b_dg[i % 2], b_hcp[i % 2]], writes=[b_ps[b]])
                P.op("act", lambda e, b=b, i=i, tt=tt: e.activation(out=acc[:, i, tt * 512:(tt + 1) * 512], in_=ps[b][:, :],
                                                                    func=AF.Identity, bias=cvvec(l, i, CW)),
                     reads=[b_ps[b], b_const], writes=[b_acc[tt]])
        for tt in range(4):
            tk = slice(tt * 512, (tt + 1) * 512)
            s1, s2 = bank(), bank()
            for i in range(8):
                zb = bring.get()
                P.op("act", lambda e, zb=zb, i=i, tk=tk: e.activation(out=zb.ap, in_=acc[:, i, tk], func=AF.Copy),
                     reads=[b_acc[tt]], writes=[zb.buf])
                zq = bring.get()
                P.op("act", lambda e, zq=zq, i=i, tk=tk: e.activation(out=zq.ap, in_=acc[:, i, tk], func=AF.Square),
                     reads=[b_acc[tt]], writes=[zq.buf])
                P.op("pe", [lambda e, zb=zb, i=i, s1=s1: e.matmul(ps[s1][:, :], lhsT=onesb, rhs=zb.ap, start=(i == 0), stop=(i == 7)),
                            lambda e, zq=zq, i=i, s2=s2: e.matmul(ps[s2][:, :], lhsT=onesb, rhs=zq.ap, start=(i == 0), stop=(i == 7))],
                     reads=[zb.buf, zq.buf, b_const], writes=[b_ps[s1], b_ps[s2]])
            stats_to_AB(s1, s2, CONV_CH, 0)
            A_t, B_t, b_A, b_B = A_ts[0], B_ts[0], b_As[0], b_Bs[0]
            for i in range(8):
                y = ring.get()
                P.op("dve", lambda e, y=y, i=i, tk=tk: e.tensor_tensor(out=y.ap, in0=acc[:, i, tk], in1=A_t, op=ALU.mult),
                     reads=[b_acc[tt], b_A], writes=[y.buf])
                P.op("dve", lambda e, y=y: e.tensor_tensor(out=y.ap, in0=y.ap, in1=B_t, op=ALU.add),
                     reads=[y.buf, b_B], writes=[y.buf])
                cb = bring.get()
                P.op("act", lambda e, y=y, cb=cb, i=i: e.activation(out=cb.ap, in_=y.ap, func=AF.Silu,
                                                                    scale=cvvec(l, i, CW + 1), bias=cvvec(l, i, CW + 2)),
                     reads=[y.buf, b_const], writes=[cb.buf])
                P.dma("sp", [(conv_v[:, i, tk], cb.ap)], cb.sem, reads=[cb.buf], writes=[b_conv[i][tt]])

        wbr = [g_w_co()[l], g_w_wo()[l], g_w_mo()[l]]
        wo = g_w_out()[l]
        for th in range(2):
            t0 = th * 1024
            reg[0] = at_off
            br = [view(lcarve(8 * 1024 * 2), [8, 1024], BF16) for _ in range(3)]
            b_br = [P.fresh() for _ in range(3)]
            mg = view(lcarve(16 * 1024 * 2), [16, 1024], BF16)
            b_mg = [P.fresh(), P.fresh()]
            bank_ring[0] = list(range(8))
            rd_conv = [b_conv[i][2 * th + j] for i in range(8) for j in range(2)]
            P.dma("sp", [(br[0], conv_v[:, :, t0:t0 + 1024])], "ld_br0", reads=rd_conv, writes=[b_br[0]])
            P.dma("sp", [(br[1], attn_v[:, :, t0:t0 + 1024])], "ld_br1", reads=b_attn, writes=[b_br[1]])
            P.dma("sp", [(br[2], memo_v[:, :, t0:t0 + 1024])], "ld_br2", reads=b_memo, writes=[b_br[2]])
            for n in range(16):
                slA, (g0v, g1v) = load_chunks([win[36 + n], win[52 + n]], 16)
                slB = wslot()
                g2v = ws_view(slB, [16, 128], 0)
                bw = [ws_view(slB, [8, 128], 4096 + j * 2048) for j in range(3)]
                load_w(slB, [(g2v, win[68 + n])] + [(bw[j], wbr[j][n]) for j in range(3)])
                gv = [g0v, g1v, g2v]
                gsl = [slA, slA, slB]
                for tt in range(2):
                    gt = 2 * th + tt
                    tk = slice(t0 + tt * 512, t0 + (tt + 1) * 512)
                    lk = slice(tt * 512, (tt + 1) * 512)
                    gb = [bank() for _ in range(3)]
                    for j in range(3):
                        mm(gb[j], lambda kc: gv[j][:, kc, :], lambda kc: XT[:, kc, tk], 16, [b_ws[gsl[j]], b_XT[gt]])
                    yb = [bank() for _ in range(3)]
                    for j in range(3):
                        mm(yb[j], lambda kc: bw[j][:, kc, :], lambda kc: br[j][:, kc, lk], 8, [b_ws[slB], b_br[j]])
                    gts = [ring.get() for _ in range(3)]
                    for j in range(3):
                        P.op("act", lambda e, j=j, gts=gts, gb=gb: e.activation(out=gts[j].ap, in_=ps[gb[j]][:, :], func=AF.Sigmoid),
                             reads=[b_ps[gb[j]]], writes=[gts[j].buf])
                    m_, t_ = ring.get(), ring.get()
                    P.op("dve", lambda e, m_=m_, gts=gts, yb=yb: e.tensor_tensor(out=m_.ap, in0=gts[0].ap, in1=ps[yb[0]][:, :], op=ALU.mult),
                         reads=[gts[0].buf, b_ps[yb[0]]], writes=[m_.buf])
                    P.op("dve", lambda e, t_=t_, gts=gts, yb=yb: e.tensor_tensor(out=t_.ap, in0=gts[1].ap, in1=ps[yb[1]][:, :], op=ALU.mult),
                         reads=[gts[1].buf, b_ps[yb[1]]], writes=[t_.buf])
                    P.op("dve", lambda e, m_=m_, t_=t_: e.tensor_tensor(out=m_.ap, in0=m_.ap, in1=t_.ap, op=ALU.add),
                         reads=[m_.buf, t_.buf], writes=[m_.buf])
                    P.op("dve", lambda e, t_=t_, gts=gts, yb=yb: e.tensor_tensor(out=t_.ap, in0=gts[2].ap, in1=ps[yb[2]][:, :], op=ALU.mult),
                         reads=[gts[2].buf, b_ps[yb[2]]], writes=[t_.buf])
                    P.op("dve", lambda e, m_=m_, t_=t_, n=n, lk=lk: e.tensor_tensor(out=mg[:, n, lk], in0=m_.ap, in1=t_.ap, op=ALU.add),
                         reads=[m_.buf, t_.buf], writes=[b_mg[tt]])
                    bg_step()
            drain_bg()
            bank_ring[0] = [0, 1, 2, 3]
            for np_ in range(8):
                sl, vs = load_chunks([wo[2 * np_], wo[2 * np_ + 1]], 16)
                for g in range(2):
                    n = 2 * np_ + g
                    for tt in range(2):
                        b = bank()
                        lk = slice(tt * 512, (tt + 1) * 512)
                        mm(b, lambda kc: vs[g][:, kc, :], lambda kc: mg[:, kc, lk], 16, [b_ws[sl], b_mg[tt]])
                        resid_epilogue(b, n, t0 + tt * 512, tt, 1.0, n == 0, n == 15)
            normalize(l, 1, t0, 2, final)

    phases = []
    for l in range(depth):
        phases.append(("ffn", l, 0))
        phases.append(("mix", l))
        phases.append(("ffn", l, 1))
    if stop is not None:
        phases = phases[:stop]
    for i, ph in enumerate(phases):
        final = (i == len(phases) - 1)
        if ph[0] == "ffn":
            ffn(ph[1], ph[2], final)
        else:
            mixer(ph[1], final)

    drain_bg()
    fin = Buf()
    fin.w = b_out.w
    allb = [b_out] + [b for row in b_h32a for b in row] + b_attn + b_memo + [b for row in b_conv for b in row]
    P.wait_all("sp", allb)
    P.emit()
    print("n_ins", P.n_ins, "n_wait", P.n_wait, "arena", off[0])
    nc._used_inputs = set(_ins)
    return nc


def _c(a):
    return np.ascontiguousarray(a, dtype=np.float32)


def host_consts():
    pos = np.arange(T, dtype=np.float32)
    inv_freq = (np.float32(500000.0) ** (-np.arange(0, 32, 2, dtype=np.float32) / np.float32(32))).astype(np.float32)
    ang = (pos[:, None] * inv_freq[None, :]).astype(np.float32)
    cos, sin = np.cos(ang).astype(np.float32).T, np.sin(ang).astype(np.float32).T
    ropeC = np.concatenate([cos, cos], 0)
    ropeS = np.concatenate([-sin, sin], 0)
    i = np.arange(128)[:, None]
    c = np.arange(384)[None, :]
    mask = np.where((c >= i) & (c <= i + 256), 0.0, NEG).astype(np.float32)
    return {"ropeC": _c(ropeC), "ropeS": _c(ropeS), "mask": _c(mask), "ident": np.eye(128, dtype=np.float32)}


def host_weights(inp):
    o = {}
    for i in (1, 2):
        wu = np.asarray(inp[f"ffn{i}_w_up"]).reshape(L, 16, 128, 2, FC, 128)
        o[f"ffn{i}_w_up"] = _c(wu.transpose(0, 4, 2, 1, 3, 5)).reshape(L, FC, 128, 16, 256)
        wd = np.asarray(inp[f"ffn{i}_w_down"]).reshape(L, FC, 128, 16, 128)
        o[f"ffn{i}_w_down"] = _c(wd.transpose(0, 3, 2, 1, 4))
    wi = np.asarray(inp["w_in"]).reshape(L, 16, 128, NCH_IN, 128)
    o["w_in"] = _c(wi.transpose(0, 3, 2, 1, 4))
    for nm in ("conv_w_out", "win_w_o", "mem_w_o"):
        w = np.asarray(inp[nm]).reshape(L, 8, 128, 16, 128)
        o[nm] = _c(w.transpose(0, 3, 2, 1, 4))
    for nm in ("mem_w_kv", "w_out"):
        w = np.asarray(inp[nm]).reshape(L, 16, 128, 16, 128)
        o[nm] = _c(w.transpose(0, 3, 2, 1, 4))
    lnp = np.zeros((L, 3, 2, 16, 128), np.float32)
    for k in range(3):
        lnp[:, k, 0] = np.asarray(inp[f"ln{k + 1}_g"]).reshape(L, 16, 128)
        lnp[:, k, 1] = np.asarray(inp[f"ln{k + 1}_b"]).reshape(L, 16, 128)
    o["lnp"] = _c(lnp.transpose(4, 0, 1, 2, 3)).reshape(128, -1)
    cvp = np.zeros((L, 8, CW + 3, 128), np.float32)
    cvp[:, :, :CW] = np.asarray(inp["conv_dw_w"]).reshape(L, CW, 8, 128).transpose(0, 2, 1, 3)
    cvp[:, :, CW] = np.asarray(inp["conv_dw_b"]).reshape(L, 8, 128)
    cvp[:, :, CW + 1] = np.asarray(inp["conv_ln_g"]).reshape(L, 8, 128)
    cvp[:, :, CW + 2] = np.asarray(inp["conv_ln_b"]).reshape(L, 8, 128)
    o["cvp"] = _c(cvp.transpose(3, 0, 1, 2)).reshape(128, -1)
    o["sinkb"] = _c(np.broadcast_to(np.asarray(inp["win_sink"]).reshape(1, L * 8), (128, L * 8)))
    o.update(host_consts())
    return o


_NC_CACHE = {}


def kernel(**inputs):
    x = np.asarray(inputs["x"], dtype=np.float32)
    mem = np.asarray(inputs["mem"], dtype=np.float32)
    nb = x.shape[0]
    shared = host_weights(inputs)
    in_maps = []
    for b in range(nb):
        m = dict(shared)
        m["xT"] = _c(x[b].T)
        m["memT"] = _c(mem[b].T)
        in_maps.append(m)
    if "nc" not in _NC_CACHE:
        _NC_CACHE["nc"] = build()
    nc = _NC_CACHE["nc"]
    in_maps = [{k: v for k, v in m.items() if k in nc._used_inputs} for m in in_maps]
    res = run_bass_kernel_spmd(nc, in_maps, core_ids=list(range(nb)))
    out = np.stack([np.ascontiguousarray(r["outT"].T) for r in res.results], 0)
    return out.astype(np.float32)
```

```python
import contextlib
import numpy as np
import concourse.bass as bass
import concourse.mybir as mybir
from concourse.bass_utils import run_bass_kernel_spmd

F32 = mybir.dt.float32
BF16 = mybir.dt.bfloat16
U8 = mybir.dt.uint8
AF = mybir.ActivationFunctionType
ALU = mybir.AluOpType
AX = mybir.AxisListType

D = 2048
T = 2048
L = 2
NMEM = 256
DFF = 5632
FC = DFF // 128
CONV_CH = 1024
CW = 31
IN_WIDTH = 10752
NCH_IN = IN_WIDTH // 128
ALPHA = float((2 * L) ** 0.25)
EPS = 1e-5
NEG = -1e30
ENGS = ["pe", "act", "dve", "pool", "sp"]


class Buf:
    __slots__ = ("name", "w", "r")

    def __init__(self, name=""):
        self.name = name
        self.w = None
        self.r = {}


class Prog:
    def __init__(self, nc):
        self.nc = nc
        self.q = {e: [] for e in ENGS}
        self.cnt = {}
        self.seen = {e: {} for e in ENGS}
        self.semnames = []
        self.n_wait = 0
        self.n_ins = 0
        for e in ENGS:
            self._sem("eng_" + e)

    def _sem(self, key):
        if key not in self.cnt:
            self.cnt[key] = 0
            self.semnames.append(key)
        return key

    def fresh(self, name=""):
        b = Buf(name)
        b.r = {k: v for k, v in self.cnt.items() if v > 0}
        return b

    def _deps(self, eng, reads, writes):
        deps = {}

        def add(k, v):
            if deps.get(k, 0) < v:
                deps[k] = v
        for b in reads:
            if b.w is not None:
                add(*b.w)
        for b in writes:
            if b.w is not None:
                add(*b.w)
            for k, v in b.r.items():
                add(k, v)
        waits = []
        own = "eng_" + eng
        seen = self.seen[eng]
        for k, v in deps.items():
            if eng == "pe" and k == own:
                continue
            if seen.get(k, 0) < v:
                seen[k] = v
                waits.append((k, v))
        return waits

    def _commit(self, ev, reads, writes):
        k, v = ev
        for b in reads:
            if b.r.get(k, 0) < v:
                b.r[k] = v
        for b in writes:
            b.w = ev
            b.r = {}

    def op(self, eng, fns, reads=(), writes=()):
        if callable(fns):
            fns = [fns]
        waits = self._deps(eng, reads, writes)
        key = "eng_" + eng
        self.cnt[key] += 1
        ev = (key, self.cnt[key])
        self.q[eng].append(("op", waits, fns, key))
        self._commit(ev, reads, writes)
        self.n_wait += len(waits)
        self.n_ins += len(fns)
        return ev

    def dma(self, eng, pairs, semkey, reads=(), writes=(), **kw):
        if isinstance(pairs, tuple):
            pairs = [pairs]
        self._sem(semkey)
        waits = self._deps(eng, reads, writes)
        self.cnt[semkey] += 16 * len(pairs)
        ev = (semkey, self.cnt[semkey])
        self.q[eng].append(("dma", waits, (pairs, kw), semkey))
        self._commit(ev, reads, writes)
        self.n_wait += len(waits)
        self.n_ins += len(pairs)
        return ev

    def wait_all(self, eng, bufs):
        waits = self._deps(eng, bufs, ())
        self.q[eng].append(("wait", waits, None, None))

    def emit(self):
        nc = self.nc
        with contextlib.ExitStack() as st:
            sems = {}
            for k in self.semnames:
                sems[k] = st.enter_context(nc.semaphore(k))
            block = st.enter_context(nc.Block())

            def run(e, items):
                for kind, waits, payload, key in items:
                    for (k, v) in waits:
                        e.wait_ge(sems[k], v)
                    if kind == "op":
                        ins = None
                        for f in payload:
                            ins = f(e)
                        ins.then_inc(sems[key], 1)
                    elif kind == "dma":
                        pairs, kw = payload
                        for (o, i) in pairs:
                            e.dma_start(out=o, in_=i, **kw).then_inc(sems[key], 16)

            @block.tensor
            def _(e):
                run(e, self.q["pe"])

            @block.scalar
            def _(e):
                run(e, self.q["act"])

            @block.vector
            def _(e):
                run(e, self.q["dve"])

            @block.gpsimd
            def _(e):
                run(e, self.q["pool"])

            @block.sync
            def _(e):
                run(e, self.q["sp"])


class Tile:
    __slots__ = ("ap", "buf", "sem", "pinned")

    def __init__(self, ap, buf, sem):
        self.ap = ap
        self.buf = buf
        self.sem = sem
        self.pinned = False


class Ring:
    def __init__(self, tiles):
        self.tiles = tiles
        self.i = 0

    def get(self, pin=False):
        for _ in range(2 * len(self.tiles)):
            t = self.tiles[self.i % len(self.tiles)]
            self.i += 1
            if not t.pinned:
                t.pinned = pin
                return t
        raise RuntimeError("ring exhausted (all tiles pinned)")


def build(depth=L, stop=None, dbg=False):
    nc = bass.Bass("TRN2", target_bir_lowering=False)
    P = Prog(nc)

    def din(name, shape, dt=F32):
        return nc.dram_tensor(name, list(shape), dt, kind="ExternalInput").ap()

    def dscr(name, shape, dt=F32, out=False):
        if out:
            return nc.dram_tensor(name, list(shape), dt, kind="ExternalOutput").ap()
        return nc.dram_tensor(name, list(shape), dt).ap()

    _ins = {}

    def lazy(name, shape):
        def get():
            if name not in _ins:
                _ins[name] = din(name, shape)
            return _ins[name]
        return get
    xT = din("xT", [D, T]); _ins["xT"] = xT
    g_memT = lazy("memT", [D, NMEM])
    g_w_up = [lazy(f"ffn{i}_w_up", [L, FC, 128, 16, 256]) for i in (1, 2)]
    g_w_dn = [lazy(f"ffn{i}_w_down", [L, 16, 128, FC, 128]) for i in (1, 2)]
    g_w_in = lazy("w_in", [L, NCH_IN, 128, 16, 128])
    g_w_co = lazy("conv_w_out", [L, 16, 128, 8, 128])
    g_w_wo = lazy("win_w_o", [L, 16, 128, 8, 128])
    g_w_mo = lazy("mem_w_o", [L, 16, 128, 8, 128])
    g_w_kv = lazy("mem_w_kv", [L, 16, 128, 16, 128])
    g_w_out = lazy("w_out", [L, 16, 128, 16, 128])
    lnp = din("lnp", [128, L * 3 * 2 * 16]); _ins["lnp"] = lnp
    cvp = din("cvp", [128, L * 8 * (CW + 3)]); _ins["cvp"] = cvp
    sinkb = din("sinkb", [128, L * 8]); _ins["sinkb"] = sinkb
    g_ropeC = lazy("ropeC", [32, T])
    g_ropeS = lazy("ropeS", [32, T])
    g_mask = lazy("mask", [128, 384])
    identd = din("ident", [128, 128]); _ins["ident"] = identd

    last_dbg = dbg
    outT = dscr("outT", [D, T], out=True)
    h32a = dscr("h32a", [D, T], out=last_dbg)
    zd = dscr("zd", [D, T])
    hc_d = dscr("hc_d", [CONV_CH, T], BF16)
    qkb_d = dscr("qkb_d", [1280, T], BF16)
    v_tok = dscr("v_tok", [T, 256], BF16)
    qm_d = dscr("qm_d", [1024, T], BF16)
    conv_d = dscr("conv_d", [1024, T], BF16, out=last_dbg)
    attn_d = dscr("attn_d", [1024, T], BF16, out=last_dbg)
    memo_d = dscr("memo_d", [1024, T], BF16, out=last_dbg)

    fm = lambda ap: ap.rearrange("(c p) t -> p c t", p=128)
    h32a_v, zd_v, outT_v, xT_v = fm(h32a), fm(zd), fm(outT), fm(xT)
    hc_v, qk_v, qm_v, conv_v, attn_v, memo_v = fm(hc_d), fm(qkb_d), fm(qm_d), fm(conv_d), fm(attn_d), fm(memo_d)

    def grid(n, m, nm):
        return [[Buf(f"{nm}{i}_{j}") for j in range(m)] for i in range(n)]
    b_h32a = grid(16, 4, "h32a")
    b_zd = grid(16, 4, "zd")
    b_hc = grid(8, 4, "hc")
    b_qk = grid(10, 4, "qk")
    b_vtok = [Buf() for _ in range(16)]
    b_qm = grid(8, 4, "qm")
    b_conv = grid(8, 4, "convd")
    b_attn = [Buf() for _ in range(8)]
    b_memo = [Buf() for _ in range(4)]
    b_out = Buf()

    ARENA = 212800
    arena = nc.alloc_sbuf_tensor("arena", [128, ARENA], U8)
    off = [0]

    def carve(nbytes, at=None):
        o = off[0] if at is None else at
        o = (o + 31) // 32 * 32
        if at is None:
            off[0] = o + nbytes
        assert o + nbytes <= ARENA, (o, nbytes)
        return o

    def view(o, shape, dt):
        esz = 2 if dt == BF16 else 4
        n = int(np.prod(shape)) * esz
        ap = arena[:, o:o + n].bitcast(dt)
        if len(shape) == 2:
            ap = ap.rearrange("p (a b) -> p a b", b=shape[1])
        elif len(shape) == 3:
            ap = ap.rearrange("p (a b c) -> p a b c", b=shape[1], c=shape[2])
        return ap

    XT = view(carve(16 * T * 2), [16, T], BF16)
    b_XT = [Buf(f"XT{i}") for i in range(4)]
    WSB = 11264
    ws_off = [carve(WSB), carve(WSB)]
    b_ws = [Buf("ws0"), Buf("ws1")]
    wsi = [0]
    c_off = carve(8192)
    co = [c_off]

    def ccarve(n):
        o = (co[0] + 31) // 32 * 32
        co[0] = o + n
        assert co[0] <= c_off + 8192
        return o
    ident32 = view(ccarve(512), [128], F32)
    identb = view(ccarve(256), [128], BF16)
    onesb = view(ccarve(256), [128], BF16)
    lnp_sb = view(ccarve(L * 3 * 2 * 16 * 4), [L * 3 * 2 * 16], F32)
    lnpa_sb = view(ccarve(L * 3 * 2 * 16 * 4), [L * 3 * 2 * 16], F32)
    cvp_sb = view(ccarve(L * 8 * (CW + 3) * 4), [L * 8 * (CW + 3)], F32)
    sink_sb = view(ccarve(L * 8 * 4), [L * 8], F32)
    small = view(ccarve(64 * 4), [64], F32)
    mask_sb = view(ccarve(1536), [384], F32)
    b_const = Buf("const")
    AT_BYTES = FC * 1024 * 2
    at_off = carve(AT_BYTES)
    A_ts = [view(carve(2048), [512], F32) for _ in range(2)]
    B_ts = [view(carve(2048), [512], F32) for _ in range(2)]
    b_As, b_Bs = [Buf("A0"), Buf("A1")], [Buf("B0"), Buf("B1")]
    NR = 7
    ring = Ring([Tile(view(carve(2048), [512], F32), Buf(f"r{i}"), f"r{i}") for i in range(NR)])
    NB = 3
    bring = Ring([Tile(view(carve(1024), [512], BF16), Buf(f"b{i}"), f"b{i}") for i in range(NB)])
    ps = [nc.alloc_psum_tensor(f"ps{i}", [128, 512], F32) for i in range(8)]
    b_ps = [Buf(f"ps{i}") for i in range(8)]
    bank_ring = [list(range(8))]
    bank_i = [0]

    def bank():
        r = bank_ring[0]
        b = r[bank_i[0] % len(r)]
        bank_i[0] += 1
        return b

    def wslot():
        s = wsi[0] % 2
        wsi[0] += 1
        return s

    def ws_view(sl, shape, byte_off=0):
        return view(ws_off[sl] + byte_off, shape, BF16)

    def load_w(sl, pieces):
        P.dma("pool", pieces, f"w{sl}", writes=[b_ws[sl]])

    def mm(b, lhs_fn, rhs_fn, kc_n, reads, ncols=512, extra_writes=()):
        fns = []
        for kc in range(kc_n):
            lh, rh = lhs_fn(kc), rhs_fn(kc)
            fns.append(lambda e, kc=kc, lh=lh, rh=rh: e.matmul(ps[b][:, 0:ncols], lhsT=lh, rhs=rh,
                                                               start=(kc == 0), stop=(kc == kc_n - 1)))
        P.op("pe", fns, reads=reads, writes=[b_ps[b]] + list(extra_writes))

    P.dma("sp", [(ident32, identd), (lnp_sb, lnp), (cvp_sb, cvp), (sink_sb, sinkb), (mask_sb, g_mask())], "cld", writes=[b_const])
    P.op("dve", lambda e: e.tensor_copy(out=identb, in_=ident32), reads=[b_const], writes=[b_const])
    P.op("dve", lambda e: e.memset(onesb, 1.0), writes=[b_const])
    P.op("dve", lambda e: e.tensor_scalar(out=lnpa_sb, in0=lnp_sb, scalar1=ALPHA, scalar2=None, op0=ALU.mult),
         reads=[b_const], writes=[b_const])

    def lnvec(l, k, gb, c, scaled=False):
        base = ((l * 3 + k) * 2 + gb) * 16 + c
        t = lnpa_sb if scaled else lnp_sb
        return t[:, base:base + 1]

    def cvvec(l, i, j):
        base = (l * 8 + i) * (CW + 3) + j
        return cvp_sb[:, base:base + 1]

    for tt in range(4):
        P.dma("pool", [(XT[:, :, tt * 512:(tt + 1) * 512], xT_v[:, :, tt * 512:(tt + 1) * 512])], "xld",
              writes=[b_XT[tt]])

    S1 = [4, 6]
    S2 = [5, 7]

    pending_stats = []

    def flush_stats():
        while pending_stats:
            pending_stats.pop(0)()

    raw_x = [False]

    def resid_epilogue(b, c, tok0, tt, scale, first, last):
        flush_stats()
        gt = tok0 // 512
        tk = slice(tok0, tok0 + 512)
        hres = ring.get()
        zt = ring.get()
        if raw_x[0]:
            P.dma("sp", [(hres.ap, xT_v[:, c, tk])], hres.sem, writes=[hres.buf])
            P.op("dve", lambda e: e.tensor_scalar(out=zt.ap, in0=ps[b][:, :], scalar1=float(scale), scalar2=None, op0=ALU.mult),
                 reads=[b_ps[b]], writes=[zt.buf])
            P.op("dve", lambda e: e.scalar_tensor_tensor(out=zt.ap, in0=hres.ap, scalar=ALPHA, in1=zt.ap,
                                                         op0=ALU.mult, op1=ALU.add),
                 reads=[hres.buf, zt.buf], writes=[zt.buf])
        else:
            P.dma("sp", [(hres.ap, h32a_v[:, c, tk])], hres.sem, reads=[b_h32a[c][gt]], writes=[hres.buf])
            P.op("dve", lambda e: e.scalar_tensor_tensor(out=zt.ap, in0=ps[b][:, :], scalar=float(scale), in1=hres.ap,
                                                         op0=ALU.mult, op1=ALU.add),
                 reads=[b_ps[b], hres.buf], writes=[zt.buf])
        P.dma("sp", [(zd_v[:, c, tk], zt.ap)], zt.sem, reads=[zt.buf], writes=[b_zd[c][gt]])
        zb = bring.get()
        P.op("act", lambda e: e.activation(out=zb.ap, in_=zt.ap, func=AF.Copy), reads=[zt.buf], writes=[zb.buf])
        zq = bring.get()
        P.op("act", lambda e: e.activation(out=zq.ap, in_=zt.ap, func=AF.Square), reads=[zt.buf], writes=[zq.buf])
        pending_stats.append(lambda: P.op(
            "pe", [lambda e: e.matmul(ps[S1[tt]][:, :], lhsT=onesb, rhs=zb.ap, start=first, stop=last),
                   lambda e: e.matmul(ps[S2[tt]][:, :], lhsT=onesb, rhs=zq.ap, start=first, stop=last)],
            reads=[zb.buf, zq.buf, b_const], writes=[b_ps[S1[tt]], b_ps[S2[tt]]]))

    def stats_to_AB(s1b, s2b, dn, ai=0):
        A_t, B_t, b_A, b_B = A_ts[ai], B_ts[ai], b_As[ai], b_Bs[ai]
        m = ring.get()
        P.op("dve", lambda e: e.tensor_scalar(out=m.ap, in0=ps[s1b][:, :], scalar1=1.0 / dn, scalar2=None, op0=ALU.mult),
             reads=[b_ps[s1b]], writes=[m.buf])
        v = ring.get()
        P.op("dve", lambda e: e.tensor_tensor(out=v.ap, in0=m.ap, in1=m.ap, op=ALU.mult), reads=[m.buf], writes=[v.buf])
        P.op("dve", lambda e: e.scalar_tensor_tensor(out=v.ap, in0=ps[s2b][:, :], scalar=1.0 / dn, in1=v.ap,
                                                     op0=ALU.mult, op1=ALU.subtract),
             reads=[b_ps[s2b], v.buf], writes=[v.buf])
        P.op("dve", lambda e: e.tensor_scalar(out=v.ap, in0=v.ap, scalar1=EPS, scalar2=None, op0=ALU.add),
             reads=[v.buf], writes=[v.buf])
        P.op("act", lambda e: e.activation(out=v.ap, in_=v.ap, func=AF.Sqrt), reads=[v.buf], writes=[v.buf])
        P.op("dve", lambda e: e.reciprocal(out=A_t, in_=v.ap), reads=[v.buf], writes=[b_A])
        P.op("dve", lambda e: e.scalar_tensor_tensor(out=B_t, in0=m.ap, scalar=-1.0, in1=A_t, op0=ALU.mult, op1=ALU.mult),
             reads=[m.buf, b_A], writes=[b_B])

    pending_norm = [None]

    def bg_step(n=1):
        g = pending_norm[0]
        if g is None:
            return
        for _ in range(n):
            try:
                next(g)
            except StopIteration:
                pending_norm[0] = None
                return

    def drain_bg():
        while pending_norm[0] is not None:
            bg_step(8)

    def normalize(l, k, t0, ntt, final, pf=3):
        assert pending_norm[0] is None
        flush_stats()
        for tt in range(ntt):
            stats_to_AB(S1[tt], S2[tt], D, tt)

        def gen():
            PF = pf
            tiles = [(tt, c) for tt in range(ntt) for c in range(16)]
            loads = {}

            def issue_load(idx):
                tt, c = tiles[idx]
                tok0 = t0 + tt * 512
                zl = ring.get(pin=True)
                P.dma("sp", [(zl.ap, zd_v[:, c, tok0:tok0 + 512])], zl.sem, reads=[b_zd[c][tok0 // 512]], writes=[zl.buf])
                loads[idx] = zl
            for idx in range(min(PF, len(tiles))):
                issue_load(idx)
            for idx, (tt, c) in enumerate(tiles):
                tok0 = t0 + tt * 512
                gt = tok0 // 512
                tk = slice(tok0, tok0 + 512)
                A_t, B_t, b_A, b_B = A_ts[tt], B_ts[tt], b_As[tt], b_Bs[tt]
                if idx + PF < len(tiles):
                    issue_load(idx + PF)
                zl = loads.pop(idx)
                P.op("dve", lambda e, zl=zl, A_t=A_t: e.tensor_tensor(out=zl.ap, in0=zl.ap, in1=A_t, op=ALU.mult),
                     reads=[zl.buf, b_A], writes=[zl.buf])
                P.op("dve", lambda e, zl=zl, B_t=B_t: e.tensor_tensor(out=zl.ap, in0=zl.ap, in1=B_t, op=ALU.add),
                     reads=[zl.buf, b_B], writes=[zl.buf])
                if final:
                    ho = ring.get()
                    P.op("act", lambda e, zl=zl, ho=ho, c=c: e.activation(
                        out=ho.ap, in_=zl.ap, func=AF.Identity, scale=lnvec(l, k, 0, c), bias=lnvec(l, k, 1, c)),
                        reads=[zl.buf, b_const], writes=[ho.buf])
                    P.dma("sp", [(outT_v[:, c, tk], ho.ap)], ho.sem, reads=[ho.buf], writes=[b_out])
                else:
                    P.op("act", lambda e, zl=zl, c=c, tk=tk: e.activation(
                        out=XT[:, c, tk], in_=zl.ap, func=AF.Identity, scale=lnvec(l, k, 0, c), bias=lnvec(l, k, 1, c)),
                        reads=[zl.buf, b_const], writes=[b_XT[gt]])
                    ho = ring.get()
                    P.op("act", lambda e, zl=zl, ho=ho, c=c: e.activation(
                        out=ho.ap, in_=zl.ap, func=AF.Identity, scale=lnvec(l, k, 0, c, True), bias=lnvec(l, k, 1, c, True)),
                        reads=[zl.buf, b_const], writes=[ho.buf])
                    P.dma("sp", [(h32a_v[:, c, tk], ho.ap)], ho.sem, reads=[ho.buf], writes=[b_h32a[c][gt]])
                zl.pinned = False
                yield
        pending_norm[0] = gen()

    def ffn(l, which, final):
        wu = g_w_up[which]()[l]
        wd = g_w_dn[which]()[l]
        AT = view(at_off, [FC, 1024], BF16)
        for th in range(2):
            t0 = th * 1024
            b_AT = [P.fresh("AT0"), P.fresh("AT1")]
            bank_ring[0] = list(range(8))
            for f in range(FC):
                sl = wslot()
                wv = ws_view(sl, [16, 256])
                load_w(sl, [(wv[:, 0:8, :], wu[f][:, 0:8, :]), (wv[:, 8:16, :], wu[f][:, 8:16, :])])
                for tt in range(2):
                    gt = th * 2 + tt
                    tk = slice(t0 + tt * 512, t0 + (tt + 1) * 512)
                    bg, bu = bank(), bank()
                    mm(bg, lambda kc: wv[:, kc, 0:128], lambda kc: XT[:, kc, tk], 16, [b_ws[sl], b_XT[gt]])
                    mm(bu, lambda kc: wv[:, kc, 128:256], lambda kc: XT[:, kc, tk], 16, [b_ws[sl], b_XT[gt]])
                    sg = ring.get()
                    P.op("act", lambda e, sg=sg, bg=bg: e.activation(out=sg.ap, in_=ps[bg][:, :], func=AF.Silu),
                         reads=[b_ps[bg]], writes=[sg.buf])
                    P.op("dve", lambda e, sg=sg, bu=bu, f=f, tt=tt: e.tensor_tensor(
                        out=AT[:, f, tt * 512:(tt + 1) * 512], in0=sg.ap, in1=ps[bu][:, :], op=ALU.mult),
                        reads=[sg.buf, b_ps[bu]], writes=[b_AT[tt]])
                    bg_step()
            drain_bg()
            bank_ring[0] = [0, 1, 2, 3]
            for n in range(16):
                sl = wslot()
                wv = ws_view(sl, [FC, 128])
                load_w(sl, [(wv[:, 0:16, :], wd[n][:, 0:16, :]), (wv[:, 16:32, :], wd[n][:, 16:32, :]),
                            (wv[:, 32:44, :], wd[n][:, 32:44, :])])
                for tt in range(2):
                    b = bank()
                    mm(b, lambda fc: wv[:, fc, :], lambda fc: AT[:, fc, tt * 512:(tt + 1) * 512], FC,
                       [b_ws[sl], b_AT[tt]])
                    raw_x[0] = (l == 0 and which == 0)
                    resid_epilogue(b, n, t0 + tt * 512, tt, 0.5, n == 0, n == 15)
                    raw_x[0] = False
            normalize(l, 0 if which == 0 else 2, t0, 2, final)

    b_small = [Buf(f"sm{i}") for i in range(64)]
    small_i = [0]

    def col():
        i = small_i[0] % 64
        small_i[0] += 1
        return small[:, i:i + 1], b_small[i]

    def load_chunks(srcs, kc):
        sl = wslot()
        views = [ws_view(sl, [kc, 128], g * kc * 256) for g in range(len(srcs))]
        load_w(sl, [(views[g], srcs[g]) for g in range(len(srcs))])
        return sl, views

    def run_rr(gens):
        gens = list(gens)
        while gens:
            for g in list(gens):
                try:
                    next(g)
                except StopIteration:
                    gens.remove(g)

    def attn_pipeline(items, s_ring, p_ring, pT_ring, K, w_out_cols, evac):
        st = [dict() for _ in items]

        def stage_a(it, d):
            nk = it["nk"]
            b = bank()
            it["qk"](b)
            yield
            s = s_ring.get()
            sa = s.ap[:, 0:nk]
            if it["mask"] is not None:
                mk = it["mask"]
                P.op("dve", lambda e: e.scalar_tensor_tensor(out=sa, in0=ps[b][:, 0:nk], scalar=float(it["scale"]), in1=mk,
                                                             op0=ALU.mult, op1=ALU.add),
                     reads=[b_ps[b], it["mask_buf"]], writes=[s.buf])
            else:
                P.op("dve", lambda e: e.tensor_scalar(out=sa, in0=ps[b][:, 0:nk], scalar1=float(it["scale"]), scalar2=None,
                                                      op0=ALU.mult), reads=[b_ps[b]], writes=[s.buf])
            yield
            mx, b_mx = col()
            P.op("dve", lambda e: e.reduce_max(out=mx, in_=sa, axis=AX.X), reads=[s.buf], writes=[b_mx])
            yield
            if it["sink"] is not None:
                P.op("dve", lambda e: e.tensor_tensor(out=mx, in0=mx, in1=it["sink"], op=ALU.max),
                     reads=[b_mx, b_const], writes=[b_mx])
                yield
            nm, b_nm = col()
            rs, b_rs = col()
            P.op("dve", [lambda e: e.memset(rs, 0.0),
                         lambda e: e.tensor_scalar(out=nm, in0=mx, scalar1=-1.0, scalar2=None, op0=ALU.mult)],
                 reads=[b_mx], writes=[b_nm, b_rs])
            yield
            P.op("act", lambda e: e.activation(out=sa, in_=sa, func=AF.Exp, bias=nm, accum_out=rs),
                 reads=[s.buf, b_nm, b_rs], writes=[s.buf, b_rs])
            if it["sink"] is not None:
                es, b_es = col()
                P.op("act", lambda e: e.activation(out=es, in_=nm, func=AF.Exp, bias=it["sink"]),
                     reads=[b_nm, b_const], writes=[b_es])
                yield
                P.op("dve", lambda e: e.tensor_tensor(out=rs, in0=rs, in1=es, op=ALU.add), reads=[b_rs, b_es], writes=[b_rs])
            yield
            P.op("dve", lambda e: e.reciprocal(out=rs, in_=rs), reads=[b_rs], writes=[b_rs])
            yield
            pb = p_ring.get()
            P.op("dve", lambda e: e.tensor_scalar(out=pb.ap[:, 0:nk], in0=sa, scalar1=rs, scalar2=None, op0=ALU.mult),
                 reads=[s.buf, b_rs], writes=[pb.buf])
            d["pb"] = pb
            yield

        def stage_b(it, d):
            nk = it["nk"]
            pb = d["pb"]
            bt = bank()
            pst = ps[bt][:, :].bitcast(BF16)
            fns = []
            for kb in range(nk // 128):
                fns.append(lambda e, kb=kb: e.transpose(out=pst[:, kb * 128:(kb + 1) * 128],
                                                        in_=pb.ap[:, kb * 128:(kb + 1) * 128], identity=identb))
            P.op("pe", fns, reads=[pb.buf, b_const], writes=[b_ps[bt]])
            yield
            pT = pT_ring.get()
            P.op("act", lambda e: e.activation(out=pT.ap[:, 0:nk], in_=pst[:, 0:nk], func=AF.Copy),
                 reads=[b_ps[bt]], writes=[pT.buf])
            d["pT"] = pT
            yield

        def stage_c(i0, n_it):
            per = 512 // w_out_cols
            for j0 in range(0, n_it, per):
                bo = bank()
                cnt = min(per, n_it - j0)
                for j in range(cnt):
                    d = st[i0 + j0 + j]
                    items[i0 + j0 + j]["pv"](bo, j, cnt, d["pT"].ap, d["pT"].buf)
                evac(bo, i0 + j0, cnt)

        def stage_a_group(idxs):
            k = len(idxs)
            blk16 = (small_i[0] % 4) * 16
            small_i[0] += 1
            mxs, nms, rss, ess = [small[:, blk16 + q * 4: blk16 + q * 4 + k] for q in range(4)]
            bmx = [b_small[blk16 + j] for j in range(k)]
            bnm = [b_small[blk16 + 4 + j] for j in range(k)]
            brs = [b_small[blk16 + 8 + j] for j in range(k)]
            bes = [b_small[blk16 + 12 + j] for j in range(k)]
            its = [items[i] for i in idxs]
            sink = its[0]["sink"]
            banks, ss = [], []
            for j, it in enumerate(its):
                b = bank()
                it["qk"](b)
                banks.append(b)
            for j, it in enumerate(its):
                nk = it["nk"]
                b = banks[j]
                s_ = s_ring.get()
                ss.append(s_)
                sa = s_.ap[:, 0:nk]
                if it["mask"] is not None:
                    P.op("dve", lambda e, sa=sa, b=b, nk=nk, it=it: e.scalar_tensor_tensor(
                        out=sa, in0=ps[b][:, 0:nk], scalar=float(it["scale"]), in1=it["mask"], op0=ALU.mult, op1=ALU.add),
                        reads=[b_ps[b], it["mask_buf"]], writes=[s_.buf])
                else:
                    P.op("dve", lambda e, sa=sa, b=b, nk=nk, it=it: e.tensor_scalar(
                        out=sa, in0=ps[b][:, 0:nk], scalar1=float(it["scale"]), scalar2=None, op0=ALU.mult),
                        reads=[b_ps[b]], writes=[s_.buf])
            for j, it in enumerate(its):
                sa = ss[j].ap[:, 0:it["nk"]]
                P.op("dve", lambda e, sa=sa, j=j: e.reduce_max(out=mxs[:, j:j + 1], in_=sa, axis=AX.X),
                     reads=[ss[j].buf], writes=[bmx[j]])
            if sink is not None:
                P.op("dve", lambda e: e.tensor_scalar(out=mxs, in0=mxs, scalar1=sink, scalar2=None, op0=ALU.max),
                     reads=bmx + [b_const], writes=bmx)
            P.op("dve", [lambda e: e.memset(rss, 0.0),
                         lambda e: e.tensor_scalar(out=nms, in0=mxs, scalar1=-1.0, scalar2=None, op0=ALU.mult)],
                 reads=bmx, writes=bnm + brs)
            for j, it in enumerate(its):
                sa = ss[j].ap[:, 0:it["nk"]]
                P.op("act", lambda e, sa=sa, j=j: e.activation(out=sa, in_=sa, func=AF.Exp, bias=nms[:, j:j + 1],
                                                               accum_out=rss[:, j:j + 1]),
                     reads=[ss[j].buf, bnm[j], brs[j]], writes=[ss[j].buf, brs[j]])
            if sink is not None:
                P.op("act", lambda e: e.activation(out=ess, in_=nms, func=AF.Exp, bias=sink), reads=bnm + [b_const], writes=bes)
            return (idxs, its, ss, rss, ess, brs, bes, sink)

        def stage_a2(state):
            idxs, its, ss, rss, ess, brs, bes, sink = state
            if sink is not None:
                P.op("dve", lambda e: e.tensor_tensor(out=rss, in0=rss, in1=ess, op=ALU.add), reads=brs + bes, writes=brs)
            P.op("dve", lambda e: e.reciprocal(out=rss, in_=rss), reads=brs, writes=brs)
            for j, it in enumerate(its):
                nk = it["nk"]
                sa = ss[j].ap[:, 0:nk]
                pb = p_ring.get()
                P.op("dve", lambda e, sa=sa, pb=pb, nk=nk, j=j: e.tensor_scalar(
                    out=pb.ap[:, 0:nk], in0=sa, scalar1=rss[:, j:j + 1], scalar2=None, op0=ALU.mult),
                    reads=[ss[j].buf, brs[j]], writes=[pb.buf])
                st[idxs[j]]["pb"] = pb

        n = len(items)
        groups = [(i, min(K, n - i)) for i in range(0, n, K)]
        G = len(groups)
        a_state = {}
        for step in range(G + 3):
            if step < G:
                i0, c = groups[step]
                a_state[step] = stage_a_group(list(range(i0, i0 + c)))
            if 0 <= step - 1 < G:
                stage_a2(a_state.pop(step - 1))
            if 0 <= step - 2 < G:
                i0, c = groups[step - 2]
                run_rr(stage_b(items[i], st[i]) for i in range(i0, i0 + c))
            if 0 <= step - 3 < G:
                i0, c = groups[step - 3]
                stage_c(i0, c)

    def mixer(l, final):
        win = g_w_in()[l]
        wkv = g_w_kv()[l]
        memT_v = fm(g_memT())
        reg = [at_off]

        def lcarve(n):
            o = (reg[0] + 31) // 32 * 32
            reg[0] = o + n
            assert reg[0] <= at_off + AT_BYTES, (reg[0] - at_off, AT_BYTES)
            return o
        bank_ring[0] = list(range(8))
        memTb = view(lcarve(16 * 256 * 2), [16, 256], BF16)
        b_memT = P.fresh()
        mkT = view(lcarve(8 * 256 * 2), [8, 256], BF16)
        b_mk = P.fresh()
        mvt = view(lcarve(2 * 1024 * 2), [2, 1024], BF16)
        b_mv = P.fresh()
        m0_end = reg[0]
        P.dma("pool", [(memTb[:, 0:8, :], memT_v[:, 0:8, :]), (memTb[:, 8:16, :], memT_v[:, 8:16, :])], "xld",
              writes=[b_memT])
        for pr in range(4):
            sl, vs = load_chunks([wkv[2 * pr], wkv[2 * pr + 1]], 16)
            for g in range(2):
                b = bank()
                mm(b, lambda kc: vs[g][:, kc, :], lambda kc: memTb[:, kc, :], 16, [b_ws[sl], b_memT], ncols=256)
                P.op("act", lambda e, b=b, ch=2 * pr + g: e.activation(out=mkT[:, ch, :], in_=ps[b][:, 0:256], func=AF.Copy),
                     reads=[b_ps[b]], writes=[b_mk])
                bg_step(2)
        for ct in range(4):
            sl = wslot()
            wv = ws_view(sl, [16, 256])
            load_w(sl, [(wv[:, :, 0:128], wkv[8 + 2 * ct]), (wv[:, :, 128:256], wkv[9 + 2 * ct])])
            for mb in range(2):
                b = bank()
                mm(b, lambda kc: memTb[:, kc, mb * 128:(mb + 1) * 128], lambda kc: wv[:, kc, :], 16,
                   [b_ws[sl], b_memT], ncols=256)
                P.op("act", lambda e, b=b, mb=mb, ct=ct: e.activation(out=mvt[:, mb, ct * 256:(ct + 1) * 256],
                                                                      in_=ps[b][:, 0:256], func=AF.Copy),
                     reads=[b_ps[b]], writes=[b_mv])
                bg_step(2)
        drain_bg()
        rC = view(lcarve(8192), [T], F32)
        rS = view(lcarve(8192), [T], F32)
        b_rt = P.fresh()
        P.dma("sp", [(rC[0:32, :], g_ropeC()), (rS[0:32, :], g_ropeS())], "ld_rt", writes=[b_rt])
        for i in range(8):
            sl, (va, vg) = load_chunks([win[i], win[8 + i]], 16)
            for tt in range(4):
                tk = slice(tt * 512, (tt + 1) * 512)
                ba, bg = bank(), bank()
                mm(ba, lambda kc: va[:, kc, :], lambda kc: XT[:, kc, tk], 16, [b_ws[sl], b_XT[tt]])
                mm(bg, lambda kc: vg[:, kc, :], lambda kc: XT[:, kc, tk], 16, [b_ws[sl], b_XT[tt]])
                sg = ring.get()
                P.op("act", lambda e, sg=sg, bg=bg: e.activation(out=sg.ap, in_=ps[bg][:, :], func=AF.Sigmoid),
                     reads=[b_ps[bg]], writes=[sg.buf])
                hb = bring.get()
                P.op("dve", lambda e, sg=sg, ba=ba, hb=hb: e.tensor_tensor(out=hb.ap, in0=sg.ap, in1=ps[ba][:, :], op=ALU.mult),
                     reads=[sg.buf, b_ps[ba]], writes=[hb.buf])
                P.dma("sp", [(hc_v[:, i, tk], hb.ap)], hb.sem, reads=[hb.buf], writes=[b_hc[i][tt]])
        qk_pending = []
        qk_chunks = [(16 + 2 * hp, 17 + 2 * hp, 2 * hp) for hp in range(4)] + [(24, 25, 8)]
        for (c0, c1, row0) in qk_chunks:
            sl, vs = load_chunks([win[c0], win[c1]], 16)
            for tt in range(4):
                tk = slice(tt * 512, (tt + 1) * 512)
                for g in range(2):
                    b = bank()
                    mm(b, lambda kc: vs[g][:, kc, :], lambda kc: XT[:, kc, tk], 16, [b_ws[sl], b_XT[tt]])
                    r = ring.get()
                    P.op("act", lambda e, r=r, b=b: e.activation(out=r.ap, in_=ps[b][:, :], func=AF.Copy),
                         reads=[b_ps[b]], writes=[r.buf])
                    ro = ring.get()
                    P.dma("sp", [(ro.ap[0:16, :], r.ap[16:32, :]), (ro.ap[16:32, :], r.ap[0:16, :])], ro.sem,
                          reads=[r.buf], writes=[ro.buf])
                    if qk_pending:
                        qk_pending.pop(0)()
                    P.op("dve", lambda e, ro=ro, tk=tk: e.tensor_tensor(out=ro.ap[0:32, :], in0=ro.ap[0:32, :], in1=rS[0:32, tk], op=ALU.mult),
                         reads=[ro.buf, b_rt], writes=[ro.buf])
                    P.op("dve", lambda e, r=r, tk=tk: e.tensor_tensor(out=r.ap[0:32, :], in0=r.ap[0:32, :], in1=rC[0:32, tk], op=ALU.mult),
                         reads=[r.buf, b_rt], writes=[r.buf])
                    P.op("dve", lambda e, r=r, ro=ro: e.tensor_tensor(out=r.ap[0:32, :], in0=r.ap[0:32, :], in1=ro.ap[0:32, :], op=ALU.add),
                         reads=[r.buf, ro.buf], writes=[r.buf])
                    qb_ = bring.get()
                    P.op("dve", lambda e, r=r, qb_=qb_: e.tensor_copy(out=qb_.ap, in_=r.ap),
                         reads=[r.buf], writes=[qb_.buf])
                    qk_pending.append(lambda qb_=qb_, row=row0 + g, tk=tk, tt=tt: P.dma(
                        "sp", [(qk_v[:, row, tk], qb_.ap)], qb_.sem, reads=[qb_.buf], writes=[b_qk[row][tt]]))
        while qk_pending:
            qk_pending.pop(0)()
        sl = wslot()
        wv = ws_view(sl, [16, 256])
        load_w(sl, [(wv[:, :, 0:128], win[26]), (wv[:, :, 128:256], win[27])])
        for tb in range(16):
            b = bank()
            mm(b, lambda kc: XT[:, kc, tb * 128:(tb + 1) * 128], lambda kc: wv[:, kc, :], 16, [b_ws[sl], b_XT[tb // 4]],
               ncols=256)
            hb = bring.get()
            P.op("act", lambda e, hb=hb, b=b: e.activation(out=hb.ap[:, 0:256], in_=ps[b][:, 0:256], func=AF.Copy),
                 reads=[b_ps[b]], writes=[hb.buf])
            P.dma("sp", [(v_tok[tb * 128:(tb + 1) * 128, :], hb.ap[:, 0:256])], hb.sem, reads=[hb.buf], writes=[b_vtok[tb]])
        for hp in range(4):
            sl, vs = load_chunks([win[28 + 2 * hp], win[29 + 2 * hp]], 16)
            for tt in range(4):
                tk = slice(tt * 512, (tt + 1) * 512)
                for g in range(2):
                    b = bank()
                    mm(b, lambda kc: vs[g][:, kc, :], lambda kc: XT[:, kc, tk], 16, [b_ws[sl], b_XT[tt]])
                    hb = bring.get()
                    P.op("act", lambda e, hb=hb, b=b: e.activation(out=hb.ap, in_=ps[b][:, :], func=AF.Copy),
                         reads=[b_ps[b]], writes=[hb.buf])
                    P.dma("sp", [(qm_v[:, 2 * hp + g, tk], hb.ap)], hb.sem, reads=[hb.buf], writes=[b_qm[2 * hp + g][tt]])

        KI = 4

        def mk_rings():
            s_r = Ring([Tile(view(lcarve(1536), [384], F32), P.fresh(), None) for _ in range(2 * KI)])
            p_r = Ring([Tile(view(lcarve(768), [384], BF16), P.fresh(), None) for _ in range(2 * KI)])
            t_r = Ring([Tile(view(lcarve(768), [384], BF16), P.fresh(), None) for _ in range(2 * KI)])
            return s_r, p_r, t_r
        reg[0] = m0_end
        kTb = view(lcarve(4096), [T], BF16); b_kT = P.fresh()
        vt = view(lcarve(4096), [16, 128], BF16); b_vt = P.fresh()
        qTb = [view(lcarve(4096), [T], BF16) for _ in range(2)]; b_qT = [P.fresh(), P.fresh()]
        ast = [view(lcarve(4096), [T], BF16) for _ in range(2)]; b_ast = [P.fresh(), P.fresh()]
        s_r, p_r, t_r = mk_rings()

        def load_q(h):
            P.dma("sp", [(qTb[h % 2], qk_v[:, h, :])], f"ld_q{h % 2}", reads=b_qk[h], writes=[b_qT[h % 2]])
        load_q(0)
        for h in range(8):
            kvh = h // 4
            if h % 4 == 0:
                P.dma("sp", [(kTb, qk_v[:, 8 + kvh, :])], "ld_k", reads=b_qk[8 + kvh], writes=[b_kT])
                P.dma("sp", [(vt, v_tok.rearrange("(tb p) d -> p tb d", p=128)[:, :, kvh * 128:(kvh + 1) * 128])], "ld_vt",
                      reads=b_vtok, writes=[b_vt])
            qT = qTb[h % 2]
            if h + 1 < 8:
                load_q(h + 1)
            a_st = ast[h % 2]
            sink_ap = sink_sb[:, l * 8 + h:l * 8 + h + 1]
            items = []
            for blk in range(16):
                lo, hi = max(0, blk - 1), min(15, blk + 1)
                nk = (hi - lo + 1) * 128
                m0 = (lo - (blk - 1)) * 128

                def qk(b, blk=blk, lo=lo, hi=hi, nk=nk, qT=qT, h=h):
                    P.op("pe", [lambda e: e.matmul(ps[b][:, 0:nk], lhsT=qT[:, blk * 128:(blk + 1) * 128],
                                                   rhs=kTb[:, lo * 128:(hi + 1) * 128], start=True, stop=True)],
                         reads=[b_qT[h % 2], b_kT], writes=[b_ps[b]])

                def pv(bo, j, cnt, pT, b_pT, blk=blk, lo=lo, nk=nk):
                    fns = []
                    nkb = nk // 128
                    for kb in range(nkb):
                        fns.append(lambda e, kb=kb: e.matmul(ps[bo][:, j * 128:(j + 1) * 128], lhsT=vt[:, lo + kb, :],
                                                             rhs=pT[:, kb * 128:(kb + 1) * 128], start=(kb == 0),
                                                             stop=(kb == nkb - 1)))
                    P.op("pe", fns, reads=[b_vt, b_pT], writes=[b_ps[bo]])
                items.append(dict(qk=qk, nk=nk, mask=mask_sb[:, m0:m0 + nk], mask_buf=b_const, scale=128 ** -0.5,
                                  sink=sink_ap, pv=pv))

            def evac3(bo, i0, cnt, a_st=a_st, h=h):
                P.op("act", lambda e: e.activation(out=a_st[:, i0 * 128:(i0 + cnt) * 128], in_=ps[bo][:, 0:cnt * 128], func=AF.Copy),
                     reads=[b_ps[bo]], writes=[b_ast[h % 2]])
            attn_pipeline(items, s_r, p_r, t_r, KI, 128, evac3)
            P.dma("sp", [(attn_v[:, h, :], a_st)], f"st_ast{h % 2}", reads=[b_ast[h % 2]], writes=[b_attn[h]])

        reg[0] = m0_end
        qmb = [view(lcarve(8192), [2, T], BF16) for _ in range(2)]; b_qmb = [P.fresh(), P.fresh()]
        mst = [view(lcarve(8192), [2, T], BF16) for _ in range(2)]; b_mst = [P.fresh(), P.fresh()]
        s_r, p_r, t_r = mk_rings()
        for mh in range(4):
            qb = qmb[mh % 2]
            ms = mst[mh % 2]
            P.dma("sp", [(qb, qm_v[:, 2 * mh:2 * mh + 2, :])], f"ld_qm{mh % 2}", reads=b_qm[2 * mh] + b_qm[2 * mh + 1],
                  writes=[b_qmb[mh % 2]])
            items = []
            for blk in range(16):
                def qk(b, blk=blk, qb=qb, mh=mh):
                    P.op("pe", [lambda e, dc=dc: e.matmul(ps[b][:, 0:256], lhsT=qb[:, dc, blk * 128:(blk + 1) * 128],
                                                          rhs=mkT[:, 2 * mh + dc, :], start=(dc == 0), stop=(dc == 1))
                                for dc in range(2)],
                         reads=[b_qmb[mh % 2], b_mk], writes=[b_ps[b]])

                def pv(bo, j, cnt, pT, b_pT, blk=blk, mh=mh):
                    fns = []
                    for dc in range(2):
                        for mb in range(2):
                            c0 = dc * 256 + j * 128
                            fns.append(lambda e, dc=dc, mb=mb, c0=c0: e.matmul(
                                ps[bo][:, c0:c0 + 128], lhsT=mvt[:, mb, mh * 256 + dc * 128:mh * 256 + (dc + 1) * 128],
                                rhs=pT[:, mb * 128:(mb + 1) * 128], start=(mb == 0), stop=(mb == 1)))
                    P.op("pe", fns, reads=[b_mv, b_pT], writes=[b_ps[bo]])
                items.append(dict(qk=qk, nk=256, mask=None, mask_buf=None, scale=256 ** -0.5, sink=None, pv=pv))

            def evac4(bo, i0, cnt, ms=ms, mh=mh):
                assert cnt == 2
                P.op("act", lambda e: e.activation(out=ms[:, :, i0 * 128:(i0 + 2) * 128],
                                                   in_=ps[bo][:, 0:512].rearrange("p (a b) -> p a b", a=2), func=AF.Copy),
                     reads=[b_ps[bo]], writes=[b_mst[mh % 2]])
            attn_pipeline(items, s_r, p_r, t_r, KI, 256, evac4)
            P.dma("sp", [(memo_v[:, 2 * mh:2 * mh + 2, :], ms)], f"st_mst{mh % 2}", reads=[b_mst[mh % 2]], writes=[b_memo[mh]])

        reg[0] = at_off
        acc = view(lcarve(8 * T * 4), [8, T], F32)
        b_acc = [P.fresh() for _ in range(4)]
        diag = [view(lcarve(CW * 128 * 2), [CW, 128], BF16) for _ in range(2)]; b_dg = [P.fresh(), P.fresh()]
        hcp = [view(lcarve(2080 * 2), [2080], BF16) for _ in range(2)]; b_hcp = [P.fresh(), P.fresh()]
        for i in range(8):
            hb_, dg = hcp[i % 2], diag[i % 2]
            P.op("dve", [lambda e, hb_=hb_: e.memset(hb_[:, 0:16], 0.0), lambda e, hb_=hb_: e.memset(hb_[:, 2064:2080], 0.0)],
                 writes=[b_hcp[i % 2]])
            P.dma("sp", [(hb_[:, 16:2064], hc_v[:, i, :])], f"ld_hcp{i % 2}", reads=b_hc[i], writes=[b_hcp[i % 2]])
            P.op("dve", [lambda e, j=j, dg=dg, i=i: e.tensor_scalar(out=dg[:, j, :], in0=identb, scalar1=cvvec(l, i, j),
                                                                     scalar2=None, op0=ALU.mult) for j in range(CW)],
                 reads=[b_const], writes=[b_dg[i % 2]])
            for tt in range(4):
                b = bank()
                fns = [lambda e, j=j, dg=dg, hb_=hb_, tt=tt, b=b: e.matmul(
                    ps[b][:, :], lhsT=dg[:, j, :], rhs=hb_[:, tt * 512 + j + 1:tt * 512 + j + 513], start=(j == 0),
                    stop=(j == CW - 1)) for j in range(CW)]
                P.op("pe", fns, reads=[b_dg[i % 2], b_hcp[i % 2]], writes=[b_ps[b]])
                P.op("act", lambda e, b=b, i=i, tt=tt: e.activation(out=acc[:, i, tt * 512:(tt + 1) * 512], in_=ps[b][:, :],
                                                                    func=AF.Identity, bias=cvvec(l, i, CW)),
                     reads=[b_ps[b], b_const], writes=[b_acc[tt]])
        for tt in range(4):
            tk = slice(tt * 512, (tt + 1) * 512)
            s1, s2 = bank(), bank()
            for i in range(8):
                zb = bring.get()
                P.op("act", lambda e, zb=zb, i=i, tk=tk: e.activation(out=zb.ap, in_=acc[:, i, tk], func=AF.Copy),
                     reads=[b_acc[tt]], writes=[zb.buf])
                zq = bring.get()
                P.op("act", lambda e, zq=zq, i=i, tk=tk: e.activation(out=zq.ap, in_=acc[:, i, tk], func=AF.Square),
                     reads=[b_acc[tt]], writes=[zq.buf])
                P.op("pe", [lambda e, zb=zb, i=i, s1=s1: e.matmul(ps[s1][:, :], lhsT=onesb, rhs=zb.ap, start=(i == 0), stop=(i == 7)),
                            lambda e, zq=zq, i=i, s2=s2: e.matmul(ps[s2][:, :], lhsT=onesb, rhs=zq.ap, start=(i == 0), stop=(i == 7))],
                     reads=[zb.buf, zq.buf, b_const], writes=[b_ps[s1], b_ps[s2]])
            stats_to_AB(s1, s2, CONV_CH, 0)
            A_t, B_t, b_A, b_B = A_ts[0], B_ts[0], b_As[0], b_Bs[0]
            for i in range(8):
                y = ring.get()
                P.op("dve", lambda e, y=y, i=i, tk=tk: e.tensor_tensor(out=y.ap, in0=acc[:, i, tk], in1=A_t, op=ALU.mult),
                     reads=[b_acc[tt], b_A], writes=[y.buf])
                P.op("dve", lambda e, y=y: e.tensor_tensor(out=y.ap, in0=y.ap, in1=B_t, op=ALU.add),
                     reads=[y.buf, b_B], writes=[y.buf])
                cb = bring.get()
                P.op("act", lambda e, y=y, cb=cb, i=i: e.activation(out=cb.ap, in_=y.ap, func=AF.Silu,
                                                                    scale=cvvec(l, i, CW + 1), bias=cvvec(l, i, CW + 2)),
                     reads=[y.buf, b_const], writes=[cb.buf])
                P.dma("sp", [(conv_v[:, i, tk], cb.ap)], cb.sem, reads=[cb.buf], writes=[b_conv[i][tt]])

        wbr = [g_w_co()[l], g_w_wo()[l], g_w_mo()[l]]
        wo = g_w_out()[l]
        for th in range(2):
            t0 = th * 1024
            reg[0] = at_off
            br = [view(lcarve(8 * 1024 * 2), [8, 1024], BF16) for _ in range(3)]
            b_br = [P.fresh() for _ in range(3)]
            mg = view(lcarve(16 * 1024 * 2), [16, 1024], BF16)
            b_mg = [P.fresh(), P.fresh()]
            bank_ring[0] = list(range(8))
            rd_conv = [b_conv[i][2 * th + j] for i in range(8) for j in range(2)]
            P.dma("sp", [(br[0], conv_v[:, :, t0:t0 + 1024])], "ld_br0", reads=rd_conv, writes=[b_br[0]])
            P.dma("sp", [(br[1], attn_v[:, :, t0:t0 + 1024])], "ld_br1", reads=b_attn, writes=[b_br[1]])
            P.dma("sp", [(br[2], memo_v[:, :, t0:t0 + 1024])], "ld_br2", reads=b_memo, writes=[b_br[2]])
            for n in range(16):
                slA, (g0v, g1v) = load_chunks([win[36 + n], win[52 + n]], 16)
                slB = wslot()
                g2v = ws_view(slB, [16, 128], 0)
                bw = [ws_view(slB, [8, 128], 4096 + j * 2048) for j in range(3)]
                load_w(slB, [(g2v, win[68 + n])] + [(bw[j], wbr[j][n]) for j in range(3)])
                gv = [g0v, g1v, g2v]
                gsl = [slA, slA, slB]
                for tt in range(2):
                    gt = 2 * th + tt
                    tk = slice(t0 + tt * 512, t0 + (tt + 1) * 512)
                    lk = slice(tt * 512, (tt + 1) * 512)
                    gb = [bank() for _ in range(3)]
                    for j in range(3):
                        mm(gb[j], lambda kc: gv[j][:, kc, :], lambda kc: XT[:, kc, tk], 16, [b_ws[gsl[j]], b_XT[gt]])
                    yb = [bank() for _ in range(3)]
                    for j in range(3):
                        mm(yb[j], lambda kc: bw[j][:, kc, :], lambda kc: br[j][:, kc, lk], 8, [b_ws[slB], b_br[j]])
                    gts = [ring.get() for _ in range(3)]
                    for j in range(3):
                        P.op("act", lambda e, j=j, gts=gts, gb=gb: e.activation(out=gts[j].ap, in_=ps[gb[j]][:, :], func=AF.Sigmoid),
                             reads=[b_ps[gb[j]]], writes=[gts[j].buf])
                    m_, t_ = ring.get(), ring.get()
                    P.op("dve", lambda e, m_=m_, gts=gts, yb=yb: e.tensor_tensor(out=m_.ap, in0=gts[0].ap, in1=ps[yb[0]][:, :], op=ALU.mult),
                         reads=[gts[0].buf, b_ps[yb[0]]], writes=[m_.buf])
                    P.op("dve", lambda e, t_=t_, gts=gts, yb=yb: e.tensor_tensor(out=t_.ap, in0=gts[1].ap, in1=ps[yb[1]][:, :], op=ALU.mult),
                         reads=[gts[1].buf, b_ps[yb[1]]], writes=[t_.buf])
                    P.op("dve", lambda e, m_=m_, t_=t_: e.tensor_tensor(out=m_.ap, in0=m_.ap, in1=t_.ap, op=ALU.add),
                         reads=[m_.buf, t_.buf], writes=[m_.buf])
                    P.op("dve", lambda e, t_=t_, gts=gts, yb=yb: e.tensor_tensor(out=t_.ap, in0=gts[2].ap, in1=ps[yb[2]][:, :], op=ALU.mult),
                         reads=[gts[2].buf, b_ps[yb[2]]], writes=[t_.buf])
                    P.op("dve", lambda e, m_=m_, t_=t_, n=n, lk=lk: e.tensor_tensor(out=mg[:, n, lk], in0=m_.ap, in1=t_.ap, op=ALU.add),
                         reads=[m_.buf, t_.buf], writes=[b_mg[tt]])
                    bg_step()
            drain_bg()
            bank_ring[0] = [0, 1, 2, 3]
            for np_ in range(8):
                sl, vs = load_chunks([wo[2 * np_], wo[2 * np_ + 1]], 16)
                for g in range(2):
                    n = 2 * np_ + g
                    for tt in range(2):
                        b = bank()
                        lk = slice(tt * 512, (tt + 1) * 512)
                        mm(b, lambda kc: vs[g][:, kc, :], lambda kc: mg[:, kc, lk], 16, [b_ws[sl], b_mg[tt]])
                        resid_epilogue(b, n, t0 + tt * 512, tt, 1.0, n == 0, n == 15)
            normalize(l, 1, t0, 2, final, pf=(1 if th == 0 else 3))

    phases = []
    for l in range(depth):
        phases.append(("ffn", l, 0))
        phases.append(("mix", l))
        phases.append(("ffn", l, 1))
    if stop is not None:
        phases = phases[:stop]
    for i, ph in enumerate(phases):
        final = (i == len(phases) - 1)
        if ph[0] == "ffn":
            ffn(ph[1], ph[2], final)
        else:
            mixer(ph[1], final)

    drain_bg()
    fin = Buf()
    fin.w = b_out.w
    allb = [b_out] + [b for row in b_h32a for b in row] + b_attn + b_memo + [b for row in b_conv for b in row]
    P.wait_all("sp", allb)
    P.emit()
    print("n_ins", P.n_ins, "n_wait", P.n_wait, "arena", off[0])
    nc._used_inputs = set(_ins)
    return nc


def _c(a):
    return np.ascontiguousarray(a, dtype=np.float32)


def host_consts():
    pos = np.arange(T, dtype=np.float32)
    inv_freq = (np.float32(500000.0) ** (-np.arange(0, 32, 2, dtype=np.float32) / np.float32(32))).astype(np.float32)
    ang = (pos[:, None] * inv_freq[None, :]).astype(np.float32)
    cos, sin = np.cos(ang).astype(np.float32).T, np.sin(ang).astype(np.float32).T
    ropeC = np.concatenate([cos, cos], 0)
    ropeS = np.concatenate([-sin, sin], 0)
    i = np.arange(128)[:, None]
    c = np.arange(384)[None, :]
    mask = np.where((c >= i) & (c <= i + 256), 0.0, NEG).astype(np.float32)
    return {"ropeC": _c(ropeC), "ropeS": _c(ropeS), "mask": _c(mask), "ident": np.eye(128, dtype=np.float32)}


def host_weights(inp):
    o = {}
    for i in (1, 2):
        wu = np.asarray(inp[f"ffn{i}_w_up"]).reshape(L, 16, 128, 2, FC, 128)
        o[f"ffn{i}_w_up"] = _c(wu.transpose(0, 4, 2, 1, 3, 5)).reshape(L, FC, 128, 16, 256)
        wd = np.asarray(inp[f"ffn{i}_w_down"]).reshape(L, FC, 128, 16, 128)
        o[f"ffn{i}_w_down"] = _c(wd.transpose(0, 3, 2, 1, 4))
    wi = np.asarray(inp["w_in"]).reshape(L, 16, 128, NCH_IN, 128)
    o["w_in"] = _c(wi.transpose(0, 3, 2, 1, 4))
    for nm in ("conv_w_out", "win_w_o", "mem_w_o"):
        w = np.asarray(inp[nm]).reshape(L, 8, 128, 16, 128)
        o[nm] = _c(w.transpose(0, 3, 2, 1, 4))
    for nm in ("mem_w_kv", "w_out"):
        w = np.asarray(inp[nm]).reshape(L, 16, 128, 16, 128)
        o[nm] = _c(w.transpose(0, 3, 2, 1, 4))
    lnp = np.zeros((L, 3, 2, 16, 128), np.float32)
    for k in range(3):
        lnp[:, k, 0] = np.asarray(inp[f"ln{k + 1}_g"]).reshape(L, 16, 128)
        lnp[:, k, 1] = np.asarray(inp[f"ln{k + 1}_b"]).reshape(L, 16, 128)
    o["lnp"] = _c(lnp.transpose(4, 0, 1, 2, 3)).reshape(128, -1)
    cvp = np.zeros((L, 8, CW + 3, 128), np.float32)
    cvp[:, :, :CW] = np.asarray(inp["conv_dw_w"]).reshape(L, CW, 8, 128).transpose(0, 2, 1, 3)
    cvp[:, :, CW] = np.asarray(inp["conv_dw_b"]).reshape(L, 8, 128)
    cvp[:, :, CW + 1] = np.asarray(inp["conv_ln_g"]).reshape(L, 8, 128)
    cvp[:, :, CW + 2] = np.asarray(inp["conv_ln_b"]).reshape(L, 8, 128)
    o["cvp"] = _c(cvp.transpose(3, 0, 1, 2)).reshape(128, -1)
    o["sinkb"] = _c(np.broadcast_to(np.asarray(inp["win_sink"]).reshape(1, L * 8), (128, L * 8)))
    o.update(host_consts())
    return o


_NC_CACHE = {}


def kernel(**inputs):
    x = np.asarray(inputs["x"], dtype=np.float32)
    mem = np.asarray(inputs["mem"], dtype=np.float32)
    nb = x.shape[0]
    shared = host_weights(inputs)
    in_maps = []
    for b in range(nb):
        m = dict(shared)
        m["xT"] = _c(x[b].T)
        m["memT"] = _c(mem[b].T)
        in_maps.append(m)
    if "nc" not in _NC_CACHE:
        _NC_CACHE["nc"] = build()
    nc = _NC_CACHE["nc"]
    in_maps = [{k: v for k, v in m.items() if k in nc._used_inputs} for m in in_maps]
    res = run_bass_kernel_spmd(nc, in_maps, core_ids=list(range(nb)))
    out = np.stack([np.ascontiguousarray(r["outT"].T) for r in res.results], 0)
    return out.astype(np.float32)
```

```python
import contextlib
import numpy as np
import concourse.bass as bass
import concourse.mybir as mybir
from concourse.bass_utils import run_bass_kernel_spmd

F32 = mybir.dt.float32
BF16 = mybir.dt.bfloat16
U8 = mybir.dt.uint8
AF = mybir.ActivationFunctionType
ALU = mybir.AluOpType
AX = mybir.AxisListType

D = 2048
T = 2048
L = 2
NMEM = 256
DFF = 5632
FC = DFF // 128
CONV_CH = 1024
CW = 31
IN_WIDTH = 10752
NCH_IN = IN_WIDTH // 128
ALPHA = float((2 * L) ** 0.25)
EPS = 1e-5
NEG = -1e30
ENGS = ["pe", "act", "dve", "pool", "sp"]


class Buf:
    __slots__ = ("name", "w", "r")

    def __init__(self, name=""):
        self.name = name
        self.w = None
        self.r = {}


class Prog:
    def __init__(self, nc):
        self.nc = nc
        self.q = {e: [] for e in ENGS}
        self.cnt = {}
        self.seen = {e: {} for e in ENGS}
        self.semnames = []
        self.n_wait = 0
        self.n_ins = 0
        for e in ENGS:
            self._sem("eng_" + e)

    def _sem(self, key):
        if key not in self.cnt:
            self.cnt[key] = 0
            self.semnames.append(key)
        return key

    def fresh(self, name=""):
        b = Buf(name)
        b.r = {k: v for k, v in self.cnt.items() if v > 0}
        return b

    def _deps(self, eng, reads, writes):
        deps = {}

        def add(k, v):
            if deps.get(k, 0) < v:
                deps[k] = v
        for b in reads:
            if b.w is not None:
                add(*b.w)
        for b in writes:
            if b.w is not None:
                add(*b.w)
            for k, v in b.r.items():
                add(k, v)
        waits = []
        own = "eng_" + eng
        seen = self.seen[eng]
        for k, v in deps.items():
            if eng == "pe" and k == own:
                continue
            if seen.get(k, 0) < v:
                seen[k] = v
                waits.append((k, v))
        return waits

    def _commit(self, ev, reads, writes):
        k, v = ev
        for b in reads:
            if b.r.get(k, 0) < v:
                b.r[k] = v
        for b in writes:
            b.w = ev
            b.r = {}

    def op(self, eng, fns, reads=(), writes=()):
        if callable(fns):
            fns = [fns]
        waits = self._deps(eng, reads, writes)
        key = "eng_" + eng
        self.cnt[key] += 1
        ev = (key, self.cnt[key])
        self.q[eng].append(("op", waits, fns, key))
        self._commit(ev, reads, writes)
        self.n_wait += len(waits)
        self.n_ins += len(fns)
        return ev

    def dma(self, eng, pairs, semkey, reads=(), writes=(), **kw):
        if isinstance(pairs, tuple):
            pairs = [pairs]
        self._sem(semkey)
        waits = self._deps(eng, reads, writes)
        self.cnt[semkey] += 16 * len(pairs)
        ev = (semkey, self.cnt[semkey])
        self.q[eng].append(("dma", waits, (pairs, kw), semkey))
        self._commit(ev, reads, writes)
        self.n_wait += len(waits)
        self.n_ins += len(pairs)
        return ev

    def wait_all(self, eng, bufs):
        waits = self._deps(eng, bufs, ())
        self.q[eng].append(("wait", waits, None, None))

    def emit(self):
        nc = self.nc
        with contextlib.ExitStack() as st:
            sems = {}
            for k in self.semnames:
                sems[k] = st.enter_context(nc.semaphore(k))
            block = st.enter_context(nc.Block())

            def run(e, items):
                for kind, waits, payload, key in items:
                    for (k, v) in waits:
                        e.wait_ge(sems[k], v)
                    if kind == "op":
                        ins = None
                        for f in payload:
                            ins = f(e)
                        ins.then_inc(sems[key], 1)
                    elif kind == "dma":
                        pairs, kw = payload
                        for (o, i) in pairs:
                            e.dma_start(out=o, in_=i, **kw).then_inc(sems[key], 16)

            @block.tensor
            def _(e):
                run(e, self.q["pe"])

            @block.scalar
            def _(e):
                run(e, self.q["act"])

            @block.vector
            def _(e):
                run(e, self.q["dve"])

            @block.gpsimd
            def _(e):
                run(e, self.q["pool"])

            @block.sync
            def _(e):
                run(e, self.q["sp"])


class Tile:
    __slots__ = ("ap", "buf", "sem", "pinned")

    def __init__(self, ap, buf, sem):
        self.ap = ap
        self.buf = buf
        self.sem = sem
        self.pinned = False


class Ring:
    def __init__(self, tiles):
        self.tiles = tiles
        self.i = 0

    def get(self, pin=False):
        for _ in range(2 * len(self.tiles)):
            t = self.tiles[self.i % len(self.tiles)]
            self.i += 1
            if not t.pinned:
                t.pinned = pin
                return t
        raise RuntimeError("ring exhausted (all tiles pinned)")


def build(depth=L, stop=None, dbg=False):
    nc = bass.Bass("TRN2", target_bir_lowering=False)
    P = Prog(nc)

    def din(name, shape, dt=F32):
        return nc.dram_tensor(name, list(shape), dt, kind="ExternalInput").ap()

    def dscr(name, shape, dt=F32, out=False):
        if out:
            return nc.dram_tensor(name, list(shape), dt, kind="ExternalOutput").ap()
        return nc.dram_tensor(name, list(shape), dt).ap()

    _ins = {}

    def lazy(name, shape):
        def get():
            if name not in _ins:
                _ins[name] = din(name, shape)
            return _ins[name]
        return get
    xT = din("xT", [D, T]); _ins["xT"] = xT
    g_memT = lazy("memT", [D, NMEM])
    g_w_up = [lazy(f"ffn{i}_w_up", [L, FC, 128, 16, 256]) for i in (1, 2)]
    g_w_dn = [lazy(f"ffn{i}_w_down", [L, 16, 128, FC, 128]) for i in (1, 2)]
    g_w_in = lazy("w_in", [L, NCH_IN, 128, 16, 128])
    g_w_co = lazy("conv_w_out", [L, 16, 128, 8, 128])
    g_w_wo = lazy("win_w_o", [L, 16, 128, 8, 128])
    g_w_mo = lazy("mem_w_o", [L, 16, 128, 8, 128])
    g_w_kv = lazy("mem_w_kv", [L, 16, 128, 16, 128])
    g_w_out = lazy("w_out", [L, 16, 128, 16, 128])
    lnp = din("lnp", [128, L * 3 * 2 * 16]); _ins["lnp"] = lnp
    cvp = din("cvp", [128, L * 8 * (CW + 3)]); _ins["cvp"] = cvp
    sinkb = din("sinkb", [128, L * 8]); _ins["sinkb"] = sinkb
    g_ropeC = lazy("ropeC", [32, T])
    g_ropeS = lazy("ropeS", [32, T])
    g_mask = lazy("mask", [128, 384])
    identd = din("ident", [128, 128]); _ins["ident"] = identd

    last_dbg = dbg
    outT = dscr("outT", [D, T], out=True)
    h32a = dscr("h32a", [D, T], out=last_dbg)
    zd = dscr("zd", [D, T])
    hc_d = dscr("hc_d", [CONV_CH, T], BF16)
    qkb_d = dscr("qkb_d", [1280, T], BF16)
    v_tok = dscr("v_tok", [T, 256], BF16)
    qm_d = dscr("qm_d", [1024, T], BF16)
    conv_d = dscr("conv_d", [1024, T], BF16, out=last_dbg)
    attn_d = dscr("attn_d", [1024, T], BF16, out=last_dbg)
    memo_d = dscr("memo_d", [1024, T], BF16, out=last_dbg)

    fm = lambda ap: ap.rearrange("(c p) t -> p c t", p=128)
    h32a_v, zd_v, outT_v, xT_v = fm(h32a), fm(zd), fm(outT), fm(xT)
    hc_v, qk_v, qm_v, conv_v, attn_v, memo_v = fm(hc_d), fm(qkb_d), fm(qm_d), fm(conv_d), fm(attn_d), fm(memo_d)

    def grid(n, m, nm):
        return [[Buf(f"{nm}{i}_{j}") for j in range(m)] for i in range(n)]
    b_h32a = grid(16, 4, "h32a")
    b_zd = grid(16, 4, "zd")
    b_hc = grid(8, 4, "hc")
    b_qk = grid(10, 4, "qk")
    b_vtok = [Buf() for _ in range(16)]
    b_qm = grid(8, 4, "qm")
    b_conv = grid(8, 4, "convd")
    b_attn = [Buf() for _ in range(8)]
    b_memo = [Buf() for _ in range(4)]
    b_out = Buf()
    b_outs = {}

    ARENA = 212800
    arena = nc.alloc_sbuf_tensor("arena", [128, ARENA], U8)
    off = [0]

    def carve(nbytes, at=None):
        o = off[0] if at is None else at
        o = (o + 31) // 32 * 32
        if at is None:
            off[0] = o + nbytes
        assert o + nbytes <= ARENA, (o, nbytes)
        return o

    def view(o, shape, dt):
        esz = 2 if dt == BF16 else 4
        n = int(np.prod(shape)) * esz
        ap = arena[:, o:o + n].bitcast(dt)
        if len(shape) == 2:
            ap = ap.rearrange("p (a b) -> p a b", b=shape[1])
        elif len(shape) == 3:
            ap = ap.rearrange("p (a b c) -> p a b c", b=shape[1], c=shape[2])
        return ap

    XT = view(carve(16 * T * 2), [16, T], BF16)
    b_XT = [Buf(f"XT{i}") for i in range(4)]
    WSB = 11264
    ws_off = [carve(WSB), carve(WSB)]
    b_ws = [Buf("ws0"), Buf("ws1")]
    wsi = [0]
    c_off = carve(8192)
    co = [c_off]

    def ccarve(n):
        o = (co[0] + 31) // 32 * 32
        co[0] = o + n
        assert co[0] <= c_off + 8192
        return o
    ident32 = view(ccarve(512), [128], F32)
    identb = view(ccarve(256), [128], BF16)
    onesb = view(ccarve(256), [128], BF16)
    lnp_sb = view(ccarve(L * 3 * 2 * 16 * 4), [L * 3 * 2 * 16], F32)
    lnpa_sb = view(ccarve(L * 3 * 2 * 16 * 4), [L * 3 * 2 * 16], F32)
    cvp_sb = view(ccarve(L * 8 * (CW + 3) * 4), [L * 8 * (CW + 3)], F32)
    sink_sb = view(ccarve(L * 8 * 4), [L * 8], F32)
    small = view(ccarve(64 * 4), [64], F32)
    mask_sb = view(ccarve(1536), [384], F32)
    b_const = Buf("const")
    AT_BYTES = FC * 1024 * 2
    at_off = carve(AT_BYTES)
    A_ts = [view(carve(2048), [512], F32) for _ in range(2)]
    B_ts = [view(carve(2048), [512], F32) for _ in range(2)]
    b_As, b_Bs = [Buf("A0"), Buf("A1")], [Buf("B0"), Buf("B1")]
    NR = 7
    ring = Ring([Tile(view(carve(2048), [512], F32), Buf(f"r{i}"), f"r{i}") for i in range(NR)])
    NB = 3
    bring = Ring([Tile(view(carve(1024), [512], BF16), Buf(f"b{i}"), f"b{i}") for i in range(NB)])
    ps = [nc.alloc_psum_tensor(f"ps{i}", [128, 512], F32) for i in range(8)]
    b_ps = [Buf(f"ps{i}") for i in range(8)]
    bank_ring = [list(range(8))]
    bank_i = [0]

    def bank():
        r = bank_ring[0]
        b = r[bank_i[0] % len(r)]
        bank_i[0] += 1
        return b

    def wslot():
        s = wsi[0] % 2
        wsi[0] += 1
        return s

    def ws_view(sl, shape, byte_off=0):
        return view(ws_off[sl] + byte_off, shape, BF16)

    def load_w(sl, pieces):
        P.dma("pool", pieces, f"w{sl}", writes=[b_ws[sl]])

    def mm(b, lhs_fn, rhs_fn, kc_n, reads, ncols=512, extra_writes=()):
        fns = []
        for kc in range(kc_n):
            lh, rh = lhs_fn(kc), rhs_fn(kc)
            fns.append(lambda e, kc=kc, lh=lh, rh=rh: e.matmul(ps[b][:, 0:ncols], lhsT=lh, rhs=rh,
                                                               start=(kc == 0), stop=(kc == kc_n - 1)))
        P.op("pe", fns, reads=reads, writes=[b_ps[b]] + list(extra_writes))

    P.dma("sp", [(ident32, identd), (lnp_sb, lnp), (cvp_sb, cvp), (sink_sb, sinkb), (mask_sb, g_mask())], "cld", writes=[b_const])
    P.op("dve", lambda e: e.tensor_copy(out=identb, in_=ident32), reads=[b_const], writes=[b_const])
    P.op("dve", lambda e: e.memset(onesb, 1.0), writes=[b_const])
    P.op("dve", lambda e: e.tensor_scalar(out=lnpa_sb, in0=lnp_sb, scalar1=ALPHA, scalar2=None, op0=ALU.mult),
         reads=[b_const], writes=[b_const])

    def lnvec(l, k, gb, c, scaled=False):
        base = ((l * 3 + k) * 2 + gb) * 16 + c
        t = lnpa_sb if scaled else lnp_sb
        return t[:, base:base + 1]

    def cvvec(l, i, j):
        base = (l * 8 + i) * (CW + 3) + j
        return cvp_sb[:, base:base + 1]

    for tt in range(4):
        P.dma("pool", [(XT[:, :, tt * 512:(tt + 1) * 512], xT_v[:, :, tt * 512:(tt + 1) * 512])], f"xld{tt}",
              writes=[b_XT[tt]])

    S1 = [4, 6]
    S2 = [5, 7]

    pending_stats = []

    def flush_stats():
        while pending_stats:
            pending_stats.pop(0)()

    raw_x = [False]

    hres_next = [None]

    def load_hres(c, tok0):
        t = ring.get(pin=True)
        tk = slice(tok0, tok0 + 512)
        if raw_x[0]:
            P.dma("sp", [(t.ap, xT_v[:, c, tk])], t.sem, writes=[t.buf])
        else:
            P.dma("sp", [(t.ap, h32a_v[:, c, tk])], t.sem, reads=[b_h32a[c][tok0 // 512]], writes=[t.buf])
        return t

    def resid_epilogue(b, c, tok0, tt, scale, first, last, nxt=None):
        flush_stats()
        gt = tok0 // 512
        tk = slice(tok0, tok0 + 512)
        if hres_next[0] is not None and hres_next[0][0] == (c, tok0):
            hres = hres_next[0][1]
        else:
            assert hres_next[0] is None
            hres = load_hres(c, tok0)
        hres_next[0] = None
        zt = ring.get()
        if nxt is not None:
            hres_next[0] = (nxt, load_hres(*nxt))
        hres.pinned = False
        if raw_x[0]:
            P.op("dve", lambda e: e.tensor_scalar(out=zt.ap, in0=ps[b][:, :], scalar1=float(scale), scalar2=None, op0=ALU.mult),
                 reads=[b_ps[b]], writes=[zt.buf])
            P.op("dve", lambda e: e.scalar_tensor_tensor(out=zt.ap, in0=hres.ap, scalar=ALPHA, in1=zt.ap,
                                                         op0=ALU.mult, op1=ALU.add),
                 reads=[hres.buf, zt.buf], writes=[zt.buf])
        else:
            P.op("dve", lambda e: e.scalar_tensor_tensor(out=zt.ap, in0=ps[b][:, :], scalar=float(scale), in1=hres.ap,
                                                         op0=ALU.mult, op1=ALU.add),
                 reads=[b_ps[b], hres.buf], writes=[zt.buf])
        P.dma("sp", [(zd_v[:, c, tk], zt.ap)], zt.sem, reads=[zt.buf], writes=[b_zd[c][gt]])
        zb = bring.get()
        P.op("act", lambda e: e.activation(out=zb.ap, in_=zt.ap, func=AF.Copy), reads=[zt.buf], writes=[zb.buf])
        zq = bring.get()
        P.op("act", lambda e: e.activation(out=zq.ap, in_=zt.ap, func=AF.Square), reads=[zt.buf], writes=[zq.buf])
        pending_stats.append(lambda: P.op(
            "pe", [lambda e: e.matmul(ps[S1[tt]][:, :], lhsT=onesb, rhs=zb.ap, start=first, stop=last),
                   lambda e: e.matmul(ps[S2[tt]][:, :], lhsT=onesb, rhs=zq.ap, start=first, stop=last)],
            reads=[zb.buf, zq.buf, b_const], writes=[b_ps[S1[tt]], b_ps[S2[tt]]]))

    def stats_to_AB(s1b, s2b, dn, ai=0):
        A_t, B_t, b_A, b_B = A_ts[ai], B_ts[ai], b_As[ai], b_Bs[ai]
        m = ring.get()
        P.op("dve", lambda e: e.tensor_scalar(out=m.ap, in0=ps[s1b][:, :], scalar1=1.0 / dn, scalar2=None, op0=ALU.mult),
             reads=[b_ps[s1b]], writes=[m.buf])
        v = ring.get()
        P.op("dve", lambda e: e.tensor_tensor(out=v.ap, in0=m.ap, in1=m.ap, op=ALU.mult), reads=[m.buf], writes=[v.buf])
        P.op("dve", lambda e: e.scalar_tensor_tensor(out=v.ap, in0=ps[s2b][:, :], scalar=1.0 / dn, in1=v.ap,
                                                     op0=ALU.mult, op1=ALU.subtract),
             reads=[b_ps[s2b], v.buf], writes=[v.buf])
        P.op("dve", lambda e: e.tensor_scalar(out=v.ap, in0=v.ap, scalar1=EPS, scalar2=None, op0=ALU.add),
             reads=[v.buf], writes=[v.buf])
        P.op("act", lambda e: e.activation(out=v.ap, in_=v.ap, func=AF.Sqrt), reads=[v.buf], writes=[v.buf])
        P.op("dve", lambda e: e.reciprocal(out=A_t, in_=v.ap), reads=[v.buf], writes=[b_A])
        P.op("dve", lambda e: e.scalar_tensor_tensor(out=B_t, in0=m.ap, scalar=-1.0, in1=A_t, op0=ALU.mult, op1=ALU.mult),
             reads=[m.buf, b_A], writes=[b_B])

    pending_norm = [None]

    def bg_step(n=1):
        g = pending_norm[0]
        if g is None:
            return
        for _ in range(n):
            try:
                next(g)
            except StopIteration:
                pending_norm[0] = None
                return

    def drain_bg():
        while pending_norm[0] is not None:
            bg_step(8)

    def normalize(l, k, t0, ntt, final, pf=3):
        assert pending_norm[0] is None
        flush_stats()
        for tt in range(ntt):
            stats_to_AB(S1[tt], S2[tt], D, tt)

        def gen():
            PF = pf
            tiles = [(tt, c) for tt in range(ntt) for c in range(16)]
            loads = {}

            def issue_load(idx):
                tt, c = tiles[idx]
                tok0 = t0 + tt * 512
                zl = ring.get(pin=True)
                P.dma("sp", [(zl.ap, zd_v[:, c, tok0:tok0 + 512])], zl.sem, reads=[b_zd[c][tok0 // 512]], writes=[zl.buf])
                loads[idx] = zl
            for idx in range(min(PF, len(tiles))):
                issue_load(idx)
            for idx, (tt, c) in enumerate(tiles):
                tok0 = t0 + tt * 512
                gt = tok0 // 512
                tk = slice(tok0, tok0 + 512)
                A_t, B_t, b_A, b_B = A_ts[tt], B_ts[tt], b_As[tt], b_Bs[tt]
                if idx + PF < len(tiles):
                    issue_load(idx + PF)
                zl = loads.pop(idx)
                P.op("dve", lambda e, zl=zl, A_t=A_t: e.tensor_tensor(out=zl.ap, in0=zl.ap, in1=A_t, op=ALU.mult),
                     reads=[zl.buf, b_A], writes=[zl.buf])
                P.op("dve", lambda e, zl=zl, B_t=B_t: e.tensor_tensor(out=zl.ap, in0=zl.ap, in1=B_t, op=ALU.add),
                     reads=[zl.buf, b_B], writes=[zl.buf])
                if final:
                    ho = ring.get()
                    P.op("act", lambda e, zl=zl, ho=ho, c=c: e.activation(
                        out=ho.ap, in_=zl.ap, func=AF.Identity, scale=lnvec(l, k, 0, c), bias=lnvec(l, k, 1, c)),
                        reads=[zl.buf, b_const], writes=[ho.buf])
                    P.dma("sp", [(outT_v[:, c, tk], ho.ap)], ho.sem, reads=[ho.buf],
                          writes=[b_outs.setdefault(ho.sem, Buf("out_" + ho.sem))])
                else:
                    P.op("act", lambda e, zl=zl, c=c, tk=tk: e.activation(
                        out=XT[:, c, tk], in_=zl.ap, func=AF.Identity, scale=lnvec(l, k, 0, c), bias=lnvec(l, k, 1, c)),
                        reads=[zl.buf, b_const], writes=[b_XT[gt]])
                    ho = ring.get()
                    P.op("act", lambda e, zl=zl, ho=ho, c=c: e.activation(
                        out=ho.ap, in_=zl.ap, func=AF.Identity, scale=lnvec(l, k, 0, c, True), bias=lnvec(l, k, 1, c, True)),
                        reads=[zl.buf, b_const], writes=[ho.buf])
                    P.dma("sp", [(h32a_v[:, c, tk], ho.ap)], ho.sem, reads=[ho.buf], writes=[b_h32a[c][gt]])
                zl.pinned = False
                yield
        pending_norm[0] = gen()

    def ffn(l, which, final):
        wu = g_w_up[which]()[l]
        wd = g_w_dn[which]()[l]
        AT = view(at_off, [FC, 1024], BF16)
        for th in range(2):
            t0 = th * 1024
            b_AT = [P.fresh("AT0"), P.fresh("AT1")]
            bank_ring[0] = list(range(8))
            for f in range(FC):
                sl = wslot()
                wv = ws_view(sl, [16, 256])
                load_w(sl, [(wv[:, 0:8, :], wu[f][:, 0:8, :]), (wv[:, 8:16, :], wu[f][:, 8:16, :])])
                for tt in range(2):
                    gt = th * 2 + tt
                    tk = slice(t0 + tt * 512, t0 + (tt + 1) * 512)
                    bg, bu = bank(), bank()
                    mm(bg, lambda kc: wv[:, kc, 0:128], lambda kc: XT[:, kc, tk], 16, [b_ws[sl], b_XT[gt]])
                    mm(bu, lambda kc: wv[:, kc, 128:256], lambda kc: XT[:, kc, tk], 16, [b_ws[sl], b_XT[gt]])
                    sg = ring.get()
                    P.op("act", lambda e, sg=sg, bg=bg: e.activation(out=sg.ap, in_=ps[bg][:, :], func=AF.Silu),
                         reads=[b_ps[bg]], writes=[sg.buf])
                    P.op("dve", lambda e, sg=sg, bu=bu, f=f, tt=tt: e.tensor_tensor(
                        out=AT[:, f, tt * 512:(tt + 1) * 512], in0=sg.ap, in1=ps[bu][:, :], op=ALU.mult),
                        reads=[sg.buf, b_ps[bu]], writes=[b_AT[tt]])
                    bg_step()
            drain_bg()
            bank_ring[0] = [0, 1, 2, 3]
            for n in range(16):
                sl = wslot()
                wv = ws_view(sl, [FC, 128])
                load_w(sl, [(wv[:, 0:16, :], wd[n][:, 0:16, :]), (wv[:, 16:32, :], wd[n][:, 16:32, :]),
                            (wv[:, 32:44, :], wd[n][:, 32:44, :])])
                for tt in range(2):
                    b = bank()
                    mm(b, lambda fc: wv[:, fc, :], lambda fc: AT[:, fc, tt * 512:(tt + 1) * 512], FC,
                       [b_ws[sl], b_AT[tt]])
                    raw_x[0] = (l == 0 and which == 0)
                    nxt = (n, t0 + 512) if tt == 0 else ((n + 1, t0) if n < 15 else None)
                    resid_epilogue(b, n, t0 + tt * 512, tt, 0.5, n == 0, n == 15, nxt)
                    raw_x[0] = False
            normalize(l, 0 if which == 0 else 2, t0, 2, final)

    b_small = [Buf(f"sm{i}") for i in range(64)]
    small_i = [0]

    def col():
        i = small_i[0] % 64
        small_i[0] += 1
        return small[:, i:i + 1], b_small[i]

    def load_chunks(srcs, kc):
        sl = wslot()
        views = [ws_view(sl, [kc, 128], g * kc * 256) for g in range(len(srcs))]
        load_w(sl, [(views[g], srcs[g]) for g in range(len(srcs))])
        return sl, views

    def run_rr(gens):
        gens = list(gens)
        while gens:
            for g in list(gens):
                try:
                    next(g)
                except StopIteration:
                    gens.remove(g)

    def attn_pipeline(items, s_ring, p_ring, pT_ring, K, w_out_cols, evac):
        st = [dict() for _ in items]

        def stage_a(it, d):
            nk = it["nk"]
            b = bank()
            it["qk"](b)
            yield
            s = s_ring.get()
            sa = s.ap[:, 0:nk]
            if it["mask"] is not None:
                mk = it["mask"]
                P.op("dve", lambda e: e.scalar_tensor_tensor(out=sa, in0=ps[b][:, 0:nk], scalar=float(it["scale"]), in1=mk,
                                                             op0=ALU.mult, op1=ALU.add),
                     reads=[b_ps[b], it["mask_buf"]], writes=[s.buf])
            else:
                P.op("dve", lambda e: e.tensor_scalar(out=sa, in0=ps[b][:, 0:nk], scalar1=float(it["scale"]), scalar2=None,
                                                      op0=ALU.mult), reads=[b_ps[b]], writes=[s.buf])
            yield
            mx, b_mx = col()
            P.op("dve", lambda e: e.reduce_max(out=mx, in_=sa, axis=AX.X), reads=[s.buf], writes=[b_mx])
            yield
            if it["sink"] is not None:
                P.op("dve", lambda e: e.tensor_tensor(out=mx, in0=mx, in1=it["sink"], op=ALU.max),
                     reads=[b_mx, b_const], writes=[b_mx])
                yield
            nm, b_nm = col()
            rs, b_rs = col()
            P.op("dve", [lambda e: e.memset(rs, 0.0),
                         lambda e: e.tensor_scalar(out=nm, in0=mx, scalar1=-1.0, scalar2=None, op0=ALU.mult)],
                 reads=[b_mx], writes=[b_nm, b_rs])
            yield
            P.op("act", lambda e: e.activation(out=sa, in_=sa, func=AF.Exp, bias=nm, accum_out=rs),
                 reads=[s.buf, b_nm, b_rs], writes=[s.buf, b_rs])
            if it["sink"] is not None:
                es, b_es = col()
                P.op("act", lambda e: e.activation(out=es, in_=nm, func=AF.Exp, bias=it["sink"]),
                     reads=[b_nm, b_const], writes=[b_es])
                yield
                P.op("dve", lambda e: e.tensor_tensor(out=rs, in0=rs, in1=es, op=ALU.add), reads=[b_rs, b_es], writes=[b_rs])
            yield
            P.op("dve", lambda e: e.reciprocal(out=rs, in_=rs), reads=[b_rs], writes=[b_rs])
            yield
            pb = p_ring.get()
            P.op("dve", lambda e: e.tensor_scalar(out=pb.ap[:, 0:nk], in0=sa, scalar1=rs, scalar2=None, op0=ALU.mult),
                 reads=[s.buf, b_rs], writes=[pb.buf])
            d["pb"] = pb
            yield

        def stage_b(it, d):
            nk = it["nk"]
            pb = d["pb"]
            bt = bank()
            pst = ps[bt][:, :].bitcast(BF16)
            fns = []
            for kb in range(nk // 128):
                fns.append(lambda e, kb=kb: e.transpose(out=pst[:, kb * 128:(kb + 1) * 128],
                                                        in_=pb.ap[:, kb * 128:(kb + 1) * 128], identity=identb))
            P.op("pe", fns, reads=[pb.buf, b_const], writes=[b_ps[bt]])
            yield
            pT = pT_ring.get()
            P.op("act", lambda e: e.activation(out=pT.ap[:, 0:nk], in_=pst[:, 0:nk], func=AF.Copy),
                 reads=[b_ps[bt]], writes=[pT.buf])
            d["pT"] = pT
            yield

        def stage_c(i0, n_it):
            per = 512 // w_out_cols
            for j0 in range(0, n_it, per):
                bo = bank()
                cnt = min(per, n_it - j0)
                for j in range(cnt):
                    d = st[i0 + j0 + j]
                    items[i0 + j0 + j]["pv"](bo, j, cnt, d["pT"].ap, d["pT"].buf)
                evac(bo, i0 + j0, cnt)

        def stage_a_group(idxs):
            k = len(idxs)
            blk16 = (small_i[0] % 4) * 16
            small_i[0] += 1
            mxs, nms, rss, ess = [small[:, blk16 + q * 4: blk16 + q * 4 + k] for q in range(4)]
            bmx = [b_small[blk16 + j] for j in range(k)]
            bnm = [b_small[blk16 + 4 + j] for j in range(k)]
            brs = [b_small[blk16 + 8 + j] for j in range(k)]
            bes = [b_small[blk16 + 12 + j] for j in range(k)]
            its = [items[i] for i in idxs]
            sink = its[0]["sink"]
            banks, ss = [], []
            for j, it in enumerate(its):
                b = bank()
                it["qk"](b)
                banks.append(b)
            for j, it in enumerate(its):
                nk = it["nk"]
                b = banks[j]
                s_ = s_ring.get()
                ss.append(s_)
                sa = s_.ap[:, 0:nk]
                if it["mask"] is not None:
                    P.op("dve", lambda e, sa=sa, b=b, nk=nk, it=it: e.scalar_tensor_tensor(
                        out=sa, in0=ps[b][:, 0:nk], scalar=float(it["scale"]), in1=it["mask"], op0=ALU.mult, op1=ALU.add),
                        reads=[b_ps[b], it["mask_buf"]], writes=[s_.buf])
                else:
                    P.op("dve", lambda e, sa=sa, b=b, nk=nk, it=it: e.tensor_scalar(
                        out=sa, in0=ps[b][:, 0:nk], scalar1=float(it["scale"]), scalar2=None, op0=ALU.mult),
                        reads=[b_ps[b]], writes=[s_.buf])
            for j, it in enumerate(its):
                sa = ss[j].ap[:, 0:it["nk"]]
                P.op("dve", lambda e, sa=sa, j=j: e.reduce_max(out=mxs[:, j:j + 1], in_=sa, axis=AX.X),
                     reads=[ss[j].buf], writes=[bmx[j]])
            if sink is not None:
                P.op("dve", lambda e: e.tensor_scalar(out=mxs, in0=mxs, scalar1=sink, scalar2=None, op0=ALU.max),
                     reads=bmx + [b_const], writes=bmx)
            P.op("dve", [lambda e: e.memset(rss, 0.0),
                         lambda e: e.tensor_scalar(out=nms, in0=mxs, scalar1=-1.0, scalar2=None, op0=ALU.mult)],
                 reads=bmx, writes=bnm + brs)
            for j, it in enumerate(its):
                sa = ss[j].ap[:, 0:it["nk"]]
                P.op("act", lambda e, sa=sa, j=j: e.activation(out=sa, in_=sa, func=AF.Exp, bias=nms[:, j:j + 1],
                                                               accum_out=rss[:, j:j + 1]),
                     reads=[ss[j].buf, bnm[j], brs[j]], writes=[ss[j].buf, brs[j]])
            if sink is not None:
                P.op("act", lambda e: e.activation(out=ess, in_=nms, func=AF.Exp, bias=sink), reads=bnm + [b_const], writes=bes)
            return (idxs, its, ss, rss, ess, brs, bes, sink)

        def stage_a2(state):
            idxs, its, ss, rss, ess, brs, bes, sink = state
            if sink is not None:
                P.op("dve", lambda e: e.tensor_tensor(out=rss, in0=rss, in1=ess, op=ALU.add), reads=brs + bes, writes=brs)
            P.op("dve", lambda e: e.reciprocal(out=rss, in_=rss), reads=brs, writes=brs)
            for j, it in enumerate(its):
                nk = it["nk"]
                sa = ss[j].ap[:, 0:nk]
                pb = p_ring.get()
                P.op("dve", lambda e, sa=sa, pb=pb, nk=nk, j=j: e.tensor_scalar(
                    out=pb.ap[:, 0:nk], in0=sa, scalar1=rss[:, j:j + 1], scalar2=None, op0=ALU.mult),
                    reads=[ss[j].buf, brs[j]], writes=[pb.buf])
                st[idxs[j]]["pb"] = pb

        n = len(items)
        groups = [(i, min(K, n - i)) for i in range(0, n, K)]
        G = len(groups)
        a_state = {}
        for step in range(G + 3):
            if step < G:
                i0, c = groups[step]
                a_state[step] = stage_a_group(list(range(i0, i0 + c)))
            if 0 <= step - 1 < G:
                stage_a2(a_state.pop(step - 1))
            if 0 <= step - 2 < G:
                i0, c = groups[step - 2]
                run_rr(stage_b(items[i], st[i]) for i in range(i0, i0 + c))
            if 0 <= step - 3 < G:
                i0, c = groups[step - 3]
                stage_c(i0, c)

    def mixer(l, final):
        win = g_w_in()[l]
        wkv = g_w_kv()[l]
        memT_v = fm(g_memT())
        reg = [at_off]

        def lcarve(n):
            o = (reg[0] + 31) // 32 * 32
            reg[0] = o + n
            assert reg[0] <= at_off + AT_BYTES, (reg[0] - at_off, AT_BYTES)
            return o
        bank_ring[0] = list(range(8))
        memTb = view(lcarve(16 * 256 * 2), [16, 256], BF16)
        b_memT = P.fresh()
        mkT = view(lcarve(8 * 256 * 2), [8, 256], BF16)
        b_mk = P.fresh()
        mvt = view(lcarve(2 * 1024 * 2), [2, 1024], BF16)
        b_mv = P.fresh()
        m0_end = reg[0]
        P.dma("pool", [(memTb[:, 0:8, :], memT_v[:, 0:8, :]), (memTb[:, 8:16, :], memT_v[:, 8:16, :])], "memld",
              writes=[b_memT])
        for pr in range(4):
            sl, vs = load_chunks([wkv[2 * pr], wkv[2 * pr + 1]], 16)
            for g in range(2):
                b = bank()
                mm(b, lambda kc: vs[g][:, kc, :], lambda kc: memTb[:, kc, :], 16, [b_ws[sl], b_memT], ncols=256)
                P.op("act", lambda e, b=b, ch=2 * pr + g: e.activation(out=mkT[:, ch, :], in_=ps[b][:, 0:256], func=AF.Copy),
                     reads=[b_ps[b]], writes=[b_mk])
                bg_step(1)
        for ct in range(4):
            sl = wslot()
            wv = ws_view(sl, [16, 256])
            load_w(sl, [(wv[:, :, 0:128], wkv[8 + 2 * ct]), (wv[:, :, 128:256], wkv[9 + 2 * ct])])
            for mb in range(2):
                b = bank()
                mm(b, lambda kc: memTb[:, kc, mb * 128:(mb + 1) * 128], lambda kc: wv[:, kc, :], 16,
                   [b_ws[sl], b_memT], ncols=256)
                P.op("act", lambda e, b=b, mb=mb, ct=ct: e.activation(out=mvt[:, mb, ct * 256:(ct + 1) * 256],
                                                                      in_=ps[b][:, 0:256], func=AF.Copy),
                     reads=[b_ps[b]], writes=[b_mv])
                bg_step(1)
        rC = view(lcarve(8192), [T], F32)
        rS = view(lcarve(8192), [T], F32)
        b_rt = P.fresh()
        P.dma("sp", [(rC[0:32, :], g_ropeC()), (rS[0:32, :], g_ropeS())], "ld_rt", writes=[b_rt])
        def conv_pair(i, tts):
            sl, (va, vg) = load_chunks([win[i], win[8 + i]], 16)
            for tt in tts:
                tk = slice(tt * 512, (tt + 1) * 512)
                ba, bg = bank(), bank()
                mm(ba, lambda kc: va[:, kc, :], lambda kc: XT[:, kc, tk], 16, [b_ws[sl], b_XT[tt]])
                mm(bg, lambda kc: vg[:, kc, :], lambda kc: XT[:, kc, tk], 16, [b_ws[sl], b_XT[tt]])
                sg = ring.get()
                P.op("act", lambda e, sg=sg, bg=bg: e.activation(out=sg.ap, in_=ps[bg][:, :], func=AF.Sigmoid),
                     reads=[b_ps[bg]], writes=[sg.buf])
                hb = bring.get()
                P.op("dve", lambda e, sg=sg, ba=ba, hb=hb: e.tensor_tensor(out=hb.ap, in0=sg.ap, in1=ps[ba][:, :], op=ALU.mult),
                     reads=[sg.buf, b_ps[ba]], writes=[hb.buf])
                P.dma("sp", [(hc_v[:, i, tk], hb.ap)], hb.sem, reads=[hb.buf], writes=[b_hc[i][tt]])
                bg_step(2)
        for i in range(4):
            conv_pair(i, [0, 1])
        drain_bg()
        for i in range(4, 8):
            conv_pair(i, [0, 1, 2, 3])
        for i in range(4):
            conv_pair(i, [2, 3])
        qk_pending = []
        qk_chunks = [(16 + 2 * hp, 17 + 2 * hp, 2 * hp) for hp in range(4)] + [(24, 25, 8)]
        for (c0, c1, row0) in qk_chunks:
            sl, vs = load_chunks([win[c0], win[c1]], 16)
            for tt in range(4):
                tk = slice(tt * 512, (tt + 1) * 512)
                for g in range(2):
                    b = bank()
                    mm(b, lambda kc: vs[g][:, kc, :], lambda kc: XT[:, kc, tk], 16, [b_ws[sl], b_XT[tt]])
                    r = ring.get()
                    P.op("act", lambda e, r=r, b=b: e.activation(out=r.ap, in_=ps[b][:, :], func=AF.Copy),
                         reads=[b_ps[b]], writes=[r.buf])
                    ro = ring.get()
                    P.dma("sp", [(ro.ap[0:16, :], r.ap[16:32, :]), (ro.ap[16:32, :], r.ap[0:16, :])], ro.sem,
                          reads=[r.buf], writes=[ro.buf])
                    if qk_pending:
                        qk_pending.pop(0)()
                    P.op("dve", lambda e, ro=ro, tk=tk: e.tensor_tensor(out=ro.ap[0:32, :], in0=ro.ap[0:32, :], in1=rS[0:32, tk], op=ALU.mult),
                         reads=[ro.buf, b_rt], writes=[ro.buf])
                    P.op("dve", lambda e, r=r, tk=tk: e.tensor_tensor(out=r.ap[0:32, :], in0=r.ap[0:32, :], in1=rC[0:32, tk], op=ALU.mult),
                         reads=[r.buf, b_rt], writes=[r.buf])
                    P.op("dve", lambda e, r=r, ro=ro: e.tensor_tensor(out=r.ap[0:32, :], in0=r.ap[0:32, :], in1=ro.ap[0:32, :], op=ALU.add),
                         reads=[r.buf, ro.buf], writes=[r.buf])
                    qb_ = bring.get()
                    P.op("dve", lambda e, r=r, qb_=qb_: e.tensor_copy(out=qb_.ap, in_=r.ap),
                         reads=[r.buf], writes=[qb_.buf])
                    qk_pending.append(lambda qb_=qb_, row=row0 + g, tk=tk, tt=tt: P.dma(
                        "sp", [(qk_v[:, row, tk], qb_.ap)], qb_.sem, reads=[qb_.buf], writes=[b_qk[row][tt]]))
        while qk_pending:
            qk_pending.pop(0)()
        sl = wslot()
        wv = ws_view(sl, [16, 256])
        load_w(sl, [(wv[:, :, 0:128], win[26]), (wv[:, :, 128:256], win[27])])
        for tb in range(16):
            b = bank()
            mm(b, lambda kc: XT[:, kc, tb * 128:(tb + 1) * 128], lambda kc: wv[:, kc, :], 16, [b_ws[sl], b_XT[tb // 4]],
               ncols=256)
            hb = bring.get()
            P.op("act", lambda e, hb=hb, b=b: e.activation(out=hb.ap[:, 0:256], in_=ps[b][:, 0:256], func=AF.Copy),
                 reads=[b_ps[b]], writes=[hb.buf])
            P.dma("sp", [(v_tok[tb * 128:(tb + 1) * 128, :], hb.ap[:, 0:256])], hb.sem, reads=[hb.buf], writes=[b_vtok[tb]])
        for hp in range(4):
            sl, vs = load_chunks([win[28 + 2 * hp], win[29 + 2 * hp]], 16)
            for tt in range(4):
                tk = slice(tt * 512, (tt + 1) * 512)
                for g in range(2):
                    b = bank()
                    mm(b, lambda kc: vs[g][:, kc, :], lambda kc: XT[:, kc, tk], 16, [b_ws[sl], b_XT[tt]])
                    hb = bring.get()
                    P.op("act", lambda e, hb=hb, b=b: e.activation(out=hb.ap, in_=ps[b][:, :], func=AF.Copy),
                         reads=[b_ps[b]], writes=[hb.buf])
                    P.dma("sp", [(qm_v[:, 2 * hp + g, tk], hb.ap)], hb.sem, reads=[hb.buf], writes=[b_qm[2 * hp + g][tt]])

        KI = 4

        def mk_rings():
            s_r = Ring([Tile(view(lcarve(1536), [384], F32), P.fresh(), None) for _ in range(2 * KI)])
            p_r = Ring([Tile(view(lcarve(768), [384], BF16), P.fresh(), None) for _ in range(2 * KI)])
            t_r = Ring([Tile(view(lcarve(768), [384], BF16), P.fresh(), None) for _ in range(2 * KI)])
            return s_r, p_r, t_r
        reg[0] = m0_end
        kTb = view(lcarve(4096), [T], BF16); b_kT = P.fresh()
        vt = view(lcarve(4096), [16, 128], BF16); b_vt = P.fresh()
        qTb = [view(lcarve(4096), [T], BF16) for _ in range(2)]; b_qT = [P.fresh(), P.fresh()]
        ast = [view(lcarve(4096), [T], BF16) for _ in range(2)]; b_ast = [P.fresh(), P.fresh()]
        s_r, p_r, t_r = mk_rings()

        def load_q(h):
            P.dma("sp", [(qTb[h % 2], qk_v[:, h, :])], f"ld_q{h % 2}", reads=b_qk[h], writes=[b_qT[h % 2]])
        load_q(0)
        for h in range(8):
            kvh = h // 4
            if h % 4 == 0:
                P.dma("sp", [(kTb, qk_v[:, 8 + kvh, :])], "ld_k", reads=b_qk[8 + kvh], writes=[b_kT])
                P.dma("sp", [(vt, v_tok.rearrange("(tb p) d -> p tb d", p=128)[:, :, kvh * 128:(kvh + 1) * 128])], "ld_vt",
                      reads=b_vtok, writes=[b_vt])
            qT = qTb[h % 2]
            if h + 1 < 8:
                load_q(h + 1)
            a_st = ast[h % 2]
            sink_ap = sink_sb[:, l * 8 + h:l * 8 + h + 1]
            items = []
            for blk in range(16):
                lo, hi = max(0, blk - 1), min(15, blk + 1)
                nk = (hi - lo + 1) * 128
                m0 = (lo - (blk - 1)) * 128

                def qk(b, blk=blk, lo=lo, hi=hi, nk=nk, qT=qT, h=h):
                    P.op("pe", [lambda e: e.matmul(ps[b][:, 0:nk], lhsT=qT[:, blk * 128:(blk + 1) * 128],
                                                   rhs=kTb[:, lo * 128:(hi + 1) * 128], start=True, stop=True)],
                         reads=[b_qT[h % 2], b_kT], writes=[b_ps[b]])

                def pv(bo, j, cnt, pT, b_pT, blk=blk, lo=lo, nk=nk):
                    fns = []
                    nkb = nk // 128
                    for kb in range(nkb):
                        fns.append(lambda e, kb=kb: e.matmul(ps[bo][:, j * 128:(j + 1) * 128], lhsT=vt[:, lo + kb, :],
                                                             rhs=pT[:, kb * 128:(kb + 1) * 128], start=(kb == 0),
                                                             stop=(kb == nkb - 1)))
                    P.op("pe", fns, reads=[b_vt, b_pT], writes=[b_ps[bo]])
                items.append(dict(qk=qk, nk=nk, mask=mask_sb[:, m0:m0 + nk], mask_buf=b_const, scale=128 ** -0.5,
                                  sink=sink_ap, pv=pv))

            def evac3(bo, i0, cnt, a_st=a_st, h=h):
                P.op("act", lambda e: e.activation(out=a_st[:, i0 * 128:(i0 + cnt) * 128], in_=ps[bo][:, 0:cnt * 128], func=AF.Copy),
                     reads=[b_ps[bo]], writes=[b_ast[h % 2]])
            attn_pipeline(items, s_r, p_r, t_r, KI, 128, evac3)
            P.dma("sp", [(attn_v[:, h, :], a_st)], f"st_ast{h % 2}", reads=[b_ast[h % 2]], writes=[b_attn[h]])

        reg[0] = m0_end
        qmb = [view(lcarve(8192), [2, T], BF16) for _ in range(2)]; b_qmb = [P.fresh(), P.fresh()]
        mst = [view(lcarve(8192), [2, T], BF16) for _ in range(2)]; b_mst = [P.fresh(), P.fresh()]
        s_r, p_r, t_r = mk_rings()
        for mh in range(4):
            qb = qmb[mh % 2]
            ms = mst[mh % 2]
            P.dma("sp", [(qb, qm_v[:, 2 * mh:2 * mh + 2, :])], f"ld_qm{mh % 2}", reads=b_qm[2 * mh] + b_qm[2 * mh + 1],
                  writes=[b_qmb[mh % 2]])
            items = []
            for blk in range(16):
                def qk(b, blk=blk, qb=qb, mh=mh):
                    P.op("pe", [lambda e, dc=dc: e.matmul(ps[b][:, 0:256], lhsT=qb[:, dc, blk * 128:(blk + 1) * 128],
                                                          rhs=mkT[:, 2 * mh + dc, :], start=(dc == 0), stop=(dc == 1))
                                for dc in range(2)],
                         reads=[b_qmb[mh % 2], b_mk], writes=[b_ps[b]])

                def pv(bo, j, cnt, pT, b_pT, blk=blk, mh=mh):
                    fns = []
                    for dc in range(2):
                        for mb in range(2):
                            c0 = dc * 256 + j * 128
                            fns.append(lambda e, dc=dc, mb=mb, c0=c0: e.matmul(
                                ps[bo][:, c0:c0 + 128], lhsT=mvt[:, mb, mh * 256 + dc * 128:mh * 256 + (dc + 1) * 128],
                                rhs=pT[:, mb * 128:(mb + 1) * 128], start=(mb == 0), stop=(mb == 1)))
                    P.op("pe", fns, reads=[b_mv, b_pT], writes=[b_ps[bo]])
                items.append(dict(qk=qk, nk=256, mask=None, mask_buf=None, scale=256 ** -0.5, sink=None, pv=pv))

            def evac4(bo, i0, cnt, ms=ms, mh=mh):
                assert cnt == 2
                P.op("act", lambda e: e.activation(out=ms[:, :, i0 * 128:(i0 + 2) * 128],
                                                   in_=ps[bo][:, 0:512].rearrange("p (a b) -> p a b", a=2), func=AF.Copy),
                     reads=[b_ps[bo]], writes=[b_mst[mh % 2]])
            attn_pipeline(items, s_r, p_r, t_r, KI, 256, evac4)
            P.dma("sp", [(memo_v[:, 2 * mh:2 * mh + 2, :], ms)], f"st_mst{mh % 2}", reads=[b_mst[mh % 2]], writes=[b_memo[mh]])

        reg[0] = at_off
        acc = view(lcarve(8 * T * 4), [8, T], F32)
        b_acc = [P.fresh() for _ in range(4)]
        diag = [view(lcarve(CW * 128 * 2), [CW, 128], BF16) for _ in range(2)]; b_dg = [P.fresh(), P.fresh()]
        hcp = [view(lcarve(2080 * 2), [2080], BF16) for _ in range(2)]; b_hcp = [P.fresh(), P.fresh()]
        for i in range(8):
            hb_, dg = hcp[i % 2], diag[i % 2]
            P.op("dve", [lambda e, hb_=hb_: e.memset(hb_[:, 0:16], 0.0), lambda e, hb_=hb_: e.memset(hb_[:, 2064:2080], 0.0)],
                 writes=[b_hcp[i % 2]])
            P.dma("sp", [(hb_[:, 16:2064], hc_v[:, i, :])], f"ld_hcp{i % 2}", reads=b_hc[i], writes=[b_hcp[i % 2]])
            P.op("dve", [lambda e, j=j, dg=dg, i=i: e.tensor_scalar(out=dg[:, j, :], in0=identb, scalar1=cvvec(l, i, j),
                                                                     scalar2=None, op0=ALU.mult) for j in range(CW)],
                 reads=[b_const], writes=[b_dg[i % 2]])
            for tt in range(4):
                b = bank()
                fns = [lambda e, j=j, dg=dg, hb_=hb_, tt=tt, b=b: e.matmul(
                    ps[b][:, :], lhsT=dg[:, j, :], rhs=hb_[:, tt * 512 + j + 1:tt * 512 + j + 513], start=(j == 0),
                    stop=(j == CW - 1)) for j in range(CW)]
                P.op("pe", fns, reads=[b_dg[i % 2], b_hcp[i % 2]], writes=[b_ps[b]])
                P.op("act", lambda e, b=b, i=i, tt=tt: e.activation(out=acc[:, i, tt * 512:(tt + 1) * 512], in_=ps[b][:, :],
                                                                    func=AF.Identity, bias=cvvec(l, i, CW)),
                     reads=[b_ps[b], b_const], writes=[b_acc[tt]])
        for tt in range(4):
            tk = slice(tt * 512, (tt + 1) * 512)
            s1, s2 = bank(), bank()
            for i in range(8):
                zb = bring.get()
                P.op("act", lambda e, zb=zb, i=i, tk=tk: e.activation(out=zb.ap, in_=acc[:, i, tk], func=AF.Copy),
                     reads=[b_acc[tt]], writes=[zb.buf])
                zq = bring.get()
                P.op("act", lambda e, zq=zq, i=i, tk=tk: e.activation(out=zq.ap, in_=acc[:, i, tk], func=AF.Square),
                     reads=[b_acc[tt]], writes=[zq.buf])
                P.op("pe", [lambda e, zb=zb, i=i, s1=s1: e.matmul(ps[s1][:, :], lhsT=onesb, rhs=zb.ap, start=(i == 0), stop=(i == 7)),
                            lambda e, zq=zq, i=i, s2=s2: e.matmul(ps[s2][:, :], lhsT=onesb, rhs=zq.ap, start=(i == 0), stop=(i == 7))],
                     reads=[zb.buf, zq.buf, b_const], writes=[b_ps[s1], b_ps[s2]])
            stats_to_AB(s1, s2, CONV_CH, 0)
            A_t, B_t, b_A, b_B = A_ts[0], B_ts[0], b_As[0], b_Bs[0]
            for i in range(8):
                y = ring.get()
                P.op("dve", lambda e, y=y, i=i, tk=tk: e.tensor_tensor(out=y.ap, in0=acc[:, i, tk], in1=A_t, op=ALU.mult),
                     reads=[b_acc[tt], b_A], writes=[y.buf])
                P.op("dve", lambda e, y=y: e.tensor_tensor(out=y.ap, in0=y.ap, in1=B_t, op=ALU.add),
                     reads=[y.buf, b_B], writes=[y.buf])
                cb = bring.get()
                P.op("act", lambda e, y=y, cb=cb, i=i: e.activation(out=cb.ap, in_=y.ap, func=AF.Silu,
                                                                    scale=cvvec(l, i, CW + 1), bias=cvvec(l, i, CW + 2)),
                     reads=[y.buf, b_const], writes=[cb.buf])
                P.dma("sp", [(conv_v[:, i, tk], cb.ap)], cb.sem, reads=[cb.buf], writes=[b_conv[i][tt]])

        wbr = [g_w_co()[l], g_w_wo()[l], g_w_mo()[l]]
        wo = g_w_out()[l]
        for th in range(2):
            t0 = th * 1024
            reg[0] = at_off
            br = [view(lcarve(8 * 1024 * 2), [8, 1024], BF16) for _ in range(3)]
            b_br = [P.fresh() for _ in range(3)]
            mg = view(lcarve(16 * 1024 * 2), [16, 1024], BF16)
            b_mg = [P.fresh(), P.fresh()]
            bank_ring[0] = list(range(8))
            rd_conv = [b_conv[i][2 * th + j] for i in range(8) for j in range(2)]
            P.dma("sp", [(br[0], conv_v[:, :, t0:t0 + 1024])], "ld_br0", reads=rd_conv, writes=[b_br[0]])
            P.dma("sp", [(br[1], attn_v[:, :, t0:t0 + 1024])], "ld_br1", reads=b_attn, writes=[b_br[1]])
            P.dma("sp", [(br[2], memo_v[:, :, t0:t0 + 1024])], "ld_br2", reads=b_memo, writes=[b_br[2]])
            for n in range(16):
                slA, (g0v, g1v) = load_chunks([win[36 + n], win[52 + n]], 16)
                slB = wslot()
                g2v = ws_view(slB, [16, 128], 0)
                bw = [ws_view(slB, [8, 128], 4096 + j * 2048) for j in range(3)]
                load_w(slB, [(g2v, win[68 + n])] + [(bw[j], wbr[j][n]) for j in range(3)])
                gv = [g0v, g1v, g2v]
                gsl = [slA, slA, slB]
                for tt in range(2):
                    gt = 2 * th + tt
                    tk = slice(t0 + tt * 512, t0 + (tt + 1) * 512)
                    lk = slice(tt * 512, (tt + 1) * 512)
                    gb = [bank() for _ in range(3)]
                    for j in range(3):
                        mm(gb[j], lambda kc: gv[j][:, kc, :], lambda kc: XT[:, kc, tk], 16, [b_ws[gsl[j]], b_XT[gt]])
                    yb = [bank() for _ in range(3)]
                    for j in range(3):
                        mm(yb[j], lambda kc: bw[j][:, kc, :], lambda kc: br[j][:, kc, lk], 8, [b_ws[slB], b_br[j]])
                    gts = [ring.get() for _ in range(3)]
                    for j in range(3):
                        P.op("act", lambda e, j=j, gts=gts, gb=gb: e.activation(out=gts[j].ap, in_=ps[gb[j]][:, :], func=AF.Sigmoid),
                             reads=[b_ps[gb[j]]], writes=[gts[j].buf])
                    m_, t_ = ring.get(), ring.get()
                    P.op("dve", lambda e, m_=m_, gts=gts, yb=yb: e.tensor_tensor(out=m_.ap, in0=gts[0].ap, in1=ps[yb[0]][:, :], op=ALU.mult),
                         reads=[gts[0].buf, b_ps[yb[0]]], writes=[m_.buf])
                    P.op("dve", lambda e, t_=t_, gts=gts, yb=yb: e.tensor_tensor(out=t_.ap, in0=gts[1].ap, in1=ps[yb[1]][:, :], op=ALU.mult),
                         reads=[gts[1].buf, b_ps[yb[1]]], writes=[t_.buf])
                    P.op("dve", lambda e, m_=m_, t_=t_: e.tensor_tensor(out=m_.ap, in0=m_.ap, in1=t_.ap, op=ALU.add),
                         reads=[m_.buf, t_.buf], writes=[m_.buf])
                    P.op("dve", lambda e, t_=t_, gts=gts, yb=yb: e.tensor_tensor(out=t_.ap, in0=gts[2].ap, in1=ps[yb[2]][:, :], op=ALU.mult),
                         reads=[gts[2].buf, b_ps[yb[2]]], writes=[t_.buf])
                    P.op("dve", lambda e, m_=m_, t_=t_, n=n, lk=lk: e.tensor_tensor(out=mg[:, n, lk], in0=m_.ap, in1=t_.ap, op=ALU.add),
                         reads=[m_.buf, t_.buf], writes=[b_mg[tt]])
                    bg_step()
            drain_bg()
            bank_ring[0] = [0, 1, 2, 3]
            for np_ in range(8):
                sl, vs = load_chunks([wo[2 * np_], wo[2 * np_ + 1]], 16)
                for g in range(2):
                    n = 2 * np_ + g
                    for tt in range(2):
                        b = bank()
                        lk = slice(tt * 512, (tt + 1) * 512)
                        mm(b, lambda kc: vs[g][:, kc, :], lambda kc: mg[:, kc, lk], 16, [b_ws[sl], b_mg[tt]])
                        nxt = (n, t0 + 512) if tt == 0 else ((n + 1, t0) if n < 15 else None)
                        resid_epilogue(b, n, t0 + tt * 512, tt, 1.0, n == 0, n == 15, nxt)
            normalize(l, 1, t0, 2, final, pf=(1 if th == 0 else 3))

    phases = []
    for l in range(depth):
        phases.append(("ffn", l, 0))
        phases.append(("mix", l))
        phases.append(("ffn", l, 1))
    if stop is not None:
        phases = phases[:stop]
    for i, ph in enumerate(phases):
        final = (i == len(phases) - 1)
        if ph[0] == "ffn":
            ffn(ph[1], ph[2], final)
        else:
            mixer(ph[1], final)

    drain_bg()
    fin = Buf()
    fin.w = b_out.w
    allb = list(b_outs.values()) + [b for row in b_h32a for b in row] + b_attn + b_memo + [b for row in b_conv for b in row]
    P.wait_all("sp", allb)
    P.emit()
    print("n_ins", P.n_ins, "n_wait", P.n_wait, "arena", off[0])
    nc._used_inputs = set(_ins)
    return nc


def _c(a):
    return np.ascontiguousarray(a, dtype=np.float32)


def host_consts():
    pos = np.arange(T, dtype=np.float32)
    inv_freq = (np.float32(500000.0) ** (-np.arange(0, 32, 2, dtype=np.float32) / np.float32(32))).astype(np.float32)
    ang = (pos[:, None] * inv_freq[None, :]).astype(np.float32)
    cos, sin = np.cos(ang).astype(np.float32).T, np.sin(ang).astype(np.float32).T
    ropeC = np.concatenate([cos, cos], 0)
    ropeS = np.concatenate([-sin, sin], 0)
    i = np.arange(128)[:, None]
    c = np.arange(384)[None, :]
    mask = np.where((c >= i) & (c <= i + 256), 0.0, NEG).astype(np.float32)
    return {"ropeC": _c(ropeC), "ropeS": _c(ropeS), "mask": _c(mask), "ident": np.eye(128, dtype=np.float32)}


def host_weights(inp):
    o = {}
    for i in (1, 2):
        wu = np.asarray(inp[f"ffn{i}_w_up"]).reshape(L, 16, 128, 2, FC, 128)
        o[f"ffn{i}_w_up"] = _c(wu.transpose(0, 4, 2, 1, 3, 5)).reshape(L, FC, 128, 16, 256)
        wd = np.asarray(inp[f"ffn{i}_w_down"]).reshape(L, FC, 128, 16, 128)
        o[f"ffn{i}_w_down"] = _c(wd.transpose(0, 3, 2, 1, 4))
    wi = np.asarray(inp["w_in"]).reshape(L, 16, 128, NCH_IN, 128)
    o["w_in"] = _c(wi.transpose(0, 3, 2, 1, 4))
    for nm in ("conv_w_out", "win_w_o", "mem_w_o"):
        w = np.asarray(inp[nm]).reshape(L, 8, 128, 16, 128)
        o[nm] = _c(w.transpose(0, 3, 2, 1, 4))
    for nm in ("mem_w_kv", "w_out"):
        w = np.asarray(inp[nm]).reshape(L, 16, 128, 16, 128)
        o[nm] = _c(w.transpose(0, 3, 2, 1, 4))
    lnp = np.zeros((L, 3, 2, 16, 128), np.float32)
    for k in range(3):
        lnp[:, k, 0] = np.asarray(inp[f"ln{k + 1}_g"]).reshape(L, 16, 128)
        lnp[:, k, 1] = np.asarray(inp[f"ln{k + 1}_b"]).reshape(L, 16, 128)
    o["lnp"] = _c(lnp.transpose(4, 0, 1, 2, 3)).reshape(128, -1)
    cvp = np.zeros((L, 8, CW + 3, 128), np.float32)
    cvp[:, :, :CW] = np.asarray(inp["conv_dw_w"]).reshape(L, CW, 8, 128).transpose(0, 2, 1, 3)
    cvp[:, :, CW] = np.asarray(inp["conv_dw_b"]).reshape(L, 8, 128)
    cvp[:, :, CW + 1] = np.asarray(inp["conv_ln_g"]).reshape(L, 8, 128)
    cvp[:, :, CW + 2] = np.asarray(inp["conv_ln_b"]).reshape(L, 8, 128)
    o["cvp"] = _c(cvp.transpose(3, 0, 1, 2)).reshape(128, -1)
    o["sinkb"] = _c(np.broadcast_to(np.asarray(inp["win_sink"]).reshape(1, L * 8), (128, L * 8)))
    o.update(host_consts())
    return o


_NC_CACHE = {}


def kernel(**inputs):
    x = np.asarray(inputs["x"], dtype=np.float32)
    mem = np.asarray(inputs["mem"], dtype=np.float32)
    nb = x.shape[0]
    shared = host_weights(inputs)
    in_maps = []
    for b in range(nb):
        m = dict(shared)
        m["xT"] = _c(x[b].T)
        m["memT"] = _c(mem[b].T)
        in_maps.append(m)
    if "nc" not in _NC_CACHE:
        _NC_CACHE["nc"] = build()
    nc = _NC_CACHE["nc"]
    in_maps = [{k: v for k, v in m.items() if k in nc._used_inputs} for m in in_maps]
    res = run_bass_kernel_spmd(nc, in_maps, core_ids=list(range(nb)))
    out = np.stack([np.ascontiguousarray(r["outT"].T) for r in res.results], 0)
    return out.astype(np.float32)
```
